# Optimizing a Trainium2 kernel written in Bass

```python
import math
import jax, jax.numpy as jnp
from jax import lax
import numpy as np

D_MODEL = 1024
BATCH = 16
SEQ = 4096
DEPTH = 1

HEAD_DIM = 64
NSA_HEADS = 8
NSA_KV_HEADS = 2
NSA_GROUP = NSA_HEADS // NSA_KV_HEADS
SB_HEADS = 8
CMP_BLOCK = 32
CMP_STRIDE = 16
CMP_HIDDEN = HEAD_DIM
SEL_BLOCK = 64
SEL_TOPK = 16
WINDOW = 512
NSA_QBLOCK = 64
SB_QBLOCK = 128
N_BUCKETS = 32
MAX_DISTANCE = 128
D_FF = 4 * D_MODEL
EPS = 1e-6
FORCED_BONUS = 1e4
NEG_BLOCK = -1e9
NEG_LOGIT = -1e30

Q_A_W = NSA_HEADS * HEAD_DIM
KV_A_W = NSA_KV_HEADS * HEAD_DIM
GATE_A_W = NSA_HEADS * 3
SB_W = SB_HEADS * HEAD_DIM
IN_SPLITS = (Q_A_W, KV_A_W, KV_A_W, KV_A_W, KV_A_W, KV_A_W, KV_A_W, GATE_A_W, SB_W, SB_W, SB_W, D_MODEL, D_MODEL)
IN_WIDTH = Q_A_W + 6 * KV_A_W + GATE_A_W + 3 * SB_W + 2 * D_MODEL

kernel_name = 'hybrid_nsa_stickbreaking_adaln_block'


def rms_norm(x, g):
    x32 = x.astype(jnp.float32)
    y = x32 * lax.rsqrt(jnp.mean(x32 * x32, axis=-1, keepdims=True) + EPS)
    return (y * g.astype(jnp.float32)).astype(x.dtype)


def rel_bucket(dist):
    n = jnp.maximum(dist, 0)
    max_exact = N_BUCKETS // 2
    nf = jnp.maximum(n, 1).astype(jnp.float32)
    large = max_exact + (jnp.log(nf / max_exact) / math.log(MAX_DISTANCE / max_exact) * (N_BUCKETS - max_exact)).astype(jnp.int32)
    large = jnp.minimum(large, N_BUCKETS - 1)
    return jnp.where(n < max_exact, n, large)


def masked_softmax(logits, mask):
    masked = jnp.where(mask, logits, NEG_LOGIT)
    m = jnp.max(masked, axis=-1, keepdims=True)
    e = jnp.where(mask, jnp.exp(masked - m), 0.0)
    return e / jnp.maximum(jnp.sum(e, axis=-1, keepdims=True), 1e-30)


def compress_blocks(src, pos, w1, w2):
    B, S = src.shape[0], src.shape[1]
    nc = (S - CMP_BLOCK) // CMP_STRIDE + 1
    idx = jnp.arange(nc)[:, None] * CMP_STRIDE + jnp.arange(CMP_BLOCK)[None, :]
    blk = src[:, idx] + pos[None, None, :, None, :]
    blk = blk.transpose(0, 1, 3, 2, 4).reshape(B, nc, NSA_KV_HEADS, CMP_BLOCK * HEAD_DIM)
    return jax.nn.silu(blk @ w1) @ w2


def cmp_to_sel_overlap(nc, nsel):
    c_start = jnp.arange(nc) * CMP_STRIDE
    s_start = jnp.arange(nsel) * SEL_BLOCK
    ov = jnp.minimum(c_start[:, None] + CMP_BLOCK, s_start[None, :] + SEL_BLOCK) - jnp.maximum(c_start[:, None], s_start[None, :])
    return jnp.clip(ov, 0, CMP_BLOCK).astype(jnp.float32) / CMP_BLOCK


def nsa_attention(q, k_cmp, v_cmp, k_slc, v_slc, k_win, v_win, gates, rel_bias):
    B, S = q.shape[0], q.shape[1]
    nc = k_cmp.shape[1]
    nsel = S // SEL_BLOCK
    n_top = min(SEL_TOPK, nsel)
    scale = HEAD_DIM ** -0.5
    cmp_last = jnp.arange(nc) * CMP_STRIDE + CMP_BLOCK - 1
    overlap = cmp_to_sel_overlap(nc, nsel)
    tbl = rel_bias.astype(jnp.float32)
    tbl_g = tbl.reshape(N_BUCKETS, NSA_KV_HEADS, NSA_GROUP)
    ksel = k_slc.reshape(B, nsel, SEL_BLOCK, NSA_KV_HEADS, HEAD_DIM).transpose(0, 3, 1, 2, 4)
    vsel = v_slc.reshape(B, nsel, SEL_BLOCK, NSA_KV_HEADS, HEAD_DIM).transpose(0, 3, 1, 2, 4)
    kw_pad = jnp.pad(k_win, ((0, 0), (WINDOW, 0), (0, 0), (0, 0)))
    vw_pad = jnp.pad(v_win, ((0, 0), (WINDOW, 0), (0, 0), (0, 0)))
    bidx = jnp.arange(B)[:, None, None, None]
    hidx = jnp.arange(NSA_KV_HEADS)[None, :, None, None]
    blk = jnp.arange(nsel)
    tok_in_blk = jnp.arange(SEL_BLOCK)
    win_off = jnp.arange(WINDOW + NSA_QBLOCK)

    def head_bias(dist):
        b = tbl[rel_bucket(dist)]
        return jnp.moveaxis(b, -1, 0).reshape(NSA_KV_HEADS, NSA_GROUP, dist.shape[0], dist.shape[1])

    def block(i):
        start = i * NSA_QBLOCK
        t = start + jnp.arange(NSA_QBLOCK)
        qb = lax.dynamic_slice_in_dim(q, start, NSA_QBLOCK, axis=1)
        gb = lax.dynamic_slice_in_dim(gates, start, NSA_QBLOCK, axis=1)
        dist_c = t[:, None] - cmp_last[None, :]
        s_c = jnp.einsum('bqhgd,bchd->bhgqc', qb, k_cmp).astype(jnp.float32) * scale + head_bias(dist_c)
        p_c = masked_softmax(s_c, dist_c >= 0)
        o_c = jnp.einsum('bhgqc,bchd->bqhgd', p_c.astype(v_cmp.dtype), v_cmp)
        imp = jnp.einsum('bhgqc,cn->bhqn', p_c, overlap)
        cur = (t // SEL_BLOCK)[:, None]
        forced = (blk == 0) | (blk == cur) | (blk == cur - 1)
        imp = jnp.where(blk > cur, NEG_BLOCK, imp + jnp.where(forced, FORCED_BONUS, 0.0))
        _, top = lax.top_k(imp, n_top)
        kg = ksel[bidx, hidx, top]
        vg = vsel[bidx, hidx, top]
        tok = top[..., None] * SEL_BLOCK + tok_in_blk
        dist_s = t[:, None, None] - tok
        bias_s = jnp.moveaxis(tbl_g[rel_bucket(dist_s), hidx[..., None]], -1, 2)
        s_s = jnp.einsum('bqhgd,bhqnkd->bhgqnk', qb, kg).astype(jnp.float32) * scale + bias_s
        shp = s_s.shape
        m_s = (dist_s >= 0)[:, :, None].reshape(B, NSA_KV_HEADS, 1, NSA_QBLOCK, -1)
        p_s = masked_softmax(s_s.reshape(shp[0], shp[1], shp[2], shp[3], -1), m_s).reshape(shp)
        o_s = jnp.einsum('bhgqnk,bhqnkd->bqhgd', p_s.astype(vg.dtype), vg)
        kwb = lax.dynamic_slice_in_dim(kw_pad, start, WINDOW + NSA_QBLOCK, axis=1)
        vwb = lax.dynamic_slice_in_dim(vw_pad, start, WINDOW + NSA_QBLOCK, axis=1)
        s_pos = start - WINDOW + win_off
        dist_w = t[:, None] - s_pos[None, :]
        m_w = (dist_w >= 0) & (dist_w < WINDOW) & (s_pos >= 0)[None, :]
        s_w = jnp.einsum('bqhgd,bkhd->bhgqk', qb, kwb).astype(jnp.float32) * scale + head_bias(dist_w)
        p_w = masked_softmax(s_w, m_w)
        o_w = jnp.einsum('bhgqk,bkhd->bqhgd', p_w.astype(vwb.dtype), vwb)
        return gb[..., 0:1] * o_c + gb[..., 1:2] * o_s + gb[..., 2:3] * o_w

    out = lax.map(block, jnp.arange(S // NSA_QBLOCK))
    return jnp.swapaxes(out, 0, 1).reshape(B, S, NSA_HEADS * HEAD_DIM)


def stick_breaking_attention(q, k, v):
    B, S = q.shape[0], q.shape[1]
    scale = HEAD_DIM ** -0.5
    key_pos = jnp.arange(S)

    def block(i):
        start = i * SB_QBLOCK
        t = start + jnp.arange(SB_QBLOCK)
        qb = lax.dynamic_slice_in_dim(q, start, SB_QBLOCK, axis=1)
        z = jnp.einsum('bqhd,bshd->bhqs', qb, k).astype(jnp.float32) * scale
        mask = key_pos[None, :] < t[:, None]
        log_keep = jnp.where(mask, jax.nn.log_sigmoid(-z), 0.0)
        suffix = lax.cumsum(log_keep, axis=3, reverse=True) - log_keep
        a = jnp.where(mask, jnp.exp(jax.nn.log_sigmoid(z) + suffix), 0.0)
        return jnp.einsum('bhqs,bshd->bqhd', a.astype(v.dtype), v)

    out = lax.map(block, jnp.arange(S // SB_QBLOCK))
    return jnp.swapaxes(out, 0, 1).reshape(B, S, SB_HEADS * HEAD_DIM)


def setup_inputs(seed: int = 0) -> dict:
    key = jax.random.key(seed)
    ks = jax.random.split(key, 20)
    f32 = jnp.float32
    nrm = lambda k, shape, s: jax.random.normal(k, shape, f32) * s
    gain = lambda k, shape: 1.0 + 0.02 * jax.random.normal(k, shape, f32)
    return {
        'x': nrm(ks[0], (BATCH, SEQ, D_MODEL), 1.0),
        'c': nrm(ks[1], (BATCH, D_MODEL), 1.0),
        'rel_bias': nrm(ks[2], (N_BUCKETS, NSA_HEADS), 0.5),
        'ada_w': nrm(ks[3], (DEPTH, D_MODEL, 6 * D_MODEL), 0.5 * D_MODEL ** -0.5),
        'ada_b': nrm(ks[4], (DEPTH, 6 * D_MODEL), 0.02),
        'norm1_g': gain(ks[5], (DEPTH, D_MODEL)),
        'norm2_g': gain(ks[6], (DEPTH, D_MODEL)),
        'w_in': nrm(ks[7], (DEPTH, D_MODEL, IN_WIDTH), D_MODEL ** -0.5),
        'cmp_pos': nrm(ks[8], (DEPTH, CMP_BLOCK, HEAD_DIM), 0.5),
        'cmp_k_w1': nrm(ks[9], (DEPTH, CMP_BLOCK * HEAD_DIM, CMP_HIDDEN), (CMP_BLOCK * HEAD_DIM) ** -0.5),
        'cmp_k_w2': nrm(ks[10], (DEPTH, CMP_HIDDEN, HEAD_DIM), CMP_HIDDEN ** -0.5),
        'cmp_v_w1': nrm(ks[11], (DEPTH, CMP_BLOCK * HEAD_DIM, CMP_HIDDEN), (CMP_BLOCK * HEAD_DIM) ** -0.5),
        'cmp_v_w2': nrm(ks[12], (DEPTH, CMP_HIDDEN, HEAD_DIM), CMP_HIDDEN ** -0.5),
        'q_norm_g': gain(ks[13], (DEPTH, HEAD_DIM)),
        'k_norm_g': gain(ks[14], (DEPTH, 3, HEAD_DIM)),
        'w_up_nsa': nrm(ks[15], (DEPTH, Q_A_W, D_MODEL), Q_A_W ** -0.5),
        'w_up_sb': nrm(ks[16], (DEPTH, SB_W, D_MODEL), SB_W ** -0.5),
        'w_out': nrm(ks[17], (DEPTH, D_MODEL, D_MODEL), D_MODEL ** -0.5),
        'mlp_w1': nrm(ks[18], (DEPTH, D_MODEL, D_FF), D_MODEL ** -0.5),
        'mlp_w2': nrm(ks[19], (DEPTH, D_FF, D_MODEL), D_FF ** -0.5),
    }


def reference(x, c, rel_bias, ada_w, ada_b, norm1_g, norm2_g, w_in, cmp_pos, cmp_k_w1, cmp_k_w2, cmp_v_w1, cmp_v_w2, q_norm_g, k_norm_g, w_up_nsa, w_up_sb, w_out, mlp_w1, mlp_w2):
    B, S, _ = x.shape
    split_at = np.cumsum(IN_SPLITS)[:-1].tolist()
    kv_shape = (B, S, NSA_KV_HEADS, HEAD_DIM)
    sb_shape = (B, S, SB_HEADS, HEAD_DIM)
    h = x
    for layer in range(DEPTH):
        mod = jax.nn.silu(c) @ ada_w[layer] + ada_b[layer]
        shift1, scale1, gate1, shift2, scale2, gate2 = [m[:, None, :] for m in jnp.split(mod, 6, axis=-1)]
        u = rms_norm(h, norm1_g[layer]) * (1 + scale1) + shift1
        z = u @ w_in[layer]
        q_a, kc, vc, ksl, vsl, kwn, vwn, g_a, q_b, k_b, v_b, m_a, m_b = jnp.split(z, split_at, axis=-1)
        q_a = rms_norm(q_a.reshape(B, S, NSA_HEADS, HEAD_DIM), q_norm_g[layer]).reshape(B, S, NSA_KV_HEADS, NSA_GROUP, HEAD_DIM)
        k_cmp = rms_norm(compress_blocks(kc.reshape(kv_shape), cmp_pos[layer], cmp_k_w1[layer], cmp_k_w2[layer]), k_norm_g[layer, 0])
        v_cmp = compress_blocks(vc.reshape(kv_shape), cmp_pos[layer], cmp_v_w1[layer], cmp_v_w2[layer])
        k_slc = rms_norm(ksl.reshape(kv_shape), k_norm_g[layer, 1])
        k_win = rms_norm(kwn.reshape(kv_shape), k_norm_g[layer, 2])
        g_nsa = jax.nn.sigmoid(g_a).reshape(B, S, NSA_KV_HEADS, NSA_GROUP, 3)
        y_a = nsa_attention(q_a, k_cmp, v_cmp, k_slc, vsl.reshape(kv_shape), k_win, vwn.reshape(kv_shape), g_nsa, rel_bias) @ w_up_nsa[layer]
        y_b = stick_breaking_attention(q_b.reshape(sb_shape), k_b.reshape(sb_shape), v_b.reshape(sb_shape)) @ w_up_sb[layer]
        mixed = (jax.nn.sigmoid(m_a) * y_a + jax.nn.sigmoid(m_b) * y_b) @ w_out[layer]
        h = h + gate1 * mixed
        u2 = rms_norm(h, norm2_g[layer]) * (1 + scale2) + shift2
        ff = jnp.square(jax.nn.relu(u2 @ mlp_w1[layer])) @ mlp_w2[layer]
        h = h + gate2 * ff
    return h
```

```python
import numpy as np
from contextlib import ExitStack
import concourse.bass as bass
import concourse.mybir as mybir
from concourse.bass_utils import run_bass_kernel_spmd

F32 = mybir.dt.float32
BF16 = mybir.dt.bfloat16
AF = mybir.ActivationFunctionType
ALU = mybir.AluOpType
AX = mybir.AxisListType

D = 1024
DH = 64
NH = 8
EPS = 1e-6
NCORES = 8
SEQ = 4096
BATCH = 16
IN_W = 4888


class Buf:
    __slots__ = ("name", "w", "r")

    def __init__(self, name=""):
        self.name = name
        self.w = {}
        self.r = {}


class V:
    __slots__ = ("ap", "buf")

    def __init__(self, ap, buf):
        self.ap = ap
        self.buf = buf

    def __getitem__(self, k):
        return V(self.ap[k], self.buf)

    def re(self, pat, **kw):
        return V(self.ap.rearrange(pat, **kw), self.buf)

    def bc(self, shape):
        return V(self.ap.to_broadcast(list(shape)), self.buf)

    def cast(self, dt):
        return V(self.ap.bitcast(dt), self.buf)


class Ev:
    __slots__ = ("sem", "seq", "key", "needed", "val")

    def __init__(self, sem, seq, key, val=None):
        self.sem = sem
        self.seq = seq
        self.key = key
        self.needed = False
        self.val = val


class Sched:
    ENGS = ("pe", "act", "dve", "pool", "sp")

    def __init__(self, sems, dma_sems):
        self.sem = dict(zip(self.ENGS, sems))
        self.epoch = 0
        self.cnt = {e: 0 for e in self.ENGS}
        self.prog = {e: [] for e in self.ENGS}
        self.seen = {e: {} for e in self.ENGS}
        self.dma_sems = dma_sems
        self.dma_cnt = {q: 0 for q in dma_sems}
        self.dma_n = 0
        self.dma_last = {}
        self.last_ev = {}
        self.all_ev = {}
        self.ninstr = 0

    def _need(self, eng, ev, waits, raw):
        key = ev.key
        if key[0] == "e" and key[1] == eng and not raw:
            return
        if self.seen[eng].get(key, 0) >= ev.seq:
            return
        cur = waits.get(key)
        if cur is None or cur.seq < ev.seq:
            waits[key] = ev

    def _commit(self, eng, waits):
        wl = list(waits.values())
        for ev in wl:
            ev.needed = True
            if self.seen[eng].get(ev.key, 0) < ev.seq:
                self.seen[eng][ev.key] = ev.seq
        return wl

    def op(self, eng, fn, ins=(), outs=(), dma=False):
        waits = {}
        for v in ins:
            for ev in v.buf.w.values():
                self._need(eng, ev, waits, True)
        for v in outs:
            b = v.buf
            for ev in b.w.values():
                self._need(eng, ev, waits, False)
            for ev in b.r.values():
                self._need(eng, ev, waits, False)
        if dma:
            pool_ = self.dma_sems[eng]
            ns = len(pool_)
            slot = self.dma_cnt[eng] % ns
            rnd = self.dma_cnt[eng] // ns
            self.dma_cnt[eng] += 1
            self.dma_n += 1
            dsem = pool_[slot]
            key = ("dma", eng, slot)
            slot = (eng, slot)
            if rnd > 0:
                self._need(eng, Ev(dsem, rnd, key, 16 * rnd), waits, True)
            ev = Ev(dsem, rnd + 1, key, 16 * (rnd + 1))
            ev.needed = True
            self.dma_last[slot] = ev
        else:
            self.cnt[eng] += 1
            key = ("e", eng, self.epoch)
            ev = Ev(self.sem[eng], self.cnt[eng], key)
            self.last_ev[eng] = ev
            self.all_ev.setdefault(key, []).append(ev)
        wl = self._commit(eng, waits)

        def emit(h, fn=fn, wl=wl, ev=ev, dma=dma):
            for w in wl:
                h.wait_ge(w.sem, w.val)
            ins_ = fn(h)
            if dma:
                ins_.then_inc(ev.sem, 16)
            elif ev.needed:
                ins_.then_inc(ev.sem, 1)

        self.prog[eng].append(emit)
        self.ninstr += 1
        for v in ins:
            v.buf.r[ev.key] = ev
        for v in outs:
            v.buf.w = {ev.key: ev}
            v.buf.r = {}
        return ev

    def barrier(self):
        evs = [self.last_ev[e] for e in self.ENGS if self.cnt[e] > 0 and e in self.last_ev]
        evs += list(self.dma_last.values())
        for eng in self.ENGS:
            waits = {}
            for ev in evs:
                if ev.key[0] == "e" and ev.key[1] == eng:
                    continue
                self._need(eng, ev, waits, True)
            wl = self._commit(eng, waits)
            if wl:
                self.prog[eng].append(lambda h, wl=wl: [h.wait_ge(w.sem, w.val) for w in wl])

    def new_epoch(self, sems):
        for eng in self.ENGS:
            for e2 in self.ENGS:
                self.seen[eng][("e", e2, self.epoch)] = 1 << 40
        self.epoch += 1
        self.sem = dict(zip(self.ENGS, sems))
        self.cnt = {e: 0 for e in self.ENGS}
        self.last_ev = {}

    def final_wait(self, eng="sp"):
        wl = list(self.dma_last.values())
        wl += [self.last_ev[e] for e in self.ENGS if e != eng and e in self.last_ev]
        for w in wl:
            w.needed = True
        self.prog[eng].append(lambda h, wl=wl: [h.wait_ge(w.sem, w.val) for w in wl])

    def finalize(self):
        ninc = 0
        for key, evs in self.all_ev.items():
            c = 0
            for ev in evs:
                if ev.needed:
                    c += 1
                    ninc += 1
                ev.val = c
        self.ninc = ninc

    def emit_all(self, block):
        self.finalize()
        prog = self.prog

        @block.tensor
        def _(h):
            for f in prog["pe"]:
                f(h)

        @block.scalar
        def _(h):
            for f in prog["act"]:
                f(h)

        @block.vector
        def _(h):
            for f in prog["dve"]:
                f(h)

        @block.gpsimd
        def _(h):
            for f in prog["pool"]:
                f(h)

        @block.sync
        def _(h):
            for f in prog["sp"]:
                f(h)


class Rot:
    def __init__(self, items):
        self.items = items
        self.i = 0

    def next(self):
        it = self.items[self.i % len(self.items)]
        self.i += 1
        return it


def _bucket(dist):
    n = np.maximum(dist, 0)
    nf = np.maximum(n, 1).astype(np.float64)
    raw = np.log(nf / 16.0) / np.log(8.0) * 16.0
    large = 16 + np.floor(raw + 1e-9).astype(np.int64)
    large = np.minimum(large, 31)
    return np.where(n < 16, n, large)


def _consts():
    a = np.arange(128)[:, None]
    ind = np.zeros((32, 128, 272), np.float32)
    m = np.arange(256)[None, :]
    dist = (1 - m // 128) * 128 + a - (m % 128)
    bk = _bucket(dist)
    for b in range(32):
        ind[b, :, :256] = ((bk == b) & (dist >= 0))
    w = np.arange(16)[None, :]
    distc = a - 16 * (w - 9) - 31
    bkc = _bucket(distc)
    for b in range(32):
        ind[b, :, 256:] = ((bkc == b) & (distc >= 0))
    fw = np.zeros((128, 126), np.float32)
    rel = np.arange(126)[None, :] - 62
    lo = (a < 64)
    fw[:] = np.where(rel >= 2, -1e9, 0.0)
    fw += np.where(rel == 1, np.where(lo, -1e9, 1e4), 0.0)
    fw += np.where(rel == 0, 1e4, 0.0)
    fw += np.where(rel == -1, np.where(lo, 1e4, 0.0), 0.0)
    cc = np.arange(256)[:, None]
    nn = np.arange(64)[None, :]
    ov = np.minimum(16 * cc + 32, 64 * nn + 64) - np.maximum(16 * cc, 64 * nn)
    ov = np.clip(ov, 0, 32).astype(np.float32) / 32.0
    ov[255] = 0.0
    ovx = np.zeros((128, 2, 65), np.float32)
    ovx[:, :, 0] = 1.0
    ovx[:, 0, 1:] = ov[:128]
    ovx[:, 1, 1:] = ov[128:]
    b_ = np.arange(128)[None, :]
    misc = np.zeros((128, 5, 128), np.float32)
    misc[:, 0] = np.eye(128)
    misc[:, 1] = ((a // 64) == (b_ // 64))
    misc[:, 2] = (b_ < a)
    misc[:, 3] = (b_ > a)
    misc[:, 4] = 1.0
    return ind, fw, ovx, misc


def _win_perm():
    q = []
    for j in range(4):
        q += list(range(j * 64, j * 64 + 64)) + list(range((j + 4) * 64, (j + 4) * 64 + 64))
    o_kc, o_vc, o_ksl, o_vsl, o_kwn, o_vwn, o_ga = 512, 640, 768, 896, 1024, 1152, 1280
    o_qb, o_kb, o_vb, o_ma, o_mb = 1304, 1816, 2328, 2840, 3864
    r = lambda s, n: list(range(s, s + n))
    perm = q + r(o_kc, 128) + r(o_vc, 128) + r(o_ksl, 128) + r(o_kwn, 128)
    perm += r(o_qb, 512) + r(o_kb, 512) + r(o_vb, 512)
    perm += r(o_vsl, 128) + r(o_vwn, 128) + r(o_ga, 24)
    perm += r(o_ma, 1024) + r(o_mb, 1024)
    assert len(perm) == IN_W and len(set(perm)) == IN_W
    return np.array(perm)


WC = [(0, 512), (512, 1024), (1024, 1536), (1536, 2048), (2048, 2560), (2560, 2840),
      (2840, 3352), (3352, 3864), (3864, 4376), (4376, 4888)]


def host_prep(inp, S, NB, ncores, batch0=0):
    f = lambda a: np.ascontiguousarray(a, dtype=np.float32)
    ind, fw, ovx, misc = _consts()
    vecT = lambda v: f(np.asarray(v).reshape(-1, 128).T)
    dup = lambda v: f(np.concatenate([v, v], axis=0))
    w1k = np.asarray(inp["cmp_k_w1"][0]).reshape(32, 64, 64).transpose(1, 0, 2).reshape(64, 2048)
    w1v = np.asarray(inp["cmp_v_w1"][0]).reshape(32, 64, 64).transpose(1, 0, 2).reshape(64, 2048)
    w2k = np.asarray(inp["cmp_k_w2"][0])
    w2kpad = np.zeros((64, 2, 128), np.float32)
    w2kpad[:, 0, 0:64] = w2k
    w2kpad[:, 1, 64:128] = w2k
    kng = np.asarray(inp["k_norm_g"][0])
    shared = {
        "adaw": f(inp["ada_w"][0]),
        "adabT": vecT(inp["ada_b"][0]),
        "adab": f(np.asarray(inp["ada_b"][0]).reshape(1, 6144)),
        "n1g": vecT(inp["norm1_g"][0]),
        "n2g": vecT(inp["norm2_g"][0]),
        "win": f(np.asarray(inp["w_in"][0])[:, _win_perm()]),
        "cw1k": dup(w1k), "cw1v": dup(w1v),
        "cposT": dup(np.asarray(inp["cmp_pos"][0]).T),
        "cw2k": f(w2kpad.reshape(64, 256)),
        "cw2v": f(inp["cmp_v_w2"][0]),
        "qkg": f(np.stack([np.tile(np.asarray(inp["q_norm_g"][0]), 2), np.tile(kng[0], 2),
                           np.tile(kng[1], 2), np.tile(kng[2], 2)], axis=1)),
        "wupn": f(inp["w_up_nsa"][0]), "wups": f(inp["w_up_sb"][0]),
        "wout": f(inp["w_out"][0]), "w1": f(inp["mlp_w1"][0]), "w2": f(inp["mlp_w2"][0]),
        "relb": f(np.asarray(inp["rel_bias"]).reshape(1, 256)),
        "c_ind": ind, "c_fw": fw, "c_ovx": f(ovx.reshape(128, 130)), "c_misc": f(misc.reshape(128, 640)),
    }
    maps = []
    x = np.asarray(inp["x"])
    c = np.asarray(inp["c"])
    for core in range(ncores):
        b0 = batch0 + core * NB
        m = dict(shared)
        m["x"] = f(x[b0:b0 + NB, :S])
        m["cT"] = f(np.stack([c[b0 + i].reshape(8, 128).T for i in range(NB)], axis=0))
        maps.append(m)
    return maps


class Prog:
    def __init__(self, S, NB):
        self.S_len = S
        self.NB = NB
        self.dbg = False
        self.dbg_names = []
        self.NG = S // 512
        self.NT = S // 128

    def mm(self, out, lhsT, rhs, start=True, stop=True):
        self.S.op("pe", lambda h: h.matmul(out.ap, lhsT.ap, rhs.ap, start=start, stop=stop),
                  [lhsT, rhs], [out])

    def tpose(self, out, in_):
        idn = self.ident
        self.S.op("pe", lambda h: h.transpose(out.ap, in_.ap, idn.ap), [in_, idn], [out])

    def act(self, out, in_, func, scale=1.0, bias=0.0, accum=None):
        ins = [in_]
        outs = [out]
        kw = dict(out=out.ap, in_=in_.ap, func=func)
        if isinstance(scale, V):
            ins.append(scale)
            kw["scale"] = scale.ap
        else:
            kw["scale"] = float(scale)
        if isinstance(bias, V):
            ins.append(bias)
            kw["bias"] = bias.ap
        elif bias != 0.0:
            kw["bias"] = float(bias)
        if accum is not None:
            outs.append(accum)
            kw["accum_out"] = accum.ap
        self.S.op("act", lambda h: h.activation(**kw), ins, outs)

    def tt(self, eng, out, a, b, op):
        self.S.op(eng, lambda h: h.tensor_tensor(out.ap, a.ap, b.ap, op), [a, b], [out])

    def ts(self, eng, out, a, s1, s2, op0, op1=None):
        ins = [a]
        if isinstance(s1, V):
            ins.append(s1)
        if isinstance(s2, V):
            ins.append(s2)
        g = lambda s: s.ap if isinstance(s, V) else s
        if op1 is None:
            self.S.op(eng, lambda h: h.tensor_scalar(out.ap, a.ap, g(s1), None, op0), ins, [out])
        else:
            self.S.op(eng, lambda h: h.tensor_scalar(out.ap, a.ap, g(s1), g(s2), op0, op1), ins, [out])

    def stt(self, out, a, s, b, op0, op1):
        ins = [a, b]
        if isinstance(s, V):
            ins.append(s)
        g = s.ap if isinstance(s, V) else s
        self.S.op("dve", lambda h: h.scalar_tensor_tensor(out.ap, a.ap, g, b.ap, op0, op1), ins, [out])

    def copy(self, eng, out, in_):
        if eng == "act":
            self.act(out, in_, AF.Copy)
        else:
            self.S.op(eng, lambda h: h.tensor_copy(out.ap, in_.ap), [in_], [out])

    def memset(self, eng, out, val):
        self.S.op(eng, lambda h: h.memset(out.ap, val), [], [out])

    def dbg_dump(self, name, v):
        if not getattr(self, "dbg", False):
            return
        shp = list(v.ap.shape)
        n = int(np.prod(shp[1:]))
        t = self.nc.dram_tensor("dbg_" + name, [shp[0], n], F32, kind="ExternalOutput").ap()
        if len(shp) == 3:
            t = t.rearrange("p (a b) -> p a b", a=shp[1])
        elif len(shp) == 4:
            t = t.rearrange("p (a b c) -> p a b c", a=shp[1], b=shp[2])
        self.dbg_names.append("dbg_" + name)
        self.dma("pool", V(t, Buf("dbg")), v)

    def dma(self, q, out, in_):
        self.S.op(q, lambda h: h.dma_start(out=out.ap, in_=in_.ap), [in_], [out], dma=True)

    def alloc(self, shape, dt, name=""):
        n = int(np.prod(shape[1:]))
        nbytes = n * (2 if dt == BF16 else 4)
        nbytes = (nbytes + 31) // 32 * 32
        off = self.sb_off
        self.sb_off += nbytes
        assert self.sb_off <= self.sb_bytes, (name, self.sb_off, self.sb_bytes)
        ap = self.sb[:, off // 4:(off + nbytes) // 4]
        if dt == BF16:
            ap = ap.bitcast(BF16)
        ap = ap[:, 0:n]
        v = V(ap, Buf(name))
        if len(shape) == 3:
            v = v.re("p (a b) -> p a b", a=shape[1])
        elif len(shape) == 4:
            v = v.re("p (a b c) -> p a b c", a=shape[1], b=shape[2])
        if shape[0] < 128:
            v = v[0:shape[0]]
        return v

    def bank(self, k, dt=F32):
        ap = self.ps[:, 512 * k:512 * (k + 1)]
        if dt == BF16:
            ap = ap.bitcast(BF16)
        return V(ap, self.bank_buf[k])

    def plan_chunks(self):
        d = self.d
        r8 = lambda ap: ap.rearrange("(k p) n -> p k n", p=128)
        ch = []
        ch.append(("cw1k", d["cw1k"], (128, 2048)))
        ch.append(("cw1v", d["cw1v"], (128, 2048)))
        for b in range(self.NB):
            for j in range(12):
                ch.append(("ada%d" % j, r8(d["adaw"])[:, :, 512 * j:512 * j + 512], (128, 8, 512)))
            for g in range(self.NG):
                for j in range(6):
                    c0, c1 = WC[j]
                    ch.append(("win%d" % j, r8(d["win"])[:, :, c0:c1], (128, 8, c1 - c0)))
                ch.append(("cw1k", d["cw1k"], (128, 2048)))
                ch.append(("cw1v", d["cw1v"], (128, 2048)))
                for j in (6, 8):
                    c0, c1 = WC[j]
                    ch.append(("win%d" % j, r8(d["win"])[:, :, c0:c1], (128, 8, c1 - c0)))
                ch.append(("wupn", r8(d["wupn"]), (128, 4, 1024)))
                ch.append(("wups", r8(d["wups"]), (128, 4, 1024)))
                for j in (7, 9):
                    c0, c1 = WC[j]
                    ch.append(("win%d" % j, r8(d["win"])[:, :, c0:c1], (128, 8, c1 - c0)))
                for j in range(2):
                    ch.append(("wout%d" % j, r8(d["wout"])[:, :, 512 * j:512 * j + 512], (128, 8, 512)))
                for j in range(8):
                    ch.append(("w1_%d" % j, r8(d["w1"])[:, :, 512 * j:512 * j + 512], (128, 8, 512)))
                    ch.append(("w2_%d" % j, r8(d["w2"])[:, 4 * j:4 * j + 4, :], (128, 4, 1024)))
        self.chunks = ch
        self.ch_pos = 0
        self.ch_issued = 0

    def _slot_view(self, k):
        tag, src, shape = self.chunks[k]
        slot = self.wslots[k % len(self.wslots)]
        if len(shape) == 2:
            return slot[:, 0:shape[1]]
        v = slot.re("p (a b) -> p a b", a=shape[1])
        if shape[1] == 8 and shape[2] < 512:
            v = v[:, :, 0:shape[2]]
        return v

    def wnext(self, tag, live=1):
        k = self.ch_pos
        assert self.chunks[k][0] == tag, (self.chunks[k][0], tag)
        self.ch_pos += 1
        ns = len(self.wslots)
        import os
        depth = int(os.environ.get("PREF", "99"))
        while self.ch_issued < min(len(self.chunks), k + min(ns - live, depth) + 1):
            kk = self.ch_issued
            if not (os.environ.get("NOWDMA") == "1" and kk > 40):
                self.dma("pool", self._slot_view(kk), V(self.chunks[kk][1], self.dram_buf))
            self.ch_issued += 1
        return self._slot_view(k)

    def build(self):
        S, NB, NG, NT = self.S_len, self.NB, self.NG, self.NT
        nc = bass.Bass("TRN2", target_bir_lowering=False)
        self.nc = nc
        din = lambda name, shape: nc.dram_tensor(name, list(shape), F32, kind="ExternalInput").ap()
        d = {}
        d["x"] = din("x", [NB, S, D])
        d["cT"] = din("cT", [NB, 128, 8])
        d["adaw"] = din("adaw", [D, 6 * D])
        d["adabT"] = din("adabT", [128, 48])
        d["adab"] = din("adab", [1, 6 * D])
        d["n1g"] = din("n1g", [128, 8])
        d["n2g"] = din("n2g", [128, 8])
        d["win"] = din("win", [D, IN_W])
        d["cw1k"] = din("cw1k", [128, 2048])
        d["cw1v"] = din("cw1v", [128, 2048])
        d["cposT"] = din("cposT", [128, 32])
        d["cw2k"] = din("cw2k", [64, 256])
        d["cw2v"] = din("cw2v", [64, 64])
        d["qkg"] = din("qkg", [128, 4])
        d["wupn"] = din("wupn", [512, D])
        d["wups"] = din("wups", [512, D])
        d["wout"] = din("wout", [D, D])
        d["w1"] = din("w1", [D, 4 * D])
        d["w2"] = din("w2", [4 * D, D])
        d["relb"] = din("relb", [1, 256])
        d["c_ind"] = din("c_ind", [32, 128, 272])
        d["c_fw"] = din("c_fw", [128, 126])
        d["c_ovx"] = din("c_ovx", [128, 130])
        d["c_misc"] = din("c_misc", [128, 640])
        self.d = d
        self.y = nc.dram_tensor("y", [NB, S, D], F32, kind="ExternalOutput").ap()
        self.dram_buf = Buf("dram_in")
        self.y_buf = Buf("y")

        with ExitStack() as st:
            self.sb_bytes = 212832
            self.sb = st.enter_context(nc.sbuf_tensor("sb", [128, self.sb_bytes // 4], F32))
            self.sb_off = 0
            self.ps = st.enter_context(nc.psum_tensor("ps", [128, 4096], F32))
            self.bank_buf = [Buf("bank%d" % k) for k in range(8)]
            sems = [st.enter_context(nc.semaphore("s_" + e)) for e in Sched.ENGS]
            dsems = {q: [st.enter_context(nc.semaphore("d%s_%d" % (q, i))) for i in range(16)] for q in ("sp", "pool")}
            self.esems = [[st.enter_context(nc.semaphore("s%d_%s" % (i, e))) for e in Sched.ENGS]
                          for i in range((NB * NG + 3) // 4)]
            self.S = Sched(sems, dsems)
            self.setup()
            for b in range(NB):
                self.seq_init(b)
                import os
                for g in range(min(NG, int(os.environ.get("MAXG", "99")))):
                    self.group(b, g)
            self.S.final_wait("sp")
            with nc.Block() as block:
                self.S.emit_all(block)
        return nc

    def setup(self):
        d = self.d
        A = self.alloc
        DR = lambda ap: V(ap, self.dram_buf)
        misc = A([128, 5, 128], BF16, "misc")
        self.dma("pool", misc, DR(d["c_misc"].rearrange("p (a b) -> p a b", a=5)))
        self.ident = misc[:, 0, :]
        self.blk1 = misc[:, 1, :]
        self.mstrict_bf = misc[:, 2, :]
        self.ustrict_bf = misc[:, 3, :]
        self.onesbf = misc[:, 4, :]
        self.mstrict_f = A([128, 128], F32, "mstrict_f")
        self.dma("sp", self.mstrict_f, DR(d["c_misc"][:, 256:384]))
        self.onecol = A([128, 1], F32, "onecol")
        self.memset("dve", self.onecol, 1.0)
        self.fwide = A([128, 126], F32, "fwide")
        self.dma("sp", self.fwide, DR(d["c_fw"]))
        self.qkg = A([128, 4], F32, "qkg")
        self.dma("sp", self.qkg, DR(d["qkg"]))
        self.n1g = A([128, 8], F32, "n1g")
        self.dma("sp", self.n1g, DR(d["n1g"]))
        self.n2g = A([128, 8], F32, "n2g")
        self.dma("sp", self.n2g, DR(d["n2g"]))
        self.adabT = A([128, 48], F32, "adabT")
        self.dma("sp", self.adabT, DR(d["adabT"]))
        self.cw2k = A([64, 2, 128], BF16, "cw2k")
        self.dma("pool", self.cw2k, DR(d["cw2k"].rearrange("p (a b) -> p a b", a=2)))
        self.cw2v = A([64, 64], BF16, "cw2v")
        self.dma("pool", self.cw2v, DR(d["cw2v"]))
        self.cposT = A([128, 32], BF16, "cposT")
        self.dma("pool", self.cposT, DR(d["cposT"]))
        self.pbias = A([64, 2], F32, "pbias")
        self.tbl = A([128, 32, 8], F32, "tbl")
        self.dma("sp", self.tbl.re("p a b -> p (a b)"), DR(d["relb"].rearrange("a b -> (a b)").partition_broadcast(128)))
        self.c31 = self.tbl[:, 31, :]
        self.R = A([128, 8, 272], BF16, "R")
        self.A1 = A([128, 8], F32, "A1")
        self.B1 = A([128, 8], F32, "B1")
        self.A2 = A([128, 8], F32, "A2")
        self.B2 = A([128, 8], F32, "B2")
        self.g1bc = A([128, 1024], F32, "g1bc")
        self.g2bc = A([128, 1024], F32, "g2bc")
        self.wslots = [A([128, 4096], BF16, "wslot%d" % i) for i in range(4)]
        S, NT = self.S_len, self.NT
        self.kbT = A([128, 4, S], BF16, "kbT")
        self.vb = A([128, NT, 512], BF16, "vb")
        self.kslT = A([128, S], BF16, "kslT")
        self.vsl = A([128, NT, 2, 65], BF16, "vsl")
        self.kwT = A([128, 1024], BF16, "kwT")
        self.vw = A([128, 8, 2, 65], BF16, "vw")
        self.kcT = A([128, 528], BF16, "kcT")
        self.vcT = A([128, 528], BF16, "vcT")
        self.kcmpT = A([128, 256], BF16, "kcmpT")
        self.hidTv = A([64, 2, 256], BF16, "hidTv")
        self.vcx = A([128, 2, 2, 129], BF16, "vcx")
        self.uT = A([128, 8, 512], BF16, "uT")
        self.gsig = A([128, 4, 24], F32, "gsig")
        self.ecb = Rot([A([128, 256], BF16, "ecb%d" % i) for i in range(2)])
        import os
        padb = int(os.environ.get("KPAD", "0"))
        if padb:
            A([128, padb // 4], F32, "pad")
        self.arena0 = self.sb_off
        self.memset("pool", self.vsl[:, :, :, 64:65], 1.0)
        self.memset("pool", self.vw[:, :, :, 64:65], 1.0)
        for ch in range(2):
            for kvh in range(2):
                self.dma("pool", self.vcx[:, ch, kvh, 64:129], DR(d["c_ovx"][:, 65 * ch:65 * ch + 65]))
        self.plan_chunks()
        etbl = A([128, 32, 8], F32, "etbl")
        R32 = A([128, 8, 272], F32, "R32")
        indb = [A([128, 272], F32, "indb%d" % i) for i in range(2)]
        self.tt("dve", etbl, self.tbl, self.tbl[:, 31:32, :].bc([128, 32, 8]), ALU.subtract)
        self.act(etbl, etbl, AF.Exp)
        for b in range(32):
            ib = indb[b % 2]
            self.dma("sp", ib, DR(d["c_ind"][b]))
            for h in range(8):
                if b == 0:
                    self.ts("dve", R32[:, h, :], ib, etbl[:, b, h:h + 1], None, ALU.mult)
                else:
                    self.stt(R32[:, h, :], ib, etbl[:, b, h:h + 1], R32[:, h, :], ALU.mult, ALU.add)
        self.copy("dve", self.R, R32)
        for i, tag in enumerate(("cw1k", "cw1v")):
            W1 = self.wnext(tag)
            pb = self.bank(i)
            for l in range(32):
                self.mm(pb[0:64, 0:1], W1[0:64, 64 * l:64 * l + 64], self.cposT[0:64, l:l + 1], l == 0, l == 31)
            self.copy("dve", self.pbias[:, i:i + 1], pb[0:64, 0:1])
        self.S.barrier()
        self.sb_off = self.arena0
        self.alloc_arena()

    def alloc_arena(self):
        A = self.alloc
        a0 = self.sb_off
        self.xbuf = A([128, 4, 1024], F32, "xbuf")
        self.xbuf_t = [V(self.xbuf.ap[:, t, :], Buf("xbuf%d" % t)) for t in range(4)]
        x_end = self.sb_off
        self.acc = A([128, 4, 1024], F32, "acc")
        self.acc_t = [V(self.acc.ap[:, t, :], Buf("acc%d" % t)) for t in range(4)]
        self.hT = Rot([A([128, 4, 512], BF16, "hT%d" % i) for i in range(2)])
        self.relu_t = Rot([A([128, 512], F32, "relu%d" % i) for i in range(2)])
        self.xn = Rot([A([128, 1024], BF16, "xn%d" % i) for i in range(2)])
        self.sq_junk = self.relu_t.items[0].cast(BF16)
        self.small = Rot([A([128, 4], F32, "small%d" % i) for i in range(4)])
        a1 = self.sb_off
        self.sb_off = a0
        self.qaT = A([128, 4, 512], BF16, "qaT")
        self.qbT = A([128, 4, 512], BF16, "qbT")
        self.cbufA = A([128, 512], F32, "cbufA")
        self.cbufB = A([128, 512], F32, "cbufB")
        self.carA = Rot([A([128, 1], F32, "carA%d" % i) for i in range(2)])
        self.carB = Rot([A([128, 1], F32, "carB%d" % i) for i in range(2)])
        bfs = [A([128, 512], BF16, "bf512_%d" % i) for i in range(8)]
        self.bf512 = Rot(bfs[0:4])
        self.bfA = Rot(bfs[4:6])
        self.bfB = Rot(bfs[6:8])
        self.rawc = A([128, 8, 129], F32, "rawc")
        self.raws = A([128, 8, 130], F32, "raws")
        self.oacc = A([128, 8, 64], F32, "oacc")
        self.imp = A([128, 2, 64], F32, "imp")
        self.imp2 = A([128, 64], F32, "imp2")
        self.sel = A([128, 2, 64], BF16, "sel")
        self.m8 = Rot([A([128, 8], F32, "m8_%d" % i) for i in range(2)])
        self.sm8 = Rot([A([128, 8, 2], F32, "sm8_%d" % i) for i in range(4)])
        self.obf = Rot([A([128, 512], BF16, "obf%d" % i) for i in range(1)])
        assert self.sb_off >= x_end, (self.sb_off, x_end)
        self.onT = A([128, 4, 512], BF16, "onT")
        self.osT = A([128, 4, 512], BF16, "osT")
        self.mixT = A([128, 8, 512], BF16, "mixT")
        f32s = [A([128, 512], F32, "f32t%d" % i) for i in range(5)]
        self.f32t = Rot(f32s[0:1] + f32s[1:5])
        self.f32n = Rot(f32s[0:1])
        self.f32A = Rot(f32s[1:3])
        self.f32B = Rot(f32s[3:5])
        self.sgt = Rot([A([128, 512], BF16, "sgt%d" % i) for i in range(2)])
        a2 = self.sb_off
        self.sb_off = max(a1, a2)
        self.arena_bytes = self.sb_off - a0

    def seq_init(self, b):
        d = self.d
        DR = lambda ap: V(ap, self.dram_buf)
        S = self.S
        S.barrier()
        cT = self.f32t.next()[:, 0:8]
        self.dma("sp", cT, DR(d["cT"][b]))
        sc = self.sgt.next()[:, 0:8]
        self.act(sc, cT, AF.Silu)
        scb = self.mixT[:, 0:2, :].re("p a b -> p (a b)").re("p (k m) -> p k m", k=8)
        for k in range(8):
            self.ts("dve", scb[:, k, :], self.onesbf, sc[:, k:k + 1], None, ALU.mult)
        fm_dst = {0: self.B1, 1: self.B1, 2: self.A1, 3: self.A1, 6: self.B2, 7: self.B2, 8: self.A2, 9: self.A2}
        pool = Rot([self.bank(k) for k in range(4)])
        for j in range(12):
            W = self.wnext("ada%d" % j)
            if j in fm_dst:
                dst = fm_dst[j]
                pb = pool.next()
                for m in range(4):
                    for k in range(8):
                        self.mm(pb[:, m:m + 1], W[:, k, 128 * m:128 * m + 128], sc[:, k:k + 1], k == 0, k == 7)
                c0 = (j % 2) * 4
                self.tt("dve", dst[:, c0:c0 + 4], pb[:, 0:4], self.adabT[:, 4 * j:4 * j + 4], ALU.add)
            else:
                dst = self.g1bc if j < 6 else self.g2bc
                pb = pool.next()
                for k in range(8):
                    self.mm(pb, scb[:, k, :], W[:, k, :], k == 0, False)
                arow = self.bf512.next()[0:1, :]
                self.dma("pool", arow, DR(d["adab"][0:1, 512 * j:512 * j + 512]))
                self.mm(pb, self.onesbf[0:1, :], arow, False, True)
                c0 = (j % 2) * 512
                self.copy("act", dst[:, c0:c0 + 512], pb)
        for Av, gv in ((self.A1, self.n1g), (self.A2, self.n2g)):
            self.stt(Av, Av, 1.0, gv, ALU.add, ALU.mult)
        self.memset("pool", self.kcT[:, 0:16], 0.0)
        self.memset("pool", self.vcT[:, 0:16], 0.0)
        self.memset("pool", self.kcmpT, 0.0)
        self.memset("pool", self.hidTv, 0.0)
        for e in self.ecb.items:
            self.memset("pool", e, 0.0)
        S.barrier()

    def norm_tile(self, src, tt_, Am, Bm):
        xn = self.xn.next()
        sm = self.small.next()
        self.act(self.sq_junk, src, AF.Square, accum=sm[:, 0:1])
        self.act(sm[:, 1:2], sm[:, 0:1], AF.Sqrt, scale=1.0 / D, bias=EPS)
        self.S.op("dve", lambda h: h.reciprocal(sm.ap[:, 2:3], sm.ap[:, 1:2]), [sm], [sm])
        self.act(xn, src, AF.Copy, scale=sm[:, 2:3])
        pT = self.tp_banks.next().cast(BF16).re("p (c n) -> p c n", c=8)
        for c in range(8):
            self.tpose(pT[:, c, :], xn[:, 128 * c:128 * c + 128])
        for c in range(8):
            self.ts("dve", self.uT[:, c, 128 * tt_:128 * tt_ + 128], pT[:, c, :], Am[:, c:c + 1], Bm[:, c:c + 1],
                    ALU.mult, ALU.add)

    def qk_norm(self, zps, gcol, out, n):
        sq = self.bf512.next()
        self.act(sq[:, 0:n], zps[:, 0:n], AF.Square)
        sp = self.pB.next()
        self.mm(sp[:, 0:n], self.blk1, sq[:, 0:n])
        rt = self.f32t.next()
        self.act(rt[:, 0:n], sp[:, 0:n], AF.Sqrt, scale=1.0 / DH, bias=EPS)
        self.S.op("dve", lambda h: h.reciprocal(rt.ap[:, 0:n], rt.ap[:, 0:n]), [rt], [rt])
        self.stt(out, zps[:, 0:n], gcol, rt[:, 0:n], ALU.mult, ALU.mult)

    def group(self, b, g):
        d = self.d
        DR = lambda ap: V(ap, self.dram_buf)
        S = self.S
        S.barrier()
        if (b * self.NG + g) % 4 == 0:
            S.new_epoch(self.esems[(b * self.NG + g) // 4])
        self.pA = Rot([self.bank(k) for k in range(4)])
        self.pB = Rot([self.bank(k) for k in (4, 5)])
        self.tp_banks = Rot([self.bank(k) for k in (6, 7)])
        for t in range(4):
            self.dma("sp", self.xbuf_t[t], DR(d["x"][b, (4 * g + t) * 128:(4 * g + t + 1) * 128, :]))
        for t in range(4):
            self.norm_tile(self.xbuf_t[t], t, self.A1, self.B1)
        import os
        stopat = int(os.environ.get("STOPAT", "99")) if g == int(os.environ.get("STOPG", "6")) else 99
        if stopat <= 1:
            return
        S.barrier()
        S.trace_ops = (g == 6 and os.environ.get("TRACEOPS") == "1")
        uT = self.uT
        gs = slice(512 * g, 512 * g + 512)

        def fm(W, m):
            pb = self.pA.next()
            for k in range(8):
                self.mm(pb, W[:, k, 128 * m:128 * m + 128], uT[:, k, :], k == 0, k == 7)
            return pb

        W = self.wnext("win0")
        for m in range(4):
            self.qk_norm(fm(W, m), self.qkg[:, 0:1], self.qaT[:, m, :], 512)
        if g == int(os.environ.get("STOPG", "6")) and os.environ.get("PJ") == "1":
            return
        W = self.wnext("win1")
        self.copy("act", self.kcT[:, 16:528], fm(W, 0))
        self.copy("act", self.vcT[:, 16:528], fm(W, 1))
        self.qk_norm(fm(W, 2), self.qkg[:, 2:3], self.kslT[:, gs], 512)
        rs = slice(512 * (g % 2), 512 * (g % 2) + 512)
        self.qk_norm(fm(W, 3), self.qkg[:, 3:4], self.kwT[:, rs], 512)
        if g == int(os.environ.get("STOPG", "6")) and os.environ.get("PJ") == "2":
            return
        W = self.wnext("win2")
        for m in range(4):
            self.copy("act" if m % 2 else "dve", self.qbT[:, m, :], fm(W, m))
        if g == int(os.environ.get("STOPG", "6")) and os.environ.get("PJ") == "3":
            return
        W = self.wnext("win3")
        for m in range(4):
            self.copy("act" if m % 2 else "dve", self.kbT[:, m, gs], fm(W, m))
        if g == int(os.environ.get("STOPG", "6")) and os.environ.get("PJ") == "4":
            return
        W = self.wnext("win4")
        for t in range(4):
            pb = self.pA.next()
            for k in range(8):
                self.mm(pb, uT[:, k, 128 * t:128 * t + 128], W[:, k, :], k == 0, k == 7)
            self.copy("act" if t % 2 else "dve", self.vb[:, 4 * g + t, :], pb)
        if g == int(os.environ.get("STOPG", "6")) and os.environ.get("PJ") == "5":
            return
        W = self.wnext("win5")
        for t in range(4):
            pb = self.pA.next()
            for k in range(8):
                self.mm(pb[:, 0:280], uT[:, k, 128 * t:128 * t + 128], W[:, k, :], k == 0, k == 7)
            self.copy("act", self.vsl[:, 4 * g + t, :, 0:64], pb[:, 0:128].re("p (a b) -> p a b", a=2))
            self.copy("act", self.vw[:, (4 * g + t) % 8, :, 0:64], pb[:, 128:256].re("p (a b) -> p a b", a=2))
            self.act(self.gsig[:, t, :], pb[:, 256:280], AF.Sigmoid)
        S.trace_ops = False
        if stopat <= 2:
            return
        c_lo, c_hi = max(0, 32 * g - 1), 32 * g + 30
        n = c_hi - c_lo + 1
        for is_k, src, tag in ((True, self.kcT, "cw1k"), (False, self.vcT, "cw1v")):
            W1 = self.wnext(tag)
            hk = self.bf512.next()[0:64, 0:64].re("p (a b) -> p a b", a=2)
            for kvh in range(2):
                pbs = 64 * kvh
                pb = self.pA.next()
                for l in range(32):
                    col0 = 16 + 16 * c_lo - 512 * g + l
                    rhs = src[pbs:pbs + 64, col0:col0 + 16 * (n - 1) + 1:16]
                    self.mm(pb[0:64, 0:n], W1[pbs:pbs + 64, 64 * l:64 * l + 64], rhs, l == 0, l == 31)
                if is_k:
                    self.act(hk[:, kvh, 0:n], pb[0:64, 0:n], AF.Silu, bias=self.pbias[:, 0:1])
                else:
                    self.act(self.hidTv[:, kvh, c_lo:c_hi + 1], pb[0:64, 0:n], AF.Silu, bias=self.pbias[:, 1:2])
            if is_k:
                pb = self.pA.next()
                self.mm(pb[:, 0:n], self.cw2k[:, 0, :], hk[:, 0, 0:n], True, False)
                self.mm(pb[:, 0:n], self.cw2k[:, 1, :], hk[:, 1, 0:n], False, True)
                self.qk_norm(pb, self.qkg[:, 1:2], self.kcmpT[:, c_lo:c_hi + 1], n)
            else:
                for ch in range(c_lo // 128, c_hi // 128 + 1):
                    for kvh in range(2):
                        pb = self.pA.next()
                        self.mm(pb[:, 0:64], self.hidTv[:, kvh, 128 * ch:128 * ch + 128], self.cw2v)
                        self.copy("act", self.vcx[:, ch, kvh, 0:64], pb[:, 0:64])
        self.copy("pool", self.kcT[:, 0:16], self.kcT[:, 512:528])
        self.copy("pool", self.vcT[:, 0:16], self.vcT[:, 512:528])
        if stopat <= 3:
            return
        self.nsa_score = Rot([self.bank(2), self.bank(4)])
        self.pro_bank = self.bank(3)
        self.tp_banks = Rot([self.bank(7)])
        import os
        skn = int(os.environ.get("SKIP_NSA_FROM", "999"))
        sks = int(os.environ.get("SKIP_SB_FROM", "999"))
        for t in range(4):
            self.attn_tile(g, t, 4 * g + t < skn, 4 * g + t < sks)
        if g == 1 and b == 0:
            self.dbg_dump("onT", self.onT)
            self.dbg_dump("osT", self.osT)
            self.dbg_dump("kcmpT", self.kcmpT)
            self.dbg_dump("vcx", self.vcx)
            self.dbg_dump("sel", self.sel)
            self.dbg_dump("imp", self.imp)
            self.dbg_dump("rawc", self.rawc)
            self.dbg_dump("raws", self.raws)
        S.barrier()
        self.pA = Rot([self.bank(k) for k in range(6)])
        for t in range(4):
            self.dma("sp", self.xbuf_t[t], DR(d["x"][b, (4 * g + t) * 128:(4 * g + t + 1) * 128, :]))
        Wma = self.wnext("win6", live=1)
        Wmb = self.wnext("win8", live=2)
        Wn = self.wnext("wupn", live=3)
        Ws = self.wnext("wups", live=4)
        for m in range(8):
            if m == 4:
                Wma = self.wnext("win7", live=4)
                Wmb = self.wnext("win9", live=4)
            sga = self.sgt.next()
            self.act(sga, fm(Wma, m % 4), AF.Sigmoid)
            sgb = self.sgt.next()
            self.act(sgb, fm(Wmb, m % 4), AF.Sigmoid)
            pa = self.pA.next()
            for k in range(4):
                self.mm(pa, Wn[:, k, 128 * m:128 * m + 128], self.onT[:, k, :], k == 0, k == 3)
            pb2 = self.pA.next()
            for k in range(4):
                self.mm(pb2, Ws[:, k, 128 * m:128 * m + 128], self.osT[:, k, :], k == 0, k == 3)
            t1 = self.f32t.next()
            self.tt("dve", t1, pa, sga, ALU.mult)
            t2 = self.f32t.next()
            self.tt("dve", t2, pb2, sgb, ALU.mult)
            self.tt("pool", self.mixT[:, m, :], t1, t2, ALU.add)
        for cc in range(2):
            W = self.wnext("wout%d" % cc)
            for t in range(4):
                pb = self.pA.next()
                for k in range(8):
                    self.mm(pb, self.mixT[:, k, 128 * t:128 * t + 128], W[:, k, :], k == 0, k == 7)
                t1 = self.f32t.next()
                self.tt("dve", t1, pb, self.g1bc[:, 512 * cc:512 * cc + 512], ALU.mult)
                hv = self.xbuf_t[t][:, 512 * cc:512 * cc + 512]
                self.tt("dve", hv, hv, t1, ALU.add)
        if stopat <= 4:
            return
        S.barrier()
        self.pA = Rot([self.bank(k) for k in range(4)])
        self.pB = Rot([self.bank(k) for k in (4, 5)])
        for t in range(4):
            self.norm_tile(self.xbuf_t[t], t, self.A2, self.B2)
        for j in range(8):
            W1 = self.wnext("w1_%d" % j)
            hT = self.hT.next()
            for m in range(4):
                pb = self.pA.next()
                for k in range(8):
                    self.mm(pb, W1[:, k, 128 * m:128 * m + 128], uT[:, k, :], k == 0, k == 7)
                r = self.relu_t.next()
                self.act(r, pb, AF.Relu)
                self.tt("pool", hT[:, m, :], r, r, ALU.mult)
            W2 = self.wnext("w2_%d" % j)
            for t in range(4):
                for cc in range(2):
                    pb = self.pB.next()
                    for m in range(4):
                        self.mm(pb, hT[:, m, 128 * t:128 * t + 128], W2[:, m, 512 * cc:512 * cc + 512], m == 0, m == 3)
                    av = self.acc_t[t][:, 512 * cc:512 * cc + 512]
                    if j == 0:
                        self.copy("act", av, pb)
                    else:
                        self.tt("dve", av, av, pb, ALU.add)
        for t in range(4):
            self.tt("dve", self.acc_t[t], self.acc_t[t], self.g2bc, ALU.mult)
            self.tt("dve", self.xbuf_t[t], self.xbuf_t[t], self.acc_t[t], ALU.add)
            self.dma("sp", V(self.y[b, (4 * g + t) * 128:(4 * g + t + 1) * 128, :], Buf("y")), self.xbuf_t[t])
        S.barrier()

    def soft_stage1(self, it):
        nb = len(it["vrhs"])
        w = 128 * nb
        sc = self.nsa_score.next()
        col = 0
        for kr, wk in it["krhs"]:
            self.mm(sc[:, col:col + wk], it["qT"], kr)
            col += wk
        E = self.bf512.next()
        h = it["h"]
        self.act(E[:, 0:w], sc[:, 0:w], AF.Exp, scale=0.125)
        it["E"] = E

    def soft_stage2(self, it):
        nb = len(it["vrhs"])
        w = 128 * nb
        E = it["E"]
        for (c0, ncol, mk, is3d) in it["masks"]:
            ev = E[:, c0:c0 + ncol]
            if is3d:
                ev = ev.re("p (n k) -> p n k", k=64)
            self.tt("dve", ev, ev, mk, ALU.mult)
        yield
        tp = self.tp_banks.next().cast(BF16)
        for n_ in range(nb):
            self.tpose(tp[:, 128 * n_:128 * n_ + 128], E[:, 128 * n_:128 * n_ + 128])
        ET = self.bf512.next()
        self.copy("dve", ET[:, 0:w], tp[:, 0:w])
        yield
        for n_ in range(nb):
            self.mm(it["acc"], ET[:, 128 * n_:128 * n_ + 128], it["vrhs"][n_],
                    it["first"] and n_ == 0, it["last"] and n_ == nb - 1)
        if it["fin"] is not None:
            self.copy("dve", self.raws[:, it["fin"], :], it["ob"][:, 0:130])
        yield

    def nsa_prologue(self, g, t):
        i = 4 * g + t
        qc = slice(128 * t, 128 * t + 128)
        Wc = 8 * i + 7
        nch = 1 if Wc <= 128 else 2
        lo, hi = max(0, 8 * i - 9), 8 * i + 7
        off = 8 * i - 9

        def st1(h):
            pbs, j = 64 * (h // 4), h % 4
            qT = self.qaT[pbs:pbs + 64, j, qc]
            sc = self.nsa_score.next()
            self.mm(sc[:, 0:Wc], qT, self.kcmpT[pbs:pbs + 64, 0:Wc])
            E = self.ecb.next()
            self.act(E[:, 0:Wc], sc[:, 0:Wc], AF.Exp, scale=0.125)
            return E

        def st2(h, E):
            kvh = h // 4
            self.tt("dve", E[:, lo:hi], E[:, lo:hi], self.R[:, h, 256 + lo - off:256 + hi - off], ALU.mult)
            yield
            tp = self.tp_banks.next().cast(BF16)
            for ch in range(nch):
                self.tpose(tp[:, 128 * ch:128 * ch + 128], E[:, 128 * ch:128 * ch + 128])
            ET = self.bf512.next()
            self.copy("act", ET[:, 0:128 * nch], tp[:, 0:128 * nch])
            yield
            oc = self.pro_bank
            for ch in range(nch):
                self.mm(oc[:, 0:129], ET[:, 128 * ch:128 * ch + 128], self.vcx[:, ch, kvh, :], ch == 0, ch == nch - 1)
            self.copy("dve", self.rawc[:, h, :], oc[:, 0:129])
            yield

        prev = None
        for h in range(8):
            E = st1(h)
            yield
            if prev is not None:
                yield from st2(*prev)
            prev = (h, E)
        yield from st2(*prev)
        rsc = self.sm8.next()[:, :, 0]
        self.ts("dve", rsc, self.rawc[:, :, 64], 1e-30, None, ALU.max)
        self.S.op("dve", lambda h_: h_.reciprocal(rsc.ap, rsc.ap), [rsc], [rsc])
        tmp = self.f32n.next().re("p (a b) -> p a b", a=8)
        rsc3 = V(rsc.ap.unsqueeze(2), rsc.buf)
        self.tt("dve", tmp, self.rawc[:, :, 65:129], rsc3.bc([128, 8, 64]), ALU.mult)
        self.S.op("dve", lambda h_: h_.tensor_reduce(self.imp.ap, tmp.ap.rearrange("p (k g) n -> p k n g", g=4),
                                                     AX.X, ALU.add), [tmp], [self.imp])
        fw = V(self.fwide.ap[:, 62 - 2 * i:126 - 2 * i].unsqueeze(1), self.fwide.buf)
        self.tt("dve", self.imp, self.imp, fw.bc([128, 2, 64]), ALU.add)
        self.ts("dve", self.imp[:, :, 0:1], self.imp[:, :, 0:1], 1e4, None, ALU.add)
        yield
        for kvh in range(2):
            iv = self.imp[:, kvh, :]
            m8 = self.m8.next()
            self.S.op("dve", lambda h_, m8=m8, iv=iv: h_.max(m8.ap, iv.ap), [iv], [m8])
            self.S.op("dve", lambda h_, m8=m8, iv=iv: h_.match_replace(self.imp2.ap, m8.ap, iv.ap, -1e30),
                      [iv, m8], [self.imp2])
            m8b = self.m8.next()
            self.S.op("dve", lambda h_, m8b=m8b: h_.max(m8b.ap, self.imp2.ap), [self.imp2], [m8b])
            self.ts("dve", self.sel[:, kvh, :], iv, m8b[:, 7:8], None, ALU.is_ge)
            yield
        coef = self.sm8.next()[:, :, 0]
        gv = self.gsig[:, t, :].re("p (h c) -> p h c", c=3)
        self.tt("dve", coef, rsc, gv[:, :, 0], ALU.mult)
        coef3 = V(coef.ap.unsqueeze(2), coef.buf)
        self.tt("dve", self.oacc, self.rawc[:, :, 0:64], coef3.bc([128, 8, 64]), ALU.mult)
        yield

    def nsa_heads(self, g, t, heads, ob):
        i = 4 * g + t
        Kt = (i + 1) * 128
        qc = slice(128 * t, 128 * t + 128)
        nchunk = i // 4 + 1
        items = []
        for h in heads:
            kvh, j, pbs = h // 4, h % 4, 64 * (h // 4)
            qT = self.qaT[pbs:pbs + 64, j, qc]
            for c in range(nchunk):
                w = min(512, Kt - 512 * c)
                nb = w // 128
                masks = []
                selv = V(self.sel.ap[:, kvh, 8 * c:8 * c + w // 64].unsqueeze(2), self.sel.buf)
                masks.append((0, w, selv.bc([128, w // 64, 64]), True))
                for kb in (i - 1, i):
                    if kb >= 0 and kb // 4 == c:
                        ro = (kb - (i - 1)) * 128
                        masks.append(((kb % 4) * 128, 128, self.R[:, h, ro:ro + 128], False))
                items.append(dict(qT=qT, krhs=[(self.kslT[pbs:pbs + 64, 512 * c:512 * c + w], w)], h=h, masks=masks,
                                  vrhs=[self.vsl[:, 4 * c + n_, kvh, :] for n_ in range(nb)],
                                  acc=ob[:, 0:65], first=(c == 0), last=(c == nchunk - 1), fin=None, ob=ob))
            kbs = list(range(max(0, i - 4), i + 1))
            parts = [p for p in (kbs[0:4], kbs[4:]) if p]
            for pi, part in enumerate(parts):
                masks = []
                for li, kb in enumerate(part):
                    if kb == i - 4:
                        masks.append((128 * li, 128, self.ustrict_bf, False))
                    if kb >= i - 1:
                        ro = (kb - (i - 1)) * 128
                        masks.append((128 * li, 128, self.R[:, h, ro:ro + 128], False))
                items.append(dict(qT=qT, krhs=[(self.kwT[pbs:pbs + 64, 128 * (kb % 8):128 * (kb % 8) + 128], 128)
                                               for kb in part], h=h, masks=masks,
                                  vrhs=[self.vw[:, kb % 8, kvh, :] for kb in part],
                                  acc=ob[:, 65:130], first=(pi == 0), last=(pi == len(parts) - 1),
                                  fin=(h if pi == len(parts) - 1 else None), ob=ob))
        prev = None
        for it in items:
            self.soft_stage1(it)
            yield
            if prev is not None:
                yield from self.soft_stage2(prev)
            prev = it
        yield from self.soft_stage2(prev)

    def nsa_epilogue(self, g, t):
        qc = slice(128 * t, 128 * t + 128)
        gv = self.gsig[:, t, :].re("p (h c) -> p h c", c=3)
        rs2 = self.sm8.next()
        r4 = self.raws.re("p h (b k) -> p h b k", k=65)
        self.ts("dve", rs2, r4[:, :, :, 64], 1e-30, None, ALU.max)
        self.S.op("dve", lambda h_: h_.reciprocal(rs2.ap, rs2.ap), [rs2], [rs2])
        for br in (0, 1):
            cf = self.sm8.next()[:, :, 0]
            self.tt("dve", cf, rs2[:, :, br], gv[:, :, 1 + br], ALU.mult)
            cf3 = V(cf.ap.unsqueeze(2), cf.buf)
            tmp = self.f32n.next().re("p (a b) -> p a b", a=8)
            self.tt("dve", tmp, r4[:, :, br, 0:64], cf3.bc([128, 8, 64]), ALU.mult)
            self.tt("dve", self.oacc, self.oacc, tmp, ALU.add)
        ob_ = self.obf.next()
        self.copy("act", ob_, self.oacc.re("p a b -> p (a b)"))
        tp = self.tp_banks.next().cast(BF16)
        for k in range(4):
            self.tpose(tp[:, 128 * k:128 * k + 128], ob_[:, 128 * k:128 * k + 128])
        self.copy("act", self.onT[:, :, qc], tp[:, 0:512].re("p (k n) -> p k n", k=4))

    def sb_heads(self, g, t, heads, cb, cars, f32r, bfr, sc, acc):
        i = 4 * g + t
        Kt = (i + 1) * 128
        qc = slice(128 * t, 128 * t + 128)
        nchunk = i // 4 + 1
        for h in heads:
            pbs, j = 64 * (h % 2), h // 2
            qT = self.qbT[pbs:pbs + 64, j, qc]
            carry = None
            first = True
            for c in range(nchunk - 1, -1, -1):
                w = min(512, Kt - 512 * c)
                nb = w // 128
                diag = (c == nchunk - 1)
                self.mm(sc[:, 0:w], qT, self.kbT[pbs:pbs + 64, j, 512 * c:512 * c + w])
                yield
                es = f32r.next()
                self.act(es[:, 0:w], sc[:, 0:w], AF.Exp, scale=0.125)
                self.act(es[:, 0:w], es[:, 0:w], AF.Ln, bias=1.0)
                if diag:
                    self.tt("pool", es[:, w - 128:w], es[:, w - 128:w], self.mstrict_f, ALU.mult)
                yield
                ones = self.onecol.bc([128, w])
                init = 0.0 if carry is None else carry
                ins = [ones, es] + ([carry] if carry is not None else [])

                def scan(h_, cb=cb, es=es, w=w, init=init, ones=ones):
                    iv = init.ap if isinstance(init, V) else init
                    return h_.tensor_tensor_scan(cb.ap[:, 0:w][:, ::-1], ones.ap, es.ap[:, 0:w][:, ::-1], iv,
                                                 ALU.mult, ALU.add)
                self.S.op("dve", scan, ins, [cb])
                carry = cars.next()
                self.copy("dve", carry, cb[:, 0:1])
                self.stt(es[:, 0:w], sc[:, 0:w], 0.125, cb[:, 0:w], ALU.mult, ALU.subtract)
                yield
                Ab = bfr.next()
                self.act(Ab[:, 0:w], es[:, 0:w], AF.Exp)
                if diag:
                    self.tt("pool", Ab[:, w - 128:w], Ab[:, w - 128:w], self.mstrict_bf, ALU.mult)
                yield
                tp = self.tp_banks.next().cast(BF16)
                for n_ in range(nb):
                    self.tpose(tp[:, 128 * n_:128 * n_ + 128], Ab[:, 128 * n_:128 * n_ + 128])
                AT = bfr.next()
                self.copy("act", AT[:, 0:w], tp[:, 0:w])
                yield
                for n_ in range(nb):
                    last = (c == 0 and n_ == nb - 1)
                    self.mm(acc[:, 64 * (h % 4):64 * (h % 4) + 64], AT[:, 128 * n_:128 * n_ + 128],
                            self.vb[:, 4 * c + n_, 64 * h:64 * h + 64], first, last)
                    first = False
                yield

    def sb_epilogue(self, g, t):
        qc = slice(128 * t, 128 * t + 128)
        ob_ = self.obf.next()
        self.copy("act", ob_[:, 0:256], self.bank(5)[:, 0:256])
        self.copy("act", ob_[:, 256:512], self.bank(6)[:, 0:256])
        tp = self.tp_banks.next().cast(BF16)
        for k in range(4):
            self.tpose(tp[:, 128 * k:128 * k + 128], ob_[:, 128 * k:128 * k + 128])
        self.copy("dve", self.osT[:, :, qc], tp[:, 0:512].re("p (k n) -> p k n", k=4))

    @staticmethod
    def run_lanes(lanes):
        lanes = list(lanes)
        while lanes:
            nxt = []
            for ln in lanes:
                try:
                    next(ln)
                    nxt.append(ln)
                except StopIteration:
                    pass
            lanes = nxt

    def attn_tile(self, g, t, do_nsa=True, do_sb=True):
        def nsa_lane():
            yield from self.nsa_prologue(g, t)
            yield from self.nsa_heads(g, t, range(0, 8), self.bank(3))
            self.nsa_epilogue(g, t)
        lanes = []
        if do_nsa:
            lanes.append(nsa_lane())
        if do_sb:
            lanes.append(self.sb_heads(g, t, range(0, 4), self.cbufA, self.carA, self.f32A, self.bfA, self.bank(0), self.bank(5)))
            lanes.append(self.sb_heads(g, t, range(4, 8), self.cbufB, self.carB, self.f32B, self.bfB, self.bank(1), self.bank(6)))
        self.run_lanes(lanes)
        if do_sb:
            self.sb_epilogue(g, t)


_CACHE = {}


def run(inputs, S, NB, ncores, batch0=0):
    key = (S, NB)
    if key not in _CACHE:
        _CACHE[key] = Prog(S, NB).build()
    nc = _CACHE[key]
    maps = host_prep(inputs, S, NB, ncores, batch0)
    res = run_bass_kernel_spmd(nc, maps, core_ids=list(range(ncores)))
    return np.concatenate([np.asarray(r["y"]) for r in res.results], axis=0)


def kernel(**inputs):
    out = run(inputs, SEQ, BATCH // NCORES, NCORES)
    return out.astype(np.float32)
```

```python
import numpy as np
from contextlib import ExitStack
import concourse.bass as bass
import concourse.mybir as mybir
from concourse.bass_utils import run_bass_kernel_spmd

F32 = mybir.dt.float32
BF16 = mybir.dt.bfloat16
AF = mybir.ActivationFunctionType
ALU = mybir.AluOpType
AX = mybir.AxisListType

D = 1024
DH = 64
NH = 8
EPS = 1e-6
NCORES = 8
SEQ = 4096
BATCH = 16
IN_W = 4888


class Buf:
    __slots__ = ("name", "w", "r")

    def __init__(self, name=""):
        self.name = name
        self.w = {}
        self.r = {}


class V:
    __slots__ = ("ap", "buf")

    def __init__(self, ap, buf):
        self.ap = ap
        self.buf = buf

    def __getitem__(self, k):
        return V(self.ap[k], self.buf)

    def re(self, pat, **kw):
        return V(self.ap.rearrange(pat, **kw), self.buf)

    def bc(self, shape):
        return V(self.ap.to_broadcast(list(shape)), self.buf)

    def cast(self, dt):
        return V(self.ap.bitcast(dt), self.buf)


class Ev:
    __slots__ = ("sem", "seq", "key", "needed", "val")

    def __init__(self, sem, seq, key, val=None):
        self.sem = sem
        self.seq = seq
        self.key = key
        self.needed = False
        self.val = val


class Sched:
    ENGS = ("pe", "act", "dve", "pool", "sp")

    def __init__(self, sems, dma_sems):
        self.sem = dict(zip(self.ENGS, sems))
        self.epoch = 0
        self.cnt = {e: 0 for e in self.ENGS}
        self.prog = {e: [] for e in self.ENGS}
        self.seen = {e: {} for e in self.ENGS}
        self.dma_sems = dma_sems
        self.dma_cnt = {q: 0 for q in dma_sems}
        self.dma_n = 0
        self.dma_last = {}
        self.last_ev = {}
        self.all_ev = {}
        self.ninstr = 0

    def _need(self, eng, ev, waits, raw):
        key = ev.key
        if key[0] == "e" and key[1] == eng and not raw:
            return
        if self.seen[eng].get(key, 0) >= ev.seq:
            return
        cur = waits.get(key)
        if cur is None or cur.seq < ev.seq:
            waits[key] = ev

    def _commit(self, eng, waits):
        wl = list(waits.values())
        for ev in wl:
            ev.needed = True
            if self.seen[eng].get(ev.key, 0) < ev.seq:
                self.seen[eng][ev.key] = ev.seq
        return wl

    def op(self, eng, fn, ins=(), outs=(), dma=False):
        waits = {}
        for v in ins:
            for ev in v.buf.w.values():
                self._need(eng, ev, waits, True)
        for v in outs:
            b = v.buf
            for ev in b.w.values():
                self._need(eng, ev, waits, False)
            for ev in b.r.values():
                self._need(eng, ev, waits, False)
        if dma:
            pool_ = self.dma_sems[eng]
            ns = len(pool_)
            slot = self.dma_cnt[eng] % ns
            rnd = self.dma_cnt[eng] // ns
            self.dma_cnt[eng] += 1
            self.dma_n += 1
            dsem = pool_[slot]
            key = ("dma", eng, slot)
            slot = (eng, slot)
            if rnd > 0:
                self._need(eng, Ev(dsem, rnd, key, 16 * rnd), waits, True)
            ev = Ev(dsem, rnd + 1, key, 16 * (rnd + 1))
            ev.needed = True
            self.dma_last[slot] = ev
        else:
            self.cnt[eng] += 1
            key = ("e", eng, self.epoch)
            ev = Ev(self.sem[eng], self.cnt[eng], key)
            self.last_ev[eng] = ev
            self.all_ev.setdefault(key, []).append(ev)
        wl = self._commit(eng, waits)

        def emit(h, fn=fn, wl=wl, ev=ev, dma=dma):
            for w in wl:
                h.wait_ge(w.sem, w.val)
            ins_ = fn(h)
            if dma:
                ins_.then_inc(ev.sem, 16)
            elif ev.needed:
                ins_.then_inc(ev.sem, 1)

        self.prog[eng].append(emit)
        self.ninstr += 1
        for v in ins:
            v.buf.r[ev.key] = ev
        for v in outs:
            v.buf.w = {ev.key: ev}
            v.buf.r = {}
        return ev

    def barrier(self):
        evs = [self.last_ev[e] for e in self.ENGS if self.cnt[e] > 0 and e in self.last_ev]
        evs += list(self.dma_last.values())
        for eng in self.ENGS:
            waits = {}
            for ev in evs:
                if ev.key[0] == "e" and ev.key[1] == eng:
                    continue
                self._need(eng, ev, waits, True)
            wl = self._commit(eng, waits)
            if wl:
                self.prog[eng].append(lambda h, wl=wl: [h.wait_ge(w.sem, w.val) for w in wl])

    def new_epoch(self, sems):
        for eng in self.ENGS:
            for e2 in self.ENGS:
                self.seen[eng][("e", e2, self.epoch)] = 1 << 40
        self.epoch += 1
        self.sem = dict(zip(self.ENGS, sems))
        self.cnt = {e: 0 for e in self.ENGS}
        self.last_ev = {}

    def final_wait(self, eng="sp"):
        wl = list(self.dma_last.values())
        wl += [self.last_ev[e] for e in self.ENGS if e != eng and e in self.last_ev]
        for w in wl:
            w.needed = True
        self.prog[eng].append(lambda h, wl=wl: [h.wait_ge(w.sem, w.val) for w in wl])

    def finalize(self):
        ninc = 0
        for key, evs in self.all_ev.items():
            c = 0
            for ev in evs:
                if ev.needed:
                    c += 1
                    ninc += 1
                ev.val = c
        self.ninc = ninc

    def emit_all(self, block):
        self.finalize()
        prog = self.prog

        @block.tensor
        def _(h):
            for f in prog["pe"]:
                f(h)

        @block.scalar
        def _(h):
            for f in prog["act"]:
                f(h)

        @block.vector
        def _(h):
            for f in prog["dve"]:
                f(h)

        @block.gpsimd
        def _(h):
            for f in prog["pool"]:
                f(h)

        @block.sync
        def _(h):
            for f in prog["sp"]:
                f(h)


class Rot:
    def __init__(self, items):
        self.items = items
        self.i = 0

    def next(self):
        it = self.items[self.i % len(self.items)]
        self.i += 1
        return it


def _bucket(dist):
    n = np.maximum(dist, 0)
    nf = np.maximum(n, 1).astype(np.float64)
    raw = np.log(nf / 16.0) / np.log(8.0) * 16.0
    large = 16 + np.floor(raw + 1e-9).astype(np.int64)
    large = np.minimum(large, 31)
    return np.where(n < 16, n, large)


def _consts():
    a = np.arange(128)[:, None]
    ind = np.zeros((32, 128, 272), np.float32)
    m = np.arange(256)[None, :]
    dist = (1 - m // 128) * 128 + a - (m % 128)
    bk = _bucket(dist)
    for b in range(32):
        ind[b, :, :256] = ((bk == b) & (dist >= 0))
    w = np.arange(16)[None, :]
    distc = a - 16 * (w - 9) - 31
    bkc = _bucket(distc)
    for b in range(32):
        ind[b, :, 256:] = ((bkc == b) & (distc >= 0))
    fw = np.zeros((128, 126), np.float32)
    rel = np.arange(126)[None, :] - 62
    lo = (a < 64)
    fw[:] = np.where(rel >= 2, -1e9, 0.0)
    fw += np.where(rel == 1, np.where(lo, -1e9, 1e4), 0.0)
    fw += np.where(rel == 0, 1e4, 0.0)
    fw += np.where(rel == -1, np.where(lo, 1e4, 0.0), 0.0)
    cc = np.arange(256)[:, None]
    nn = np.arange(64)[None, :]
    ov = np.minimum(16 * cc + 32, 64 * nn + 64) - np.maximum(16 * cc, 64 * nn)
    ov = np.clip(ov, 0, 32).astype(np.float32) / 32.0
    ov[255] = 0.0
    ovx = np.zeros((128, 2, 65), np.float32)
    ovx[:, :, 0] = 1.0
    ovx[:, 0, 1:] = ov[:128]
    ovx[:, 1, 1:] = ov[128:]
    b_ = np.arange(128)[None, :]
    misc = np.zeros((128, 5, 128), np.float32)
    misc[:, 0] = np.eye(128)
    misc[:, 1] = ((a // 64) == (b_ // 64))
    misc[:, 2] = (b_ < a)
    misc[:, 3] = (b_ > a)
    misc[:, 4] = 1.0
    return ind, fw, ovx, misc


def _win_perm():
    q = []
    for j in range(4):
        q += list(range(j * 64, j * 64 + 64)) + list(range((j + 4) * 64, (j + 4) * 64 + 64))
    o_kc, o_vc, o_ksl, o_vsl, o_kwn, o_vwn, o_ga = 512, 640, 768, 896, 1024, 1152, 1280
    o_qb, o_kb, o_vb, o_ma, o_mb = 1304, 1816, 2328, 2840, 3864
    r = lambda s, n: list(range(s, s + n))
    perm = q + r(o_kc, 128) + r(o_vc, 128) + r(o_ksl, 128) + r(o_kwn, 128)
    perm += r(o_qb, 512) + r(o_kb, 512) + r(o_vb, 512)
    perm += r(o_vsl, 128) + r(o_vwn, 128) + r(o_ga, 24)
    perm += r(o_ma, 1024) + r(o_mb, 1024)
    assert len(perm) == IN_W and len(set(perm)) == IN_W
    return np.array(perm)


WC = [(0, 512), (512, 1024), (1024, 1536), (1536, 2048), (2048, 2560), (2560, 2840),
      (2840, 3352), (3352, 3864), (3864, 4376), (4376, 4888)]


def host_prep(inp, S, NB, ncores, batch0=0):
    f = lambda a: np.ascontiguousarray(a, dtype=np.float32)
    ind, fw, ovx, misc = _consts()
    vecT = lambda v: f(np.asarray(v).reshape(-1, 128).T)
    dup = lambda v: f(np.concatenate([v, v], axis=0))
    w1k = np.asarray(inp["cmp_k_w1"][0]).reshape(32, 64, 64).transpose(1, 0, 2).reshape(64, 2048)
    w1v = np.asarray(inp["cmp_v_w1"][0]).reshape(32, 64, 64).transpose(1, 0, 2).reshape(64, 2048)
    w2k = np.asarray(inp["cmp_k_w2"][0])
    w2kpad = np.zeros((64, 2, 128), np.float32)
    w2kpad[:, 0, 0:64] = w2k
    w2kpad[:, 1, 64:128] = w2k
    kng = np.asarray(inp["k_norm_g"][0])
    shared = {
        "adaw": f(inp["ada_w"][0]),
        "adabT": vecT(inp["ada_b"][0]),
        "adab": f(np.asarray(inp["ada_b"][0]).reshape(1, 6144)),
        "n1g": vecT(inp["norm1_g"][0]),
        "n2g": vecT(inp["norm2_g"][0]),
        "win": f(np.asarray(inp["w_in"][0])[:, _win_perm()]),
        "cw1k": dup(w1k), "cw1v": dup(w1v),
        "cposT": dup(np.asarray(inp["cmp_pos"][0]).T),
        "cw2k": f(w2kpad.reshape(64, 256)),
        "cw2v": f(inp["cmp_v_w2"][0]),
        "qkg": f(np.stack([np.tile(np.asarray(inp["q_norm_g"][0]), 2), np.tile(kng[0], 2),
                           np.tile(kng[1], 2), np.tile(kng[2], 2)], axis=1)),
        "wupn": f(inp["w_up_nsa"][0]), "wups": f(inp["w_up_sb"][0]),
        "wout": f(inp["w_out"][0]), "w1": f(inp["mlp_w1"][0]), "w2": f(inp["mlp_w2"][0]),
        "relb": f(np.asarray(inp["rel_bias"]).reshape(1, 256)),
        "c_ind": ind, "c_fw": fw, "c_ovx": f(ovx.reshape(128, 130)), "c_misc": f(misc.reshape(128, 640)),
    }
    maps = []
    x = np.asarray(inp["x"])
    c = np.asarray(inp["c"])
    for core in range(ncores):
        b0 = batch0 + core * NB
        m = dict(shared)
        m["x"] = f(x[b0:b0 + NB, :S])
        m["cT"] = f(np.stack([c[b0 + i].reshape(8, 128).T for i in range(NB)], axis=0))
        maps.append(m)
    return maps


class Prog:
    def __init__(self, S, NB):
        self.S_len = S
        self.NB = NB
        self.dbg = False
        self.dbg_names = []
        self.NG = S // 512
        self.NT = S // 128

    def mm(self, out, lhsT, rhs, start=True, stop=True):
        self.S.op("pe", lambda h: h.matmul(out.ap, lhsT.ap, rhs.ap, start=start, stop=stop),
                  [lhsT, rhs], [out])

    def tpose(self, out, in_):
        idn = self.ident
        self.S.op("pe", lambda h: h.transpose(out.ap, in_.ap, idn.ap), [in_, idn], [out])

    def act(self, out, in_, func, scale=1.0, bias=0.0, accum=None):
        ins = [in_]
        outs = [out]
        kw = dict(out=out.ap, in_=in_.ap, func=func)
        if isinstance(scale, V):
            ins.append(scale)
            kw["scale"] = scale.ap
        else:
            kw["scale"] = float(scale)
        if isinstance(bias, V):
            ins.append(bias)
            kw["bias"] = bias.ap
        elif bias != 0.0:
            kw["bias"] = float(bias)
        if accum is not None:
            outs.append(accum)
            kw["accum_out"] = accum.ap
        self.S.op("act", lambda h: h.activation(**kw), ins, outs)

    def tt(self, eng, out, a, b, op):
        self.S.op(eng, lambda h: h.tensor_tensor(out.ap, a.ap, b.ap, op), [a, b], [out])

    def ts(self, eng, out, a, s1, s2, op0, op1=None):
        ins = [a]
        if isinstance(s1, V):
            ins.append(s1)
        if isinstance(s2, V):
            ins.append(s2)
        g = lambda s: s.ap if isinstance(s, V) else s
        if op1 is None:
            self.S.op(eng, lambda h: h.tensor_scalar(out.ap, a.ap, g(s1), None, op0), ins, [out])
        else:
            self.S.op(eng, lambda h: h.tensor_scalar(out.ap, a.ap, g(s1), g(s2), op0, op1), ins, [out])

    def stt(self, out, a, s, b, op0, op1):
        ins = [a, b]
        if isinstance(s, V):
            ins.append(s)
        g = s.ap if isinstance(s, V) else s
        self.S.op("dve", lambda h: h.scalar_tensor_tensor(out.ap, a.ap, g, b.ap, op0, op1), ins, [out])

    def copy(self, eng, out, in_):
        if eng == "act":
            self.act(out, in_, AF.Copy)
        else:
            self.S.op(eng, lambda h: h.tensor_copy(out.ap, in_.ap), [in_], [out])

    def memset(self, eng, out, val):
        self.S.op(eng, lambda h: h.memset(out.ap, val), [], [out])

    def dbg_dump(self, name, v):
        if not getattr(self, "dbg", False):
            return
        shp = list(v.ap.shape)
        n = int(np.prod(shp[1:]))
        t = self.nc.dram_tensor("dbg_" + name, [shp[0], n], F32, kind="ExternalOutput").ap()
        if len(shp) == 3:
            t = t.rearrange("p (a b) -> p a b", a=shp[1])
        elif len(shp) == 4:
            t = t.rearrange("p (a b c) -> p a b c", a=shp[1], b=shp[2])
        self.dbg_names.append("dbg_" + name)
        self.dma("pool", V(t, Buf("dbg")), v)

    def dma(self, q, out, in_):
        self.S.op(q, lambda h: h.dma_start(out=out.ap, in_=in_.ap), [in_], [out], dma=True)

    def alloc(self, shape, dt, name=""):
        n = int(np.prod(shape[1:]))
        nbytes = n * (2 if dt == BF16 else 4)
        nbytes = (nbytes + 31) // 32 * 32
        off = self.sb_off
        self.sb_off += nbytes
        assert self.sb_off <= self.sb_bytes, (name, self.sb_off, self.sb_bytes)
        ap = self.sb[:, off // 4:(off + nbytes) // 4]
        if dt == BF16:
            ap = ap.bitcast(BF16)
        ap = ap[:, 0:n]
        v = V(ap, Buf(name))
        if len(shape) == 3:
            v = v.re("p (a b) -> p a b", a=shape[1])
        elif len(shape) == 4:
            v = v.re("p (a b c) -> p a b c", a=shape[1], b=shape[2])
        if shape[0] < 128:
            v = v[0:shape[0]]
        return v

    def bank(self, k, dt=F32):
        ap = self.ps[:, 512 * k:512 * (k + 1)]
        if dt == BF16:
            ap = ap.bitcast(BF16)
        return V(ap, self.bank_buf[k])

    def plan_chunks(self):
        d = self.d
        r8 = lambda ap: ap.rearrange("(k p) n -> p k n", p=128)
        ch = []
        ch.append(("cw1k", d["cw1k"], (128, 2048)))
        ch.append(("cw1v", d["cw1v"], (128, 2048)))
        for b in range(self.NB):
            for j in range(12):
                ch.append(("ada%d" % j, r8(d["adaw"])[:, :, 512 * j:512 * j + 512], (128, 8, 512)))
            for g in range(self.NG):
                for j in range(6):
                    c0, c1 = WC[j]
                    ch.append(("win%d" % j, r8(d["win"])[:, :, c0:c1], (128, 8, c1 - c0)))
                ch.append(("cw1k", d["cw1k"], (128, 2048)))
                ch.append(("cw1v", d["cw1v"], (128, 2048)))
                for j in (6, 8):
                    c0, c1 = WC[j]
                    ch.append(("win%d" % j, r8(d["win"])[:, :, c0:c1], (128, 8, c1 - c0)))
                ch.append(("wupn", r8(d["wupn"]), (128, 4, 1024)))
                ch.append(("wups", r8(d["wups"]), (128, 4, 1024)))
                for j in (7, 9):
                    c0, c1 = WC[j]
                    ch.append(("win%d" % j, r8(d["win"])[:, :, c0:c1], (128, 8, c1 - c0)))
                for j in range(2):
                    ch.append(("wout%d" % j, r8(d["wout"])[:, :, 512 * j:512 * j + 512], (128, 8, 512)))
                for j in range(8):
                    ch.append(("w1_%d" % j, r8(d["w1"])[:, :, 512 * j:512 * j + 512], (128, 8, 512)))
                    ch.append(("w2_%d" % j, r8(d["w2"])[:, 4 * j:4 * j + 4, :], (128, 4, 1024)))
        self.chunks = ch
        self.ch_pos = 0
        self.ch_issued = 0

    def _slot_view(self, k):
        tag, src, shape = self.chunks[k]
        slot = self.wslots[k % len(self.wslots)]
        if len(shape) == 2:
            return slot[:, 0:shape[1]]
        v = slot.re("p (a b) -> p a b", a=shape[1])
        if shape[1] == 8 and shape[2] < 512:
            v = v[:, :, 0:shape[2]]
        return v

    def wnext(self, tag, live=1):
        k = self.ch_pos
        assert self.chunks[k][0] == tag, (self.chunks[k][0], tag)
        self.ch_pos += 1
        ns = len(self.wslots)
        import os
        depth = int(os.environ.get("PREF", "99"))
        while self.ch_issued < min(len(self.chunks), k + min(ns - live, depth) + 1):
            kk = self.ch_issued
            if not (os.environ.get("NOWDMA") == "1" and kk > 40):
                self.dma("pool", self._slot_view(kk), V(self.chunks[kk][1], self.dram_buf))
            self.ch_issued += 1
        return self._slot_view(k)

    def build(self):
        S, NB, NG, NT = self.S_len, self.NB, self.NG, self.NT
        nc = bass.Bass("TRN2", target_bir_lowering=False)
        self.nc = nc
        din = lambda name, shape: nc.dram_tensor(name, list(shape), F32, kind="ExternalInput").ap()
        d = {}
        d["x"] = din("x", [NB, S, D])
        d["cT"] = din("cT", [NB, 128, 8])
        d["adaw"] = din("adaw", [D, 6 * D])
        d["adabT"] = din("adabT", [128, 48])
        d["adab"] = din("adab", [1, 6 * D])
        d["n1g"] = din("n1g", [128, 8])
        d["n2g"] = din("n2g", [128, 8])
        d["win"] = din("win", [D, IN_W])
        d["cw1k"] = din("cw1k", [128, 2048])
        d["cw1v"] = din("cw1v", [128, 2048])
        d["cposT"] = din("cposT", [128, 32])
        d["cw2k"] = din("cw2k", [64, 256])
        d["cw2v"] = din("cw2v", [64, 64])
        d["qkg"] = din("qkg", [128, 4])
        d["wupn"] = din("wupn", [512, D])
        d["wups"] = din("wups", [512, D])
        d["wout"] = din("wout", [D, D])
        d["w1"] = din("w1", [D, 4 * D])
        d["w2"] = din("w2", [4 * D, D])
        d["relb"] = din("relb", [1, 256])
        d["c_ind"] = din("c_ind", [32, 128, 272])
        d["c_fw"] = din("c_fw", [128, 126])
        d["c_ovx"] = din("c_ovx", [128, 130])
        d["c_misc"] = din("c_misc", [128, 640])
        self.d = d
        self.y = nc.dram_tensor("y", [NB, S, D], F32, kind="ExternalOutput").ap()
        self.dram_buf = Buf("dram_in")
        self.y_buf = Buf("y")

        with ExitStack() as st:
            self.sb_bytes = 212832
            self.sb = st.enter_context(nc.sbuf_tensor("sb", [128, self.sb_bytes // 4], F32))
            self.sb_off = 0
            self.ps = st.enter_context(nc.psum_tensor("ps", [128, 4096], F32))
            self.bank_buf = [Buf("bank%d" % k) for k in range(8)]
            sems = [st.enter_context(nc.semaphore("s_" + e)) for e in Sched.ENGS]
            dsems = {q: [st.enter_context(nc.semaphore("d%s_%d" % (q, i))) for i in range(16)] for q in ("sp", "pool")}
            self.esems = [[st.enter_context(nc.semaphore("s%d_%s" % (i, e))) for e in Sched.ENGS]
                          for i in range((NB * NG + 3) // 4)]
            self.S = Sched(sems, dsems)
            self.setup()
            for b in range(NB):
                self.seq_init(b)
                import os
                for g in range(min(NG, int(os.environ.get("MAXG", "99")))):
                    self.group(b, g)
            self.S.final_wait("sp")
            with nc.Block() as block:
                self.S.emit_all(block)
        return nc

    def setup(self):
        d = self.d
        A = self.alloc
        DR = lambda ap: V(ap, self.dram_buf)
        misc = A([128, 5, 128], BF16, "misc")
        self.dma("pool", misc, DR(d["c_misc"].rearrange("p (a b) -> p a b", a=5)))
        self.ident = misc[:, 0, :]
        self.blk1 = misc[:, 1, :]
        self.mstrict_bf = misc[:, 2, :]
        self.ustrict_bf = misc[:, 3, :]
        self.onesbf = misc[:, 4, :]
        self.mstrict_f = A([128, 128], F32, "mstrict_f")
        self.dma("sp", self.mstrict_f, DR(d["c_misc"][:, 256:384]))
        self.onecol = A([128, 1], F32, "onecol")
        self.memset("dve", self.onecol, 1.0)
        self.fwide = A([128, 126], F32, "fwide")
        self.dma("sp", self.fwide, DR(d["c_fw"]))
        self.qkg = A([128, 4], F32, "qkg")
        self.dma("sp", self.qkg, DR(d["qkg"]))
        self.n1g = A([128, 8], F32, "n1g")
        self.dma("sp", self.n1g, DR(d["n1g"]))
        self.n2g = A([128, 8], F32, "n2g")
        self.dma("sp", self.n2g, DR(d["n2g"]))
        self.adabT = A([128, 48], F32, "adabT")
        self.dma("sp", self.adabT, DR(d["adabT"]))
        self.cw2k = A([64, 2, 128], BF16, "cw2k")
        self.dma("pool", self.cw2k, DR(d["cw2k"].rearrange("p (a b) -> p a b", a=2)))
        self.cw2v = A([64, 64], BF16, "cw2v")
        self.dma("pool", self.cw2v, DR(d["cw2v"]))
        self.cposT = A([128, 32], BF16, "cposT")
        self.dma("pool", self.cposT, DR(d["cposT"]))
        self.pbias = A([64, 2], F32, "pbias")
        self.tbl = A([128, 32, 8], F32, "tbl")
        self.dma("sp", self.tbl.re("p a b -> p (a b)"), DR(d["relb"].rearrange("a b -> (a b)").partition_broadcast(128)))
        self.c31 = self.tbl[:, 31, :]
        self.R = A([128, 8, 272], BF16, "R")
        self.A1 = A([128, 8], F32, "A1")
        self.B1 = A([128, 8], F32, "B1")
        self.A2 = A([128, 8], F32, "A2")
        self.B2 = A([128, 8], F32, "B2")
        self.g1bc = A([128, 1024], F32, "g1bc")
        self.g2bc = A([128, 1024], F32, "g2bc")
        self.wslots = [A([128, 4096], BF16, "wslot%d" % i) for i in range(4)]
        S, NT = self.S_len, self.NT
        self.kbT = A([128, 4, S], BF16, "kbT")
        self.vb = A([128, NT, 512], BF16, "vb")
        self.kslT = A([128, S], BF16, "kslT")
        self.vsl = A([128, NT, 2, 65], BF16, "vsl")
        self.kwT = A([128, 1024], BF16, "kwT")
        self.vw = A([128, 8, 2, 65], BF16, "vw")
        self.kcT = A([128, 528], BF16, "kcT")
        self.vcT = A([128, 528], BF16, "vcT")
        self.kcmpT = A([128, 256], BF16, "kcmpT")
        self.hidTv = A([64, 2, 256], BF16, "hidTv")
        self.vcx = A([128, 2, 2, 129], BF16, "vcx")
        self.uT = A([128, 8, 512], BF16, "uT")
        self.gsig = A([128, 4, 24], F32, "gsig")
        self.ecb = Rot([A([128, 256], BF16, "ecb%d" % i) for i in range(2)])
        import os
        padb = int(os.environ.get("KPAD", "0"))
        if padb:
            A([128, padb // 4], F32, "pad")
        self.arena0 = self.sb_off
        self.memset("pool", self.vsl[:, :, :, 64:65], 1.0)
        self.memset("pool", self.vw[:, :, :, 64:65], 1.0)
        for ch in range(2):
            for kvh in range(2):
                self.dma("pool", self.vcx[:, ch, kvh, 64:129], DR(d["c_ovx"][:, 65 * ch:65 * ch + 65]))
        self.plan_chunks()
        etbl = A([128, 32, 8], F32, "etbl")
        R32 = A([128, 8, 272], F32, "R32")
        indb = [A([128, 272], F32, "indb%d" % i) for i in range(2)]
        self.tt("dve", etbl, self.tbl, self.tbl[:, 31:32, :].bc([128, 32, 8]), ALU.subtract)
        self.act(etbl, etbl, AF.Exp)
        for b in range(32):
            ib = indb[b % 2]
            self.dma("sp", ib, DR(d["c_ind"][b]))
            for h in range(8):
                if b == 0:
                    self.ts("dve", R32[:, h, :], ib, etbl[:, b, h:h + 1], None, ALU.mult)
                else:
                    self.stt(R32[:, h, :], ib, etbl[:, b, h:h + 1], R32[:, h, :], ALU.mult, ALU.add)
        self.copy("dve", self.R, R32)
        for i, tag in enumerate(("cw1k", "cw1v")):
            W1 = self.wnext(tag)
            pb = self.bank(i)
            for l in range(32):
                self.mm(pb[0:64, 0:1], W1[0:64, 64 * l:64 * l + 64], self.cposT[0:64, l:l + 1], l == 0, l == 31)
            self.copy("dve", self.pbias[:, i:i + 1], pb[0:64, 0:1])
        self.S.barrier()
        self.sb_off = self.arena0
        self.alloc_arena()

    def alloc_arena(self):
        A = self.alloc
        a0 = self.sb_off
        self.xbuf = A([128, 4, 1024], F32, "xbuf")
        self.xbuf_t = [V(self.xbuf.ap[:, t, :], Buf("xbuf%d" % t)) for t in range(4)]
        x_end = self.sb_off
        self.acc = A([128, 4, 1024], F32, "acc")
        self.acc_t = [V(self.acc.ap[:, t, :], Buf("acc%d" % t)) for t in range(4)]
        self.hT = Rot([A([128, 4, 512], BF16, "hT%d" % i) for i in range(2)])
        self.relu_t = Rot([A([128, 512], F32, "relu%d" % i) for i in range(2)])
        self.xn = Rot([A([128, 1024], BF16, "xn%d" % i) for i in range(2)])
        self.sq_junk = self.relu_t.items[0].cast(BF16)
        self.small = Rot([A([128, 4], F32, "small%d" % i) for i in range(4)])
        a1 = self.sb_off
        self.sb_off = a0
        self.qaT = A([128, 4, 512], BF16, "qaT")
        self.qbT = A([128, 4, 512], BF16, "qbT")
        self.cbufA = A([128, 512], F32, "cbufA")
        self.cbufB = A([128, 512], F32, "cbufB")
        self.carA = Rot([A([128, 1], F32, "carA%d" % i) for i in range(2)])
        self.carB = Rot([A([128, 1], F32, "carB%d" % i) for i in range(2)])
        bfs = [A([128, 512], BF16, "bf512_%d" % i) for i in range(8)]
        self.bf512 = Rot(bfs[0:4])
        self.bfA = Rot(bfs[4:6])
        self.bfB = Rot(bfs[6:8])
        mix_off = self.sb_off
        self.rawc = A([128, 8, 129], F32, "rawc")
        self.raws = A([128, 8, 130], F32, "raws")
        raw_end = self.sb_off
        assert mix_off >= x_end, (mix_off, x_end)
        self.sb_off = mix_off
        self.mixT = A([128, 8, 512], BF16, "mixT")
        assert self.sb_off <= raw_end, (self.sb_off, raw_end)
        self.sb_off = raw_end
        self.oacc = A([128, 8, 64], F32, "oacc")
        self.imp = A([128, 2, 64], F32, "imp")
        self.imp2 = A([128, 64], F32, "imp2")
        self.sel = A([128, 2, 64], BF16, "sel")
        self.m8 = Rot([A([128, 8], F32, "m8_%d" % i) for i in range(2)])
        self.sm8 = Rot([A([128, 8, 2], F32, "sm8_%d" % i) for i in range(4)])
        self.obf = Rot([A([128, 512], BF16, "obf%d" % i) for i in range(1)])
        assert self.sb_off >= x_end, (self.sb_off, x_end)
        self.onT = A([128, 4, 512], BF16, "onT")
        self.osT = A([128, 4, 512], BF16, "osT")
        f32s = [A([128, 512], F32, "f32t%d" % i) for i in range(9)]
        self.f32t = Rot(f32s[0:5])
        self.f32n = Rot(f32s[0:1])
        self.eA = Rot(f32s[1:3])
        self.spA = Rot(f32s[3:5])
        self.eB = Rot(f32s[5:7])
        self.spB = Rot(f32s[7:9])
        self.sgt = Rot([A([128, 512], BF16, "sgt%d" % i) for i in range(2)])
        a2 = self.sb_off
        self.sb_off = max(a1, a2)
        self.arena_bytes = self.sb_off - a0

    def seq_init(self, b):
        d = self.d
        DR = lambda ap: V(ap, self.dram_buf)
        S = self.S
        S.barrier()
        cT = self.f32t.next()[:, 0:8]
        self.dma("sp", cT, DR(d["cT"][b]))
        sc = self.sgt.next()[:, 0:8]
        self.act(sc, cT, AF.Silu)
        scb = self.mixT[:, 0:2, :].re("p a b -> p (a b)").re("p (k m) -> p k m", k=8)
        for k in range(8):
            self.ts("dve", scb[:, k, :], self.onesbf, sc[:, k:k + 1], None, ALU.mult)
        fm_dst = {0: self.B1, 1: self.B1, 2: self.A1, 3: self.A1, 6: self.B2, 7: self.B2, 8: self.A2, 9: self.A2}
        pool = Rot([self.bank(k) for k in range(4)])
        for j in range(12):
            W = self.wnext("ada%d" % j)
            if j in fm_dst:
                dst = fm_dst[j]
                pb = pool.next()
                for m in range(4):
                    for k in range(8):
                        self.mm(pb[:, m:m + 1], W[:, k, 128 * m:128 * m + 128], sc[:, k:k + 1], k == 0, k == 7)
                c0 = (j % 2) * 4
                self.tt("dve", dst[:, c0:c0 + 4], pb[:, 0:4], self.adabT[:, 4 * j:4 * j + 4], ALU.add)
            else:
                dst = self.g1bc if j < 6 else self.g2bc
                pb = pool.next()
                for k in range(8):
                    self.mm(pb, scb[:, k, :], W[:, k, :], k == 0, False)
                arow = self.bf512.next()[0:1, :]
                self.dma("pool", arow, DR(d["adab"][0:1, 512 * j:512 * j + 512]))
                self.mm(pb, self.onesbf[0:1, :], arow, False, True)
                c0 = (j % 2) * 512
                self.copy("act", dst[:, c0:c0 + 512], pb)
        for Av, gv in ((self.A1, self.n1g), (self.A2, self.n2g)):
            self.stt(Av, Av, 1.0, gv, ALU.add, ALU.mult)
        self.memset("pool", self.kcT[:, 0:16], 0.0)
        self.memset("pool", self.vcT[:, 0:16], 0.0)
        self.memset("pool", self.kcmpT, 0.0)
        self.memset("pool", self.hidTv, 0.0)
        for e in self.ecb.items:
            self.memset("pool", e, 0.0)
        S.barrier()

    def norm_tile(self, src, tt_, Am, Bm):
        xn = self.xn.next()
        sm = self.small.next()
        self.act(self.sq_junk, src, AF.Square, accum=sm[:, 0:1])
        self.act(sm[:, 1:2], sm[:, 0:1], AF.Sqrt, scale=1.0 / D, bias=EPS)
        self.S.op("dve", lambda h: h.reciprocal(sm.ap[:, 2:3], sm.ap[:, 1:2]), [sm], [sm])
        self.act(xn, src, AF.Copy, scale=sm[:, 2:3])
        pT = self.tp_banks.next().cast(BF16).re("p (c n) -> p c n", c=8)
        for c in range(8):
            self.tpose(pT[:, c, :], xn[:, 128 * c:128 * c + 128])
        for c in range(8):
            self.ts("dve", self.uT[:, c, 128 * tt_:128 * tt_ + 128], pT[:, c, :], Am[:, c:c + 1], Bm[:, c:c + 1],
                    ALU.mult, ALU.add)

    def qk_norm(self, zps, gcol, out, n):
        sq = self.bf512.next()
        self.act(sq[:, 0:n], zps[:, 0:n], AF.Square)
        sp = self.pB.next()
        self.mm(sp[:, 0:n], self.blk1, sq[:, 0:n])
        rt = self.f32t.next()
        self.act(rt[:, 0:n], sp[:, 0:n], AF.Sqrt, scale=1.0 / DH, bias=EPS)
        self.S.op("dve", lambda h: h.reciprocal(rt.ap[:, 0:n], rt.ap[:, 0:n]), [rt], [rt])
        self.stt(out, zps[:, 0:n], gcol, rt[:, 0:n], ALU.mult, ALU.mult)

    def group(self, b, g):
        d = self.d
        DR = lambda ap: V(ap, self.dram_buf)
        S = self.S
        S.barrier()
        if (b * self.NG + g) % 4 == 0:
            S.new_epoch(self.esems[(b * self.NG + g) // 4])
        self.pA = Rot([self.bank(k) for k in range(4)])
        self.pB = Rot([self.bank(k) for k in (4, 5)])
        self.tp_banks = Rot([self.bank(k) for k in (6, 7)])
        for t in range(4):
            self.dma("sp", self.xbuf_t[t], DR(d["x"][b, (4 * g + t) * 128:(4 * g + t + 1) * 128, :]))
        for t in range(4):
            self.norm_tile(self.xbuf_t[t], t, self.A1, self.B1)
        import os
        stopat = int(os.environ.get("STOPAT", "99")) if g == int(os.environ.get("STOPG", "6")) else 99
        if stopat <= 1:
            return
        S.barrier()
        S.trace_ops = (g == 6 and os.environ.get("TRACEOPS") == "1")
        uT = self.uT
        gs = slice(512 * g, 512 * g + 512)

        def fm(W, m):
            pb = self.pA.next()
            for k in range(8):
                self.mm(pb, W[:, k, 128 * m:128 * m + 128], uT[:, k, :], k == 0, k == 7)
            return pb

        W = self.wnext("win0")
        for m in range(4):
            self.qk_norm(fm(W, m), self.qkg[:, 0:1], self.qaT[:, m, :], 512)
        if g == int(os.environ.get("STOPG", "6")) and os.environ.get("PJ") == "1":
            return
        W = self.wnext("win1")
        self.copy("act", self.kcT[:, 16:528], fm(W, 0))
        self.copy("act", self.vcT[:, 16:528], fm(W, 1))
        self.qk_norm(fm(W, 2), self.qkg[:, 2:3], self.kslT[:, gs], 512)
        rs = slice(512 * (g % 2), 512 * (g % 2) + 512)
        self.qk_norm(fm(W, 3), self.qkg[:, 3:4], self.kwT[:, rs], 512)
        if g == int(os.environ.get("STOPG", "6")) and os.environ.get("PJ") == "2":
            return
        W = self.wnext("win2")
        for m in range(4):
            self.copy("act" if m % 2 else "dve", self.qbT[:, m, :], fm(W, m))
        if g == int(os.environ.get("STOPG", "6")) and os.environ.get("PJ") == "3":
            return
        W = self.wnext("win3")
        for m in range(4):
            self.copy("act" if m % 2 else "dve", self.kbT[:, m, gs], fm(W, m))
        if g == int(os.environ.get("STOPG", "6")) and os.environ.get("PJ") == "4":
            return
        W = self.wnext("win4")
        for t in range(4):
            pb = self.pA.next()
            for k in range(8):
                self.mm(pb, uT[:, k, 128 * t:128 * t + 128], W[:, k, :], k == 0, k == 7)
            self.copy("act" if t % 2 else "dve", self.vb[:, 4 * g + t, :], pb)
        if g == int(os.environ.get("STOPG", "6")) and os.environ.get("PJ") == "5":
            return
        W = self.wnext("win5")
        for t in range(4):
            pb = self.pA.next()
            for k in range(8):
                self.mm(pb[:, 0:280], uT[:, k, 128 * t:128 * t + 128], W[:, k, :], k == 0, k == 7)
            self.copy("act", self.vsl[:, 4 * g + t, :, 0:64], pb[:, 0:128].re("p (a b) -> p a b", a=2))
            self.copy("act", self.vw[:, (4 * g + t) % 8, :, 0:64], pb[:, 128:256].re("p (a b) -> p a b", a=2))
            self.act(self.gsig[:, t, :], pb[:, 256:280], AF.Sigmoid)
        S.trace_ops = False
        if stopat <= 2:
            return
        c_lo, c_hi = max(0, 32 * g - 1), 32 * g + 30
        n = c_hi - c_lo + 1
        for is_k, src, tag in ((True, self.kcT, "cw1k"), (False, self.vcT, "cw1v")):
            W1 = self.wnext(tag)
            hk = self.bf512.next()[0:64, 0:64].re("p (a b) -> p a b", a=2)
            for kvh in range(2):
                pbs = 64 * kvh
                pb = self.pA.next()
                for l in range(32):
                    col0 = 16 + 16 * c_lo - 512 * g + l
                    rhs = src[pbs:pbs + 64, col0:col0 + 16 * (n - 1) + 1:16]
                    self.mm(pb[0:64, 0:n], W1[pbs:pbs + 64, 64 * l:64 * l + 64], rhs, l == 0, l == 31)
                if is_k:
                    self.act(hk[:, kvh, 0:n], pb[0:64, 0:n], AF.Silu, bias=self.pbias[:, 0:1])
                else:
                    self.act(self.hidTv[:, kvh, c_lo:c_hi + 1], pb[0:64, 0:n], AF.Silu, bias=self.pbias[:, 1:2])
            if is_k:
                pb = self.pA.next()
                self.mm(pb[:, 0:n], self.cw2k[:, 0, :], hk[:, 0, 0:n], True, False)
                self.mm(pb[:, 0:n], self.cw2k[:, 1, :], hk[:, 1, 0:n], False, True)
                self.qk_norm(pb, self.qkg[:, 1:2], self.kcmpT[:, c_lo:c_hi + 1], n)
            else:
                for ch in range(c_lo // 128, c_hi // 128 + 1):
                    for kvh in range(2):
                        pb = self.pA.next()
                        self.mm(pb[:, 0:64], self.hidTv[:, kvh, 128 * ch:128 * ch + 128], self.cw2v)
                        self.copy("act", self.vcx[:, ch, kvh, 0:64], pb[:, 0:64])
        self.copy("pool", self.kcT[:, 0:16], self.kcT[:, 512:528])
        self.copy("pool", self.vcT[:, 0:16], self.vcT[:, 512:528])
        if stopat <= 3:
            return
        self.nsa_score = Rot([self.bank(2), self.bank(4)])
        self.pro_bank = self.bank(3)
        self.tp_banks = Rot([self.bank(7)])
        import os
        skn = int(os.environ.get("SKIP_NSA_FROM", "999"))
        sks = int(os.environ.get("SKIP_SB_FROM", "999"))
        for t in range(4):
            self.attn_tile(g, t, 4 * g + t < skn, 4 * g + t < sks)
        if g == 1 and b == 0:
            self.dbg_dump("onT", self.onT)
            self.dbg_dump("osT", self.osT)
            self.dbg_dump("kcmpT", self.kcmpT)
            self.dbg_dump("vcx", self.vcx)
            self.dbg_dump("sel", self.sel)
            self.dbg_dump("imp", self.imp)
            self.dbg_dump("rawc", self.rawc)
            self.dbg_dump("raws", self.raws)
        S.barrier()
        self.pA = Rot([self.bank(k) for k in range(6)])
        for t in range(4):
            self.dma("sp", self.xbuf_t[t], DR(d["x"][b, (4 * g + t) * 128:(4 * g + t + 1) * 128, :]))
        Wma = self.wnext("win6", live=1)
        Wmb = self.wnext("win8", live=2)
        Wn = self.wnext("wupn", live=3)
        Ws = self.wnext("wups", live=4)
        for m in range(8):
            if m == 4:
                Wma = self.wnext("win7", live=4)
                Wmb = self.wnext("win9", live=4)
            sga = self.sgt.next()
            self.act(sga, fm(Wma, m % 4), AF.Sigmoid)
            sgb = self.sgt.next()
            self.act(sgb, fm(Wmb, m % 4), AF.Sigmoid)
            pa = self.pA.next()
            for k in range(4):
                self.mm(pa, Wn[:, k, 128 * m:128 * m + 128], self.onT[:, k, :], k == 0, k == 3)
            pb2 = self.pA.next()
            for k in range(4):
                self.mm(pb2, Ws[:, k, 128 * m:128 * m + 128], self.osT[:, k, :], k == 0, k == 3)
            t1 = self.f32t.next()
            self.tt("dve", t1, pa, sga, ALU.mult)
            t2 = self.f32t.next()
            self.tt("dve", t2, pb2, sgb, ALU.mult)
            self.tt("pool", self.mixT[:, m, :], t1, t2, ALU.add)
        for cc in range(2):
            W = self.wnext("wout%d" % cc)
            for t in range(4):
                pb = self.pA.next()
                for k in range(8):
                    self.mm(pb, self.mixT[:, k, 128 * t:128 * t + 128], W[:, k, :], k == 0, k == 7)
                t1 = self.f32t.next()
                self.tt("dve", t1, pb, self.g1bc[:, 512 * cc:512 * cc + 512], ALU.mult)
                hv = self.xbuf_t[t][:, 512 * cc:512 * cc + 512]
                self.tt("dve", hv, hv, t1, ALU.add)
        if stopat <= 4:
            return
        S.barrier()
        self.pA = Rot([self.bank(k) for k in range(4)])
        self.pB = Rot([self.bank(k) for k in (4, 5)])
        for t in range(4):
            self.norm_tile(self.xbuf_t[t], t, self.A2, self.B2)
        for j in range(8):
            W1 = self.wnext("w1_%d" % j)
            hT = self.hT.next()
            for m in range(4):
                pb = self.pA.next()
                for k in range(8):
                    self.mm(pb, W1[:, k, 128 * m:128 * m + 128], uT[:, k, :], k == 0, k == 7)
                r = self.relu_t.next()
                self.act(r, pb, AF.Relu)
                self.tt("pool", hT[:, m, :], r, r, ALU.mult)
            W2 = self.wnext("w2_%d" % j)
            for t in range(4):
                for cc in range(2):
                    pb = self.pB.next()
                    for m in range(4):
                        self.mm(pb, hT[:, m, 128 * t:128 * t + 128], W2[:, m, 512 * cc:512 * cc + 512], m == 0, m == 3)
                    av = self.acc_t[t][:, 512 * cc:512 * cc + 512]
                    if j == 0:
                        self.copy("act", av, pb)
                    else:
                        self.tt("dve", av, av, pb, ALU.add)
        for t in range(4):
            self.tt("dve", self.acc_t[t], self.acc_t[t], self.g2bc, ALU.mult)
            self.tt("dve", self.xbuf_t[t], self.xbuf_t[t], self.acc_t[t], ALU.add)
            self.dma("sp", V(self.y[b, (4 * g + t) * 128:(4 * g + t + 1) * 128, :], Buf("y")), self.xbuf_t[t])
        S.barrier()

    def soft_stage1(self, it):
        nb = len(it["vrhs"])
        w = 128 * nb
        sc = self.nsa_score.next()
        col = 0
        for kr, wk in it["krhs"]:
            self.mm(sc[:, col:col + wk], it["qT"], kr)
            col += wk
        E = self.bf512.next()
        h = it["h"]
        self.act(E[:, 0:w], sc[:, 0:w], AF.Exp, scale=0.125)
        it["E"] = E

    def soft_stage2(self, it):
        nb = len(it["vrhs"])
        w = 128 * nb
        E = it["E"]
        for (c0, ncol, mk, is3d) in it["masks"]:
            ev = E[:, c0:c0 + ncol]
            if is3d:
                ev = ev.re("p (n k) -> p n k", k=64)
            self.tt("dve", ev, ev, mk, ALU.mult)
        yield
        tp = self.tp_banks.next().cast(BF16)
        for n_ in range(nb):
            self.tpose(tp[:, 128 * n_:128 * n_ + 128], E[:, 128 * n_:128 * n_ + 128])
        ET = self.bf512.next()
        self.copy("dve", ET[:, 0:w], tp[:, 0:w])
        yield
        for n_ in range(nb):
            self.mm(it["acc"], ET[:, 128 * n_:128 * n_ + 128], it["vrhs"][n_],
                    it["first"] and n_ == 0, it["last"] and n_ == nb - 1)
        if it["fin"] is not None:
            self.copy("dve", self.raws[:, it["fin"], :], it["ob"][:, 0:130])
        yield

    def nsa_prologue(self, g, t):
        i = 4 * g + t
        qc = slice(128 * t, 128 * t + 128)
        Wc = 8 * i + 7
        nch = 1 if Wc <= 128 else 2
        lo, hi = max(0, 8 * i - 9), 8 * i + 7
        off = 8 * i - 9

        def st1(h):
            pbs, j = 64 * (h // 4), h % 4
            qT = self.qaT[pbs:pbs + 64, j, qc]
            sc = self.nsa_score.next()
            self.mm(sc[:, 0:Wc], qT, self.kcmpT[pbs:pbs + 64, 0:Wc])
            E = self.ecb.next()
            self.act(E[:, 0:Wc], sc[:, 0:Wc], AF.Exp, scale=0.125)
            return E

        def st2(h, E):
            kvh = h // 4
            self.tt("dve", E[:, lo:hi], E[:, lo:hi], self.R[:, h, 256 + lo - off:256 + hi - off], ALU.mult)
            yield
            tp = self.tp_banks.next().cast(BF16)
            for ch in range(nch):
                self.tpose(tp[:, 128 * ch:128 * ch + 128], E[:, 128 * ch:128 * ch + 128])
            ET = self.bf512.next()
            self.copy("act", ET[:, 0:128 * nch], tp[:, 0:128 * nch])
            yield
            oc = self.pro_bank
            for ch in range(nch):
                self.mm(oc[:, 0:129], ET[:, 128 * ch:128 * ch + 128], self.vcx[:, ch, kvh, :], ch == 0, ch == nch - 1)
            self.copy("dve", self.rawc[:, h, :], oc[:, 0:129])
            yield

        prev = None
        for h in range(8):
            E = st1(h)
            yield
            if prev is not None:
                yield from st2(*prev)
            prev = (h, E)
        yield from st2(*prev)
        rsc = self.sm8.next()[:, :, 0]
        self.ts("dve", rsc, self.rawc[:, :, 64], 1e-30, None, ALU.max)
        self.S.op("dve", lambda h_: h_.reciprocal(rsc.ap, rsc.ap), [rsc], [rsc])
        tmp = self.f32n.next().re("p (a b) -> p a b", a=8)
        rsc3 = V(rsc.ap.unsqueeze(2), rsc.buf)
        self.tt("dve", tmp, self.rawc[:, :, 65:129], rsc3.bc([128, 8, 64]), ALU.mult)
        self.S.op("dve", lambda h_: h_.tensor_reduce(self.imp.ap, tmp.ap.rearrange("p (k g) n -> p k n g", g=4),
                                                     AX.X, ALU.add), [tmp], [self.imp])
        fw = V(self.fwide.ap[:, 62 - 2 * i:126 - 2 * i].unsqueeze(1), self.fwide.buf)
        self.tt("dve", self.imp, self.imp, fw.bc([128, 2, 64]), ALU.add)
        self.ts("dve", self.imp[:, :, 0:1], self.imp[:, :, 0:1], 1e4, None, ALU.add)
        yield
        for kvh in range(2):
            iv = self.imp[:, kvh, :]
            m8 = self.m8.next()
            self.S.op("dve", lambda h_, m8=m8, iv=iv: h_.max(m8.ap, iv.ap), [iv], [m8])
            self.S.op("dve", lambda h_, m8=m8, iv=iv: h_.match_replace(self.imp2.ap, m8.ap, iv.ap, -1e30),
                      [iv, m8], [self.imp2])
            m8b = self.m8.next()
            self.S.op("dve", lambda h_, m8b=m8b: h_.max(m8b.ap, self.imp2.ap), [self.imp2], [m8b])
            self.ts("dve", self.sel[:, kvh, :], iv, m8b[:, 7:8], None, ALU.is_ge)
            yield
        coef = self.sm8.next()[:, :, 0]
        gv = self.gsig[:, t, :].re("p (h c) -> p h c", c=3)
        self.tt("dve", coef, rsc, gv[:, :, 0], ALU.mult)
        coef3 = V(coef.ap.unsqueeze(2), coef.buf)
        self.tt("dve", self.oacc, self.rawc[:, :, 0:64], coef3.bc([128, 8, 64]), ALU.mult)
        yield

    def nsa_heads(self, g, t, heads, ob):
        i = 4 * g + t
        Kt = (i + 1) * 128
        qc = slice(128 * t, 128 * t + 128)
        nchunk = i // 4 + 1
        items = []
        for h in heads:
            kvh, j, pbs = h // 4, h % 4, 64 * (h // 4)
            qT = self.qaT[pbs:pbs + 64, j, qc]
            for c in range(nchunk):
                w = min(512, Kt - 512 * c)
                nb = w // 128
                masks = []
                selv = V(self.sel.ap[:, kvh, 8 * c:8 * c + w // 64].unsqueeze(2), self.sel.buf)
                masks.append((0, w, selv.bc([128, w // 64, 64]), True))
                for kb in (i - 1, i):
                    if kb >= 0 and kb // 4 == c:
                        ro = (kb - (i - 1)) * 128
                        masks.append(((kb % 4) * 128, 128, self.R[:, h, ro:ro + 128], False))
                items.append(dict(qT=qT, krhs=[(self.kslT[pbs:pbs + 64, 512 * c:512 * c + w], w)], h=h, masks=masks,
                                  vrhs=[self.vsl[:, 4 * c + n_, kvh, :] for n_ in range(nb)],
                                  acc=ob[:, 0:65], first=(c == 0), last=(c == nchunk - 1), fin=None, ob=ob))
            kbs = list(range(max(0, i - 4), i + 1))
            parts = [p for p in (kbs[0:4], kbs[4:]) if p]
            for pi, part in enumerate(parts):
                masks = []
                for li, kb in enumerate(part):
                    if kb == i - 4:
                        masks.append((128 * li, 128, self.ustrict_bf, False))
                    if kb >= i - 1:
                        ro = (kb - (i - 1)) * 128
                        masks.append((128 * li, 128, self.R[:, h, ro:ro + 128], False))
                items.append(dict(qT=qT, krhs=[(self.kwT[pbs:pbs + 64, 128 * (kb % 8):128 * (kb % 8) + 128], 128)
                                               for kb in part], h=h, masks=masks,
                                  vrhs=[self.vw[:, kb % 8, kvh, :] for kb in part],
                                  acc=ob[:, 65:130], first=(pi == 0), last=(pi == len(parts) - 1),
                                  fin=(h if pi == len(parts) - 1 else None), ob=ob))
        prev = None
        for it in items:
            self.soft_stage1(it)
            yield
            if prev is not None:
                yield from self.soft_stage2(prev)
            prev = it
        yield from self.soft_stage2(prev)

    def nsa_epilogue(self, g, t):
        qc = slice(128 * t, 128 * t + 128)
        gv = self.gsig[:, t, :].re("p (h c) -> p h c", c=3)
        rs2 = self.sm8.next()
        r4 = self.raws.re("p h (b k) -> p h b k", k=65)
        self.ts("dve", rs2, r4[:, :, :, 64], 1e-30, None, ALU.max)
        self.S.op("dve", lambda h_: h_.reciprocal(rs2.ap, rs2.ap), [rs2], [rs2])
        for br in (0, 1):
            cf = self.sm8.next()[:, :, 0]
            self.tt("dve", cf, rs2[:, :, br], gv[:, :, 1 + br], ALU.mult)
            cf3 = V(cf.ap.unsqueeze(2), cf.buf)
            tmp = self.f32n.next().re("p (a b) -> p a b", a=8)
            self.tt("dve", tmp, r4[:, :, br, 0:64], cf3.bc([128, 8, 64]), ALU.mult)
            self.tt("dve", self.oacc, self.oacc, tmp, ALU.add)
        ob_ = self.obf.next()
        self.copy("act", ob_, self.oacc.re("p a b -> p (a b)"))
        tp = self.tp_banks.next().cast(BF16)
        for k in range(4):
            self.tpose(tp[:, 128 * k:128 * k + 128], ob_[:, 128 * k:128 * k + 128])
        self.copy("act", self.onT[:, :, qc], tp[:, 0:512].re("p (k n) -> p k n", k=4))

    def sb_heads(self, g, t, heads, cb, cars, er, spr, bfr, sc, acc):
        i = 4 * g + t
        Kt = (i + 1) * 128
        qc = slice(128 * t, 128 * t + 128)
        nchunk = i // 4 + 1
        items = []
        for h in heads:
            pbs, j = 64 * (h % 2), h // 2
            for c in range(nchunk - 1, -1, -1):
                w = min(512, Kt - 512 * c)
                items.append(dict(h=h, c=c, w=w, nb=w // 128, diag=(c == nchunk - 1), pbs=pbs, j=j,
                                  qT=self.qbT[pbs:pbs + 64, j, qc]))

        def st1(it):
            w, c, pbs, j = it["w"], it["c"], it["pbs"], it["j"]
            self.mm(sc[:, 0:w], it["qT"], self.kbT[pbs:pbs + 64, j, 512 * c:512 * c + w])
            e = er.next()
            self.act(e[:, 0:w], sc[:, 0:w], AF.Exp, scale=0.125)
            sp = spr.next()
            self.act(sp[:, 0:w], e[:, 0:w], AF.Ln, bias=1.0)
            if it["diag"]:
                self.tt("pool", sp[:, w - 128:w], sp[:, w - 128:w], self.mstrict_f, ALU.mult)
            it["e"], it["sp"] = e, sp

        def st2(it, carry):
            w, nb, h, c = it["w"], it["nb"], it["h"], it["c"]
            e, sp = it["e"], it["sp"]
            ones = self.onecol.bc([128, w])
            init = 0.0 if carry is None else carry
            ins = [ones, sp] + ([carry] if carry is not None else [])

            def scan(h_, sp=sp, w=w, init=init, ones=ones):
                iv = init.ap if isinstance(init, V) else init
                return h_.tensor_tensor_scan(cb.ap[:, 0:w][:, ::-1], ones.ap, sp.ap[:, 0:w][:, ::-1], iv,
                                             ALU.mult, ALU.add)
            self.S.op("dve", scan, ins, [cb])
            ncar = cars.next()
            self.copy("dve", ncar, cb[:, 0:1])
            it["carry_out"] = ncar
            yield
            self.act(sp[:, 0:w], cb[:, 0:w], AF.Exp, scale=-1.0)
            yield
            Ab = bfr.next()
            self.tt("dve", Ab[:, 0:w], e[:, 0:w], sp[:, 0:w], ALU.mult)
            if it["diag"]:
                self.tt("pool", Ab[:, w - 128:w], Ab[:, w - 128:w], self.mstrict_bf, ALU.mult)
            yield
            tp = self.tp_banks.next().cast(BF16)
            for n_ in range(nb):
                self.tpose(tp[:, 128 * n_:128 * n_ + 128], Ab[:, 128 * n_:128 * n_ + 128])
            AT = bfr.next()
            self.copy("act", AT[:, 0:w], tp[:, 0:w])
            yield
            for n_ in range(nb):
                self.mm(acc[:, 64 * (h % 4):64 * (h % 4) + 64], AT[:, 128 * n_:128 * n_ + 128],
                        self.vb[:, 4 * c + n_, 64 * h:64 * h + 64], it["diag"] and n_ == 0, c == 0 and n_ == nb - 1)
            yield

        prev = None
        for it in items:
            st1(it)
            yield
            if prev is not None:
                carry = None if prev["diag"] else prev["carry_in"]
                yield from st2(prev, carry)
                it["carry_in"] = prev["carry_out"]
            prev = it
        carry = None if prev["diag"] else prev["carry_in"]
        yield from st2(prev, carry)

    def sb_epilogue(self, g, t):
        qc = slice(128 * t, 128 * t + 128)
        ob_ = self.obf.next()
        self.copy("act", ob_[:, 0:256], self.bank(5)[:, 0:256])
        self.copy("act", ob_[:, 256:512], self.bank(6)[:, 0:256])
        tp = self.tp_banks.next().cast(BF16)
        for k in range(4):
            self.tpose(tp[:, 128 * k:128 * k + 128], ob_[:, 128 * k:128 * k + 128])
        self.copy("dve", self.osT[:, :, qc], tp[:, 0:512].re("p (k n) -> p k n", k=4))

    @staticmethod
    def run_lanes(lanes):
        lanes = list(lanes)
        while lanes:
            nxt = []
            for ln in lanes:
                try:
                    next(ln)
                    nxt.append(ln)
                except StopIteration:
                    pass
            lanes = nxt

    def attn_tile(self, g, t, do_nsa=True, do_sb=True):
        def nsa_lane():
            yield from self.nsa_prologue(g, t)
            yield from self.nsa_heads(g, t, range(0, 8), self.bank(3))
            self.nsa_epilogue(g, t)
        lanes = []
        if do_nsa:
            lanes.append(nsa_lane())
        if do_sb:
            lanes.append(self.sb_heads(g, t, range(0, 4), self.cbufA, self.carA, self.eA, self.spA, self.bfA, self.bank(0), self.bank(5)))
            lanes.append(self.sb_heads(g, t, range(4, 8), self.cbufB, self.carB, self.eB, self.spB, self.bfB, self.bank(1), self.bank(6)))
        self.run_lanes(lanes)
        if do_sb:
            self.sb_epilogue(g, t)


_CACHE = {}


def run(inputs, S, NB, ncores, batch0=0):
    key = (S, NB)
    if key not in _CACHE:
        _CACHE[key] = Prog(S, NB).build()
    nc = _CACHE[key]
    maps = host_prep(inputs, S, NB, ncores, batch0)
    res = run_bass_kernel_spmd(nc, maps, core_ids=list(range(ncores)))
    return np.concatenate([np.asarray(r["y"]) for r in res.results], axis=0)


def kernel(**inputs):
    out = run(inputs, SEQ, BATCH // NCORES, NCORES)
    return out.astype(np.float32)
```

```python
import numpy as np
from contextlib import ExitStack
import concourse.bass as bass
import concourse.mybir as mybir
from concourse.bass_utils import run_bass_kernel_spmd

F32 = mybir.dt.float32
BF16 = mybir.dt.bfloat16
AF = mybir.ActivationFunctionType
ALU = mybir.AluOpType
AX = mybir.AxisListType

D = 1024
DH = 64
NH = 8
EPS = 1e-6
NCORES = 8
SEQ = 4096
BATCH = 16
IN_W = 4888


class Buf:
    __slots__ = ("name", "w", "r")

    def __init__(self, name=""):
        self.name = name
        self.w = {}
        self.r = {}


class V:
    __slots__ = ("ap", "buf")

    def __init__(self, ap, buf):
        self.ap = ap
        self.buf = buf

    def __getitem__(self, k):
        return V(self.ap[k], self.buf)

    def re(self, pat, **kw):
        return V(self.ap.rearrange(pat, **kw), self.buf)

    def bc(self, shape):
        return V(self.ap.to_broadcast(list(shape)), self.buf)

    def cast(self, dt):
        return V(self.ap.bitcast(dt), self.buf)


class Ev:
    __slots__ = ("sem", "seq", "key", "needed", "val")

    def __init__(self, sem, seq, key, val=None):
        self.sem = sem
        self.seq = seq
        self.key = key
        self.needed = False
        self.val = val


class Sched:
    ENGS = ("pe", "act", "dve", "pool", "sp")

    def __init__(self, sems, dma_sems):
        self.sem = dict(zip(self.ENGS, sems))
        self.epoch = 0
        self.cnt = {e: 0 for e in self.ENGS}
        self.prog = {e: [] for e in self.ENGS}
        self.seen = {e: {} for e in self.ENGS}
        self.dma_sems = dma_sems
        self.dma_cnt = {q: 0 for q in dma_sems}
        self.dma_n = 0
        self.dma_last = {}
        self.last_ev = {}
        self.all_ev = {}
        self.ninstr = 0

    def _need(self, eng, ev, waits, raw):
        key = ev.key
        if key[0] == "e" and key[1] == eng and not raw:
            return
        if self.seen[eng].get(key, 0) >= ev.seq:
            return
        cur = waits.get(key)
        if cur is None or cur.seq < ev.seq:
            waits[key] = ev

    def _commit(self, eng, waits):
        wl = list(waits.values())
        for ev in wl:
            ev.needed = True
            if self.seen[eng].get(ev.key, 0) < ev.seq:
                self.seen[eng][ev.key] = ev.seq
        return wl

    def op(self, eng, fn, ins=(), outs=(), dma=False):
        waits = {}
        for v in ins:
            for ev in v.buf.w.values():
                self._need(eng, ev, waits, True)
        for v in outs:
            b = v.buf
            for ev in b.w.values():
                self._need(eng, ev, waits, False)
            for ev in b.r.values():
                self._need(eng, ev, waits, False)
        if dma:
            pool_ = self.dma_sems[eng]
            ns = len(pool_)
            slot = self.dma_cnt[eng] % ns
            rnd = self.dma_cnt[eng] // ns
            self.dma_cnt[eng] += 1
            self.dma_n += 1
            dsem = pool_[slot]
            key = ("dma", eng, slot)
            slot = (eng, slot)
            if rnd > 0:
                self._need(eng, Ev(dsem, rnd, key, 16 * rnd), waits, True)
            ev = Ev(dsem, rnd + 1, key, 16 * (rnd + 1))
            ev.needed = True
            self.dma_last[slot] = ev
        else:
            self.cnt[eng] += 1
            key = ("e", eng, self.epoch)
            ev = Ev(self.sem[eng], self.cnt[eng], key)
            self.last_ev[eng] = ev
            self.all_ev.setdefault(key, []).append(ev)
        wl = self._commit(eng, waits)

        def emit(h, fn=fn, wl=wl, ev=ev, dma=dma):
            for w in wl:
                h.wait_ge(w.sem, w.val)
            ins_ = fn(h)
            if dma:
                ins_.then_inc(ev.sem, 16)
            elif ev.needed:
                ins_.then_inc(ev.sem, 1)

        self.prog[eng].append(emit)
        self.ninstr += 1
        for v in ins:
            v.buf.r[ev.key] = ev
        for v in outs:
            v.buf.w = {ev.key: ev}
            v.buf.r = {}
        return ev

    def barrier(self):
        evs = [self.last_ev[e] for e in self.ENGS if self.cnt[e] > 0 and e in self.last_ev]
        evs += list(self.dma_last.values())
        for eng in self.ENGS:
            waits = {}
            for ev in evs:
                if ev.key[0] == "e" and ev.key[1] == eng:
                    continue
                self._need(eng, ev, waits, True)
            wl = self._commit(eng, waits)
            if wl:
                self.prog[eng].append(lambda h, wl=wl: [h.wait_ge(w.sem, w.val) for w in wl])

    def new_epoch(self, sems):
        for eng in self.ENGS:
            for e2 in self.ENGS:
                self.seen[eng][("e", e2, self.epoch)] = 1 << 40
        self.epoch += 1
        self.sem = dict(zip(self.ENGS, sems))
        self.cnt = {e: 0 for e in self.ENGS}
        self.last_ev = {}

    def final_wait(self, eng="sp"):
        wl = list(self.dma_last.values())
        wl += [self.last_ev[e] for e in self.ENGS if e != eng and e in self.last_ev]
        for w in wl:
            w.needed = True
        self.prog[eng].append(lambda h, wl=wl: [h.wait_ge(w.sem, w.val) for w in wl])

    def finalize(self):
        ninc = 0
        for key, evs in self.all_ev.items():
            c = 0
            for ev in evs:
                if ev.needed:
                    c += 1
                    ninc += 1
                ev.val = c
        self.ninc = ninc

    def emit_all(self, block):
        self.finalize()
        prog = self.prog

        @block.tensor
        def _(h):
            for f in prog["pe"]:
                f(h)

        @block.scalar
        def _(h):
            for f in prog["act"]:
                f(h)

        @block.vector
        def _(h):
            for f in prog["dve"]:
                f(h)

        @block.gpsimd
        def _(h):
            for f in prog["pool"]:
                f(h)

        @block.sync
        def _(h):
            for f in prog["sp"]:
                f(h)


class Rot:
    def __init__(self, items):
        self.items = items
        self.i = 0

    def next(self):
        it = self.items[self.i % len(self.items)]
        self.i += 1
        return it


def _bucket(dist):
    n = np.maximum(dist, 0)
    nf = np.maximum(n, 1).astype(np.float64)
    raw = np.log(nf / 16.0) / np.log(8.0) * 16.0
    large = 16 + np.floor(raw + 1e-9).astype(np.int64)
    large = np.minimum(large, 31)
    return np.where(n < 16, n, large)


def _consts():
    a = np.arange(128)[:, None]
    ind = np.zeros((32, 128, 272), np.float32)
    m = np.arange(256)[None, :]
    dist = (1 - m // 128) * 128 + a - (m % 128)
    bk = _bucket(dist)
    for b in range(32):
        ind[b, :, :256] = ((bk == b) & (dist >= 0))
    w = np.arange(16)[None, :]
    distc = a - 16 * (w - 9) - 31
    bkc = _bucket(distc)
    for b in range(32):
        ind[b, :, 256:] = ((bkc == b) & (distc >= 0))
    fw = np.zeros((128, 126), np.float32)
    rel = np.arange(126)[None, :] - 62
    lo = (a < 64)
    fw[:] = np.where(rel >= 2, -1e9, 0.0)
    fw += np.where(rel == 1, np.where(lo, -1e9, 1e4), 0.0)
    fw += np.where(rel == 0, 1e4, 0.0)
    fw += np.where(rel == -1, np.where(lo, 1e4, 0.0), 0.0)
    cc = np.arange(256)[:, None]
    nn = np.arange(64)[None, :]
    ov = np.minimum(16 * cc + 32, 64 * nn + 64) - np.maximum(16 * cc, 64 * nn)
    ov = np.clip(ov, 0, 32).astype(np.float32) / 32.0
    ov[255] = 0.0
    ovx = np.zeros((128, 2, 65), np.float32)
    ovx[:, :, 0] = 1.0
    ovx[:, 0, 1:] = ov[:128]
    ovx[:, 1, 1:] = ov[128:]
    b_ = np.arange(128)[None, :]
    misc = np.zeros((128, 5, 128), np.float32)
    misc[:, 0] = np.eye(128)
    misc[:, 1] = ((a // 64) == (b_ // 64))
    misc[:, 2] = (b_ < a)
    misc[:, 3] = (b_ > a)
    misc[:, 4] = 1.0
    return ind, fw, ovx, misc


def _win_perm():
    q = []
    for j in range(4):
        q += list(range(j * 64, j * 64 + 64)) + list(range((j + 4) * 64, (j + 4) * 64 + 64))
    o_kc, o_vc, o_ksl, o_vsl, o_kwn, o_vwn, o_ga = 512, 640, 768, 896, 1024, 1152, 1280
    o_qb, o_kb, o_vb, o_ma, o_mb = 1304, 1816, 2328, 2840, 3864
    r = lambda s, n: list(range(s, s + n))
    perm = q + r(o_kc, 128) + r(o_vc, 128) + r(o_ksl, 128) + r(o_kwn, 128)
    perm += r(o_qb, 512) + r(o_kb, 512) + r(o_vb, 512)
    perm += r(o_vsl, 128) + r(o_vwn, 128) + r(o_ga, 24)
    perm += r(o_ma, 1024) + r(o_mb, 1024)
    assert len(perm) == IN_W and len(set(perm)) == IN_W
    return np.array(perm)


WC = [(0, 512), (512, 1024), (1024, 1536), (1536, 2048), (2048, 2560), (2560, 2840),
      (2840, 3352), (3352, 3864), (3864, 4376), (4376, 4888)]


def host_prep(inp, S, NB, ncores, batch0=0):
    f = lambda a: np.ascontiguousarray(a, dtype=np.float32)
    ind, fw, ovx, misc = _consts()
    vecT = lambda v: f(np.asarray(v).reshape(-1, 128).T)
    dup = lambda v: f(np.concatenate([v, v], axis=0))
    w1k = np.asarray(inp["cmp_k_w1"][0]).reshape(32, 64, 64).transpose(1, 0, 2).reshape(64, 2048)
    w1v = np.asarray(inp["cmp_v_w1"][0]).reshape(32, 64, 64).transpose(1, 0, 2).reshape(64, 2048)
    w2k = np.asarray(inp["cmp_k_w2"][0])
    w2kpad = np.zeros((64, 2, 128), np.float32)
    w2kpad[:, 0, 0:64] = w2k
    w2kpad[:, 1, 64:128] = w2k
    kng = np.asarray(inp["k_norm_g"][0])
    shared = {
        "adaw": f(inp["ada_w"][0]),
        "adabT": vecT(inp["ada_b"][0]),
        "adab": f(np.asarray(inp["ada_b"][0]).reshape(1, 6144)),
        "n1g": vecT(inp["norm1_g"][0]),
        "n2g": vecT(inp["norm2_g"][0]),
        "win": f(np.asarray(inp["w_in"][0])[:, _win_perm()]),
        "cw1k": dup(w1k), "cw1v": dup(w1v),
        "cposT": dup(np.asarray(inp["cmp_pos"][0]).T),
        "cw2k": f(w2kpad.reshape(64, 256)),
        "cw2v": f(inp["cmp_v_w2"][0]),
        "qkg": f(np.stack([np.tile(np.asarray(inp["q_norm_g"][0]), 2), np.tile(kng[0], 2),
                           np.tile(kng[1], 2), np.tile(kng[2], 2)], axis=1)),
        "wupn": f(inp["w_up_nsa"][0]), "wups": f(inp["w_up_sb"][0]),
        "wout": f(inp["w_out"][0]), "w1": f(inp["mlp_w1"][0]), "w2": f(inp["mlp_w2"][0]),
        "relb": f(np.asarray(inp["rel_bias"]).reshape(1, 256)),
        "c_ind": ind, "c_fw": fw, "c_ovx": f(ovx.reshape(128, 130)), "c_misc": f(misc.reshape(128, 640)),
    }
    maps = []
    x = np.asarray(inp["x"])
    c = np.asarray(inp["c"])
    for core in range(ncores):
        b0 = batch0 + core * NB
        m = dict(shared)
        m["x"] = f(x[b0:b0 + NB, :S])
        m["cT"] = f(np.stack([c[b0 + i].reshape(8, 128).T for i in range(NB)], axis=0))
        maps.append(m)
    return maps


class Prog:
    def __init__(self, S, NB):
        self.S_len = S
        self.NB = NB
        self.dbg = False
        self.dbg_names = []
        self.NG = S // 512
        self.NT = S // 128

    def mm(self, out, lhsT, rhs, start=True, stop=True):
        self.S.op("pe", lambda h: h.matmul(out.ap, lhsT.ap, rhs.ap, start=start, stop=stop),
                  [lhsT, rhs], [out])

    def tpose(self, out, in_):
        idn = self.ident
        self.S.op("pe", lambda h: h.transpose(out.ap, in_.ap, idn.ap), [in_, idn], [out])

    def act(self, out, in_, func, scale=1.0, bias=0.0, accum=None):
        ins = [in_]
        outs = [out]
        kw = dict(out=out.ap, in_=in_.ap, func=func)
        if isinstance(scale, V):
            ins.append(scale)
            kw["scale"] = scale.ap
        else:
            kw["scale"] = float(scale)
        if isinstance(bias, V):
            ins.append(bias)
            kw["bias"] = bias.ap
        elif bias != 0.0:
            kw["bias"] = float(bias)
        if accum is not None:
            outs.append(accum)
            kw["accum_out"] = accum.ap
        self.S.op("act", lambda h: h.activation(**kw), ins, outs)

    def tt(self, eng, out, a, b, op):
        self.S.op(eng, lambda h: h.tensor_tensor(out.ap, a.ap, b.ap, op), [a, b], [out])

    def ts(self, eng, out, a, s1, s2, op0, op1=None):
        ins = [a]
        if isinstance(s1, V):
            ins.append(s1)
        if isinstance(s2, V):
            ins.append(s2)
        g = lambda s: s.ap if isinstance(s, V) else s
        if op1 is None:
            self.S.op(eng, lambda h: h.tensor_scalar(out.ap, a.ap, g(s1), None, op0), ins, [out])
        else:
            self.S.op(eng, lambda h: h.tensor_scalar(out.ap, a.ap, g(s1), g(s2), op0, op1), ins, [out])

    def stt(self, out, a, s, b, op0, op1):
        ins = [a, b]
        if isinstance(s, V):
            ins.append(s)
        g = s.ap if isinstance(s, V) else s
        self.S.op("dve", lambda h: h.scalar_tensor_tensor(out.ap, a.ap, g, b.ap, op0, op1), ins, [out])

    def copy(self, eng, out, in_):
        if eng == "act":
            self.act(out, in_, AF.Copy)
        else:
            self.S.op(eng, lambda h: h.tensor_copy(out.ap, in_.ap), [in_], [out])

    def memset(self, eng, out, val):
        self.S.op(eng, lambda h: h.memset(out.ap, val), [], [out])

    def dbg_dump(self, name, v):
        if not getattr(self, "dbg", False):
            return
        shp = list(v.ap.shape)
        n = int(np.prod(shp[1:]))
        t = self.nc.dram_tensor("dbg_" + name, [shp[0], n], F32, kind="ExternalOutput").ap()
        if len(shp) == 3:
            t = t.rearrange("p (a b) -> p a b", a=shp[1])
        elif len(shp) == 4:
            t = t.rearrange("p (a b c) -> p a b c", a=shp[1], b=shp[2])
        self.dbg_names.append("dbg_" + name)
        self.dma("pool", V(t, Buf("dbg")), v)

    def dma(self, q, out, in_):
        self.S.op(q, lambda h: h.dma_start(out=out.ap, in_=in_.ap), [in_], [out], dma=True)

    def alloc(self, shape, dt, name=""):
        n = int(np.prod(shape[1:]))
        nbytes = n * (2 if dt == BF16 else 4)
        nbytes = (nbytes + 31) // 32 * 32
        off = self.sb_off
        self.sb_off += nbytes
        assert self.sb_off <= self.sb_bytes, (name, self.sb_off, self.sb_bytes)
        ap = self.sb[:, off // 4:(off + nbytes) // 4]
        if dt == BF16:
            ap = ap.bitcast(BF16)
        ap = ap[:, 0:n]
        v = V(ap, Buf(name))
        if len(shape) == 3:
            v = v.re("p (a b) -> p a b", a=shape[1])
        elif len(shape) == 4:
            v = v.re("p (a b c) -> p a b c", a=shape[1], b=shape[2])
        if shape[0] < 128:
            v = v[0:shape[0]]
        return v

    def bank(self, k, dt=F32):
        ap = self.ps[:, 512 * k:512 * (k + 1)]
        if dt == BF16:
            ap = ap.bitcast(BF16)
        return V(ap, self.bank_buf[k])

    def plan_chunks(self):
        d = self.d
        r8 = lambda ap: ap.rearrange("(k p) n -> p k n", p=128)
        ch = []
        ch.append(("cw1k", d["cw1k"], (128, 2048)))
        ch.append(("cw1v", d["cw1v"], (128, 2048)))
        for b in range(self.NB):
            for j in range(12):
                ch.append(("ada%d" % j, r8(d["adaw"])[:, :, 512 * j:512 * j + 512], (128, 8, 512)))
            for g in range(self.NG):
                for j in range(6):
                    c0, c1 = WC[j]
                    ch.append(("win%d" % j, r8(d["win"])[:, :, c0:c1], (128, 8, c1 - c0)))
                ch.append(("cw1k", d["cw1k"], (128, 2048)))
                ch.append(("cw1v", d["cw1v"], (128, 2048)))
                for j in (6, 8):
                    c0, c1 = WC[j]
                    ch.append(("win%d" % j, r8(d["win"])[:, :, c0:c1], (128, 8, c1 - c0)))
                ch.append(("wupn", r8(d["wupn"]), (128, 4, 1024)))
                ch.append(("wups", r8(d["wups"]), (128, 4, 1024)))
                for j in (7, 9):
                    c0, c1 = WC[j]
                    ch.append(("win%d" % j, r8(d["win"])[:, :, c0:c1], (128, 8, c1 - c0)))
                for j in range(2):
                    ch.append(("wout%d" % j, r8(d["wout"])[:, :, 512 * j:512 * j + 512], (128, 8, 512)))
                for j in range(8):
                    ch.append(("w1_%d" % j, r8(d["w1"])[:, :, 512 * j:512 * j + 512], (128, 8, 512)))
                    ch.append(("w2_%d" % j, r8(d["w2"])[:, 4 * j:4 * j + 4, :], (128, 4, 1024)))
        self.chunks = ch
        self.ch_pos = 0
        self.ch_issued = 0

    def _slot_view(self, k):
        tag, src, shape = self.chunks[k]
        slot = self.wslots[k % len(self.wslots)]
        if len(shape) == 2:
            return slot[:, 0:shape[1]]
        v = slot.re("p (a b) -> p a b", a=shape[1])
        if shape[1] == 8 and shape[2] < 512:
            v = v[:, :, 0:shape[2]]
        return v

    def wnext(self, tag, live=1):
        k = self.ch_pos
        assert self.chunks[k][0] == tag, (self.chunks[k][0], tag)
        self.ch_pos += 1
        ns = len(self.wslots)
        import os
        depth = int(os.environ.get("PREF", "99"))
        while self.ch_issued < min(len(self.chunks), k + min(ns - live, depth) + 1):
            kk = self.ch_issued
            if not (os.environ.get("NOWDMA") == "1" and kk > 40):
                self.dma("pool", self._slot_view(kk), V(self.chunks[kk][1], self.dram_buf))
            self.ch_issued += 1
        return self._slot_view(k)

    def build(self):
        S, NB, NG, NT = self.S_len, self.NB, self.NG, self.NT
        nc = bass.Bass("TRN2", target_bir_lowering=False)
        self.nc = nc
        din = lambda name, shape: nc.dram_tensor(name, list(shape), F32, kind="ExternalInput").ap()
        d = {}
        d["x"] = din("x", [NB, S, D])
        d["cT"] = din("cT", [NB, 128, 8])
        d["adaw"] = din("adaw", [D, 6 * D])
        d["adabT"] = din("adabT", [128, 48])
        d["adab"] = din("adab", [1, 6 * D])
        d["n1g"] = din("n1g", [128, 8])
        d["n2g"] = din("n2g", [128, 8])
        d["win"] = din("win", [D, IN_W])
        d["cw1k"] = din("cw1k", [128, 2048])
        d["cw1v"] = din("cw1v", [128, 2048])
        d["cposT"] = din("cposT", [128, 32])
        d["cw2k"] = din("cw2k", [64, 256])
        d["cw2v"] = din("cw2v", [64, 64])
        d["qkg"] = din("qkg", [128, 4])
        d["wupn"] = din("wupn", [512, D])
        d["wups"] = din("wups", [512, D])
        d["wout"] = din("wout", [D, D])
        d["w1"] = din("w1", [D, 4 * D])
        d["w2"] = din("w2", [4 * D, D])
        d["relb"] = din("relb", [1, 256])
        d["c_ind"] = din("c_ind", [32, 128, 272])
        d["c_fw"] = din("c_fw", [128, 126])
        d["c_ovx"] = din("c_ovx", [128, 130])
        d["c_misc"] = din("c_misc", [128, 640])
        self.d = d
        self.y = nc.dram_tensor("y", [NB, S, D], F32, kind="ExternalOutput").ap()
        self.dram_buf = Buf("dram_in")
        self.y_buf = Buf("y")

        with ExitStack() as st:
            self.sb_bytes = 212832
            self.sb = st.enter_context(nc.sbuf_tensor("sb", [128, self.sb_bytes // 4], F32))
            self.sb_off = 0
            self.ps = st.enter_context(nc.psum_tensor("ps", [128, 4096], F32))
            self.bank_buf = [Buf("bank%d" % k) for k in range(8)]
            sems = [st.enter_context(nc.semaphore("s_" + e)) for e in Sched.ENGS]
            dsems = {q: [st.enter_context(nc.semaphore("d%s_%d" % (q, i))) for i in range(16)] for q in ("sp", "pool")}
            self.esems = [[st.enter_context(nc.semaphore("s%d_%s" % (i, e))) for e in Sched.ENGS]
                          for i in range((NB * NG + 3) // 4)]
            self.S = Sched(sems, dsems)
            self.setup()
            for b in range(NB):
                self.seq_init(b)
                import os
                for g in range(min(NG, int(os.environ.get("MAXG", "99")))):
                    self.group(b, g)
            self.S.final_wait("sp")
            with nc.Block() as block:
                self.S.emit_all(block)
        return nc

    def setup(self):
        d = self.d
        A = self.alloc
        DR = lambda ap: V(ap, self.dram_buf)
        misc = A([128, 5, 128], BF16, "misc")
        self.dma("pool", misc, DR(d["c_misc"].rearrange("p (a b) -> p a b", a=5)))
        self.ident = misc[:, 0, :]
        self.blk1 = misc[:, 1, :]
        self.mstrict_bf = misc[:, 2, :]
        self.ustrict_bf = misc[:, 3, :]
        self.onesbf = misc[:, 4, :]
        self.mstrict_f = A([128, 128], F32, "mstrict_f")
        self.dma("sp", self.mstrict_f, DR(d["c_misc"][:, 256:384]))
        self.onecol = A([128, 1], F32, "onecol")
        self.memset("dve", self.onecol, 1.0)
        self.fwide = A([128, 126], F32, "fwide")
        self.dma("sp", self.fwide, DR(d["c_fw"]))
        self.qkg = A([128, 4], F32, "qkg")
        self.dma("sp", self.qkg, DR(d["qkg"]))
        self.n1g = A([128, 8], F32, "n1g")
        self.dma("sp", self.n1g, DR(d["n1g"]))
        self.n2g = A([128, 8], F32, "n2g")
        self.dma("sp", self.n2g, DR(d["n2g"]))
        self.adabT = A([128, 48], F32, "adabT")
        self.dma("sp", self.adabT, DR(d["adabT"]))
        self.cw2k = A([64, 2, 128], BF16, "cw2k")
        self.dma("pool", self.cw2k, DR(d["cw2k"].rearrange("p (a b) -> p a b", a=2)))
        self.cw2v = A([64, 64], BF16, "cw2v")
        self.dma("pool", self.cw2v, DR(d["cw2v"]))
        self.cposT = A([128, 32], BF16, "cposT")
        self.dma("pool", self.cposT, DR(d["cposT"]))
        self.pbias = A([64, 2], F32, "pbias")
        self.tbl = A([128, 32, 8], F32, "tbl")
        self.dma("sp", self.tbl.re("p a b -> p (a b)"), DR(d["relb"].rearrange("a b -> (a b)").partition_broadcast(128)))
        self.c31 = self.tbl[:, 31, :]
        self.R = A([128, 8, 272], BF16, "R")
        self.A1 = A([128, 8], F32, "A1")
        self.B1 = A([128, 8], F32, "B1")
        self.A2 = A([128, 8], F32, "A2")
        self.B2 = A([128, 8], F32, "B2")
        self.g1bc = A([128, 1024], F32, "g1bc")
        self.g2bc = A([128, 1024], F32, "g2bc")
        self.wslots = [A([128, 4096], BF16, "wslot%d" % i) for i in range(4)]
        S, NT = self.S_len, self.NT
        self.kbT = A([128, 4, S], BF16, "kbT")
        self.vb = A([128, NT, 512], BF16, "vb")
        self.kslT = A([128, S], BF16, "kslT")
        self.vsl = A([128, NT, 2, 65], BF16, "vsl")
        self.kwT = A([128, 1024], BF16, "kwT")
        self.vw = A([128, 8, 2, 65], BF16, "vw")
        self.kcT = A([128, 528], BF16, "kcT")
        self.vcT = A([128, 528], BF16, "vcT")
        self.kcmpT = A([128, 256], BF16, "kcmpT")
        self.hidTv = A([64, 2, 256], BF16, "hidTv")
        self.vcx = A([128, 2, 2, 129], BF16, "vcx")
        self.uT = A([128, 8, 512], BF16, "uT")
        self.gsig = A([128, 4, 24], F32, "gsig")
        self.ecb = Rot([A([128, 256], BF16, "ecb%d" % i) for i in range(2)])
        import os
        padb = int(os.environ.get("KPAD", "0"))
        if padb:
            A([128, padb // 4], F32, "pad")
        self.arena0 = self.sb_off
        self.memset("pool", self.vsl[:, :, :, 64:65], 1.0)
        self.memset("pool", self.vw[:, :, :, 64:65], 1.0)
        for ch in range(2):
            for kvh in range(2):
                self.dma("pool", self.vcx[:, ch, kvh, 64:129], DR(d["c_ovx"][:, 65 * ch:65 * ch + 65]))
        self.plan_chunks()
        etbl = A([128, 32, 8], F32, "etbl")
        R32 = A([128, 8, 272], F32, "R32")
        indb = [A([128, 272], F32, "indb%d" % i) for i in range(2)]
        self.tt("dve", etbl, self.tbl, self.tbl[:, 31:32, :].bc([128, 32, 8]), ALU.subtract)
        self.act(etbl, etbl, AF.Exp)
        for b in range(32):
            ib = indb[b % 2]
            self.dma("sp", ib, DR(d["c_ind"][b]))
            for h in range(8):
                if b == 0:
                    self.ts("dve", R32[:, h, :], ib, etbl[:, b, h:h + 1], None, ALU.mult)
                else:
                    self.stt(R32[:, h, :], ib, etbl[:, b, h:h + 1], R32[:, h, :], ALU.mult, ALU.add)
        self.copy("dve", self.R, R32)
        for i, tag in enumerate(("cw1k", "cw1v")):
            W1 = self.wnext(tag)
            pb = self.bank(i)
            for l in range(32):
                self.mm(pb[0:64, 0:1], W1[0:64, 64 * l:64 * l + 64], self.cposT[0:64, l:l + 1], l == 0, l == 31)
            self.copy("dve", self.pbias[:, i:i + 1], pb[0:64, 0:1])
        self.S.barrier()
        self.sb_off = self.arena0
        self.alloc_arena()

    def alloc_arena(self):
        A = self.alloc
        a0 = self.sb_off
        self.xbuf = A([128, 4, 1024], F32, "xbuf")
        self.xbuf_t = [V(self.xbuf.ap[:, t, :], Buf("xbuf%d" % t)) for t in range(4)]
        x_end = self.sb_off
        self.acc = A([128, 4, 1024], F32, "acc")
        self.acc_t = [V(self.acc.ap[:, t, :], Buf("acc%d" % t)) for t in range(4)]
        self.hT = Rot([A([128, 4, 512], BF16, "hT%d" % i) for i in range(2)])
        self.relu_t = Rot([A([128, 512], F32, "relu%d" % i) for i in range(2)])
        self.xn = Rot([A([128, 1024], BF16, "xn%d" % i) for i in range(2)])
        self.sq_junk = self.relu_t.items[0].cast(BF16)
        self.small = Rot([A([128, 4], F32, "small%d" % i) for i in range(4)])
        a1 = self.sb_off
        self.sb_off = a0
        self.qaT = A([128, 4, 512], BF16, "qaT")
        self.qbT = A([128, 4, 512], BF16, "qbT")
        self.cbufA = A([128, 512], F32, "cbufA")
        self.cbufB = A([128, 512], F32, "cbufB")
        self.carA = Rot([A([128, 1], F32, "carA%d" % i) for i in range(2)])
        self.carB = Rot([A([128, 1], F32, "carB%d" % i) for i in range(2)])
        bfs = [A([128, 512], BF16, "bf512_%d" % i) for i in range(8)]
        self.bf512 = Rot(bfs[0:4])
        self.bfA = Rot(bfs[4:6])
        self.bfB = Rot(bfs[6:8])
        mix_off = self.sb_off
        self.rawc = A([128, 8, 129], F32, "rawc")
        self.raws = A([128, 8, 130], F32, "raws")
        raw_end = self.sb_off
        assert mix_off >= x_end, (mix_off, x_end)
        self.sb_off = mix_off
        self.mixT = A([128, 8, 512], BF16, "mixT")
        assert self.sb_off <= raw_end, (self.sb_off, raw_end)
        self.sb_off = raw_end
        self.oacc = A([128, 8, 64], F32, "oacc")
        self.imp = A([128, 2, 64], F32, "imp")
        self.imp2 = A([128, 64], F32, "imp2")
        self.sel = A([128, 2, 64], BF16, "sel")
        self.m8 = Rot([A([128, 8], F32, "m8_%d" % i) for i in range(2)])
        self.sm8 = Rot([A([128, 8, 2], F32, "sm8_%d" % i) for i in range(4)])
        self.obf = Rot([A([128, 512], BF16, "obf%d" % i) for i in range(1)])
        assert self.sb_off >= x_end, (self.sb_off, x_end)
        self.onT = A([128, 4, 512], BF16, "onT")
        self.osT = A([128, 4, 512], BF16, "osT")
        f32s = [A([128, 512], F32, "f32t%d" % i) for i in range(9)]
        self.f32t = Rot(f32s[0:5])
        self.f32n = Rot(f32s[0:1])
        self.eA = Rot(f32s[1:3])
        self.spA = Rot(f32s[3:5])
        self.eB = Rot(f32s[5:7])
        self.spB = Rot(f32s[7:9])
        self.sgt = Rot([A([128, 512], BF16, "sgt%d" % i) for i in range(2)])
        a2 = self.sb_off
        self.sb_off = max(a1, a2)
        self.arena_bytes = self.sb_off - a0

    def seq_init(self, b):
        d = self.d
        DR = lambda ap: V(ap, self.dram_buf)
        S = self.S
        S.barrier()
        cT = self.f32t.next()[:, 0:8]
        self.dma("sp", cT, DR(d["cT"][b]))
        sc = self.sgt.next()[:, 0:8]
        self.act(sc, cT, AF.Silu)
        scb = self.mixT[:, 0:2, :].re("p a b -> p (a b)").re("p (k m) -> p k m", k=8)
        for k in range(8):
            self.ts("dve", scb[:, k, :], self.onesbf, sc[:, k:k + 1], None, ALU.mult)
        fm_dst = {0: self.B1, 1: self.B1, 2: self.A1, 3: self.A1, 6: self.B2, 7: self.B2, 8: self.A2, 9: self.A2}
        pool = Rot([self.bank(k) for k in range(4)])
        for j in range(12):
            W = self.wnext("ada%d" % j)
            if j in fm_dst:
                dst = fm_dst[j]
                pb = pool.next()
                for m in range(4):
                    for k in range(8):
                        self.mm(pb[:, m:m + 1], W[:, k, 128 * m:128 * m + 128], sc[:, k:k + 1], k == 0, k == 7)
                c0 = (j % 2) * 4
                self.tt("dve", dst[:, c0:c0 + 4], pb[:, 0:4], self.adabT[:, 4 * j:4 * j + 4], ALU.add)
            else:
                dst = self.g1bc if j < 6 else self.g2bc
                pb = pool.next()
                for k in range(8):
                    self.mm(pb, scb[:, k, :], W[:, k, :], k == 0, False)
                arow = self.bf512.next()[0:1, :]
                self.dma("pool", arow, DR(d["adab"][0:1, 512 * j:512 * j + 512]))
                self.mm(pb, self.onesbf[0:1, :], arow, False, True)
                c0 = (j % 2) * 512
                self.copy("act", dst[:, c0:c0 + 512], pb)
        for Av, gv in ((self.A1, self.n1g), (self.A2, self.n2g)):
            self.stt(Av, Av, 1.0, gv, ALU.add, ALU.mult)
        self.memset("pool", self.kcT[:, 0:16], 0.0)
        self.memset("pool", self.vcT[:, 0:16], 0.0)
        self.memset("pool", self.kcmpT, 0.0)
        self.memset("pool", self.hidTv, 0.0)
        for e in self.ecb.items:
            self.memset("pool", e, 0.0)
        S.barrier()

    def norm_tile(self, src, tt_, Am, Bm):
        xn = self.xn.next()
        sm = self.small.next()
        self.act(self.sq_junk, src, AF.Square, accum=sm[:, 0:1])
        self.act(sm[:, 1:2], sm[:, 0:1], AF.Sqrt, scale=1.0 / D, bias=EPS)
        self.S.op("dve", lambda h: h.reciprocal(sm.ap[:, 2:3], sm.ap[:, 1:2]), [sm], [sm])
        self.act(xn, src, AF.Copy, scale=sm[:, 2:3])
        pT = self.tp_banks.next().cast(BF16).re("p (c n) -> p c n", c=8)
        for c in range(8):
            self.tpose(pT[:, c, :], xn[:, 128 * c:128 * c + 128])
        for c in range(8):
            self.ts("dve", self.uT[:, c, 128 * tt_:128 * tt_ + 128], pT[:, c, :], Am[:, c:c + 1], Bm[:, c:c + 1],
                    ALU.mult, ALU.add)

    def qk_norm(self, zps, gcol, out, n):
        sq = self.bf512.next()
        self.act(sq[:, 0:n], zps[:, 0:n], AF.Square)
        sp = self.pB.next()
        self.mm(sp[:, 0:n], self.blk1, sq[:, 0:n])
        rt = self.f32t.next()
        self.act(rt[:, 0:n], sp[:, 0:n], AF.Sqrt, scale=1.0 / DH, bias=EPS)
        self.S.op("dve", lambda h: h.reciprocal(rt.ap[:, 0:n], rt.ap[:, 0:n]), [rt], [rt])
        self.stt(out, zps[:, 0:n], gcol, rt[:, 0:n], ALU.mult, ALU.mult)

    def group(self, b, g):
        d = self.d
        DR = lambda ap: V(ap, self.dram_buf)
        S = self.S
        S.barrier()
        if (b * self.NG + g) % 4 == 0:
            S.new_epoch(self.esems[(b * self.NG + g) // 4])
        self.pA = Rot([self.bank(k) for k in range(4)])
        self.pB = Rot([self.bank(k) for k in (4, 5)])
        self.tp_banks = Rot([self.bank(k) for k in (6, 7)])
        for t in range(4):
            self.dma("sp", self.xbuf_t[t], DR(d["x"][b, (4 * g + t) * 128:(4 * g + t + 1) * 128, :]))
        for t in range(4):
            self.norm_tile(self.xbuf_t[t], t, self.A1, self.B1)
        import os
        stopat = int(os.environ.get("STOPAT", "99")) if g == int(os.environ.get("STOPG", "6")) else 99
        if stopat <= 1:
            return
        S.barrier()
        S.trace_ops = (g == 6 and os.environ.get("TRACEOPS") == "1")
        uT = self.uT
        gs = slice(512 * g, 512 * g + 512)

        def fm(W, m):
            pb = self.pA.next()
            for k in range(8):
                self.mm(pb, W[:, k, 128 * m:128 * m + 128], uT[:, k, :], k == 0, k == 7)
            return pb

        W = self.wnext("win0")
        for m in range(4):
            self.qk_norm(fm(W, m), self.qkg[:, 0:1], self.qaT[:, m, :], 512)
        if g == int(os.environ.get("STOPG", "6")) and os.environ.get("PJ") == "1":
            return
        W = self.wnext("win1")
        self.copy("act", self.kcT[:, 16:528], fm(W, 0))
        self.copy("act", self.vcT[:, 16:528], fm(W, 1))
        self.qk_norm(fm(W, 2), self.qkg[:, 2:3], self.kslT[:, gs], 512)
        rs = slice(512 * (g % 2), 512 * (g % 2) + 512)
        self.qk_norm(fm(W, 3), self.qkg[:, 3:4], self.kwT[:, rs], 512)
        if g == int(os.environ.get("STOPG", "6")) and os.environ.get("PJ") == "2":
            return
        W = self.wnext("win2")
        for m in range(4):
            self.copy("act" if m % 2 else "dve", self.qbT[:, m, :], fm(W, m))
        if g == int(os.environ.get("STOPG", "6")) and os.environ.get("PJ") == "3":
            return
        W = self.wnext("win3")
        for m in range(4):
            self.copy("act" if m % 2 else "dve", self.kbT[:, m, gs], fm(W, m))
        if g == int(os.environ.get("STOPG", "6")) and os.environ.get("PJ") == "4":
            return
        W = self.wnext("win4")
        for t in range(4):
            pb = self.pA.next()
            for k in range(8):
                self.mm(pb, uT[:, k, 128 * t:128 * t + 128], W[:, k, :], k == 0, k == 7)
            self.copy("act" if t % 2 else "dve", self.vb[:, 4 * g + t, :], pb)
        if g == int(os.environ.get("STOPG", "6")) and os.environ.get("PJ") == "5":
            return
        W = self.wnext("win5")
        for t in range(4):
            pb = self.pA.next()
            for k in range(8):
                self.mm(pb[:, 0:280], uT[:, k, 128 * t:128 * t + 128], W[:, k, :], k == 0, k == 7)
            self.copy("act", self.vsl[:, 4 * g + t, :, 0:64], pb[:, 0:128].re("p (a b) -> p a b", a=2))
            self.copy("act", self.vw[:, (4 * g + t) % 8, :, 0:64], pb[:, 128:256].re("p (a b) -> p a b", a=2))
            self.act(self.gsig[:, t, :], pb[:, 256:280], AF.Sigmoid)
        S.trace_ops = False
        if stopat <= 2:
            return
        c_lo, c_hi = max(0, 32 * g - 1), 32 * g + 30
        n = c_hi - c_lo + 1
        for is_k, src, tag in ((True, self.kcT, "cw1k"), (False, self.vcT, "cw1v")):
            W1 = self.wnext(tag)
            hk = self.bf512.next()[0:64, 0:64].re("p (a b) -> p a b", a=2)
            for kvh in range(2):
                pbs = 64 * kvh
                pb = self.pA.next()
                for l in range(32):
                    col0 = 16 + 16 * c_lo - 512 * g + l
                    rhs = src[pbs:pbs + 64, col0:col0 + 16 * (n - 1) + 1:16]
                    self.mm(pb[0:64, 0:n], W1[pbs:pbs + 64, 64 * l:64 * l + 64], rhs, l == 0, l == 31)
                if is_k:
                    self.act(hk[:, kvh, 0:n], pb[0:64, 0:n], AF.Silu, bias=self.pbias[:, 0:1])
                else:
                    self.act(self.hidTv[:, kvh, c_lo:c_hi + 1], pb[0:64, 0:n], AF.Silu, bias=self.pbias[:, 1:2])
            if is_k:
                pb = self.pA.next()
                self.mm(pb[:, 0:n], self.cw2k[:, 0, :], hk[:, 0, 0:n], True, False)
                self.mm(pb[:, 0:n], self.cw2k[:, 1, :], hk[:, 1, 0:n], False, True)
                self.qk_norm(pb, self.qkg[:, 1:2], self.kcmpT[:, c_lo:c_hi + 1], n)
            else:
                for ch in range(c_lo // 128, c_hi // 128 + 1):
                    for kvh in range(2):
                        pb = self.pA.next()
                        self.mm(pb[:, 0:64], self.hidTv[:, kvh, 128 * ch:128 * ch + 128], self.cw2v)
                        self.copy("act", self.vcx[:, ch, kvh, 0:64], pb[:, 0:64])
        self.copy("pool", self.kcT[:, 0:16], self.kcT[:, 512:528])
        self.copy("pool", self.vcT[:, 0:16], self.vcT[:, 512:528])
        if stopat <= 3:
            return
        self.nsa_score = Rot([self.bank(2), self.bank(4)])
        self.pro_bank = self.bank(3)
        self.tp_banks = Rot([self.bank(7)])
        import os
        skn = int(os.environ.get("SKIP_NSA_FROM", "999"))
        sks = int(os.environ.get("SKIP_SB_FROM", "999"))
        for t in range(4):
            self.attn_tile(g, t, 4 * g + t < skn, 4 * g + t < sks)
        if g == 1 and b == 0:
            self.dbg_dump("onT", self.onT)
            self.dbg_dump("osT", self.osT)
            self.dbg_dump("kcmpT", self.kcmpT)
            self.dbg_dump("vcx", self.vcx)
            self.dbg_dump("sel", self.sel)
            self.dbg_dump("imp", self.imp)
            self.dbg_dump("rawc", self.rawc)
            self.dbg_dump("raws", self.raws)
        S.barrier()
        self.pA = Rot([self.bank(k) for k in range(6)])
        for t in range(4):
            self.dma("sp", self.xbuf_t[t], DR(d["x"][b, (4 * g + t) * 128:(4 * g + t + 1) * 128, :]))
        Wma = self.wnext("win6", live=1)
        Wmb = self.wnext("win8", live=2)
        Wn = self.wnext("wupn", live=3)
        Ws = self.wnext("wups", live=4)
        for m in range(8):
            if m == 4:
                Wma = self.wnext("win7", live=4)
                Wmb = self.wnext("win9", live=4)
            sga = self.sgt.next()
            self.act(sga, fm(Wma, m % 4), AF.Sigmoid)
            sgb = self.sgt.next()
            self.act(sgb, fm(Wmb, m % 4), AF.Sigmoid)
            pa = self.pA.next()
            for k in range(4):
                self.mm(pa, Wn[:, k, 128 * m:128 * m + 128], self.onT[:, k, :], k == 0, k == 3)
            pb2 = self.pA.next()
            for k in range(4):
                self.mm(pb2, Ws[:, k, 128 * m:128 * m + 128], self.osT[:, k, :], k == 0, k == 3)
            t1 = self.f32t.next()
            self.tt("dve", t1, pa, sga, ALU.mult)
            t2 = self.f32t.next()
            self.tt("dve", t2, pb2, sgb, ALU.mult)
            self.tt("pool", self.mixT[:, m, :], t1, t2, ALU.add)
        for cc in range(2):
            W = self.wnext("wout%d" % cc)
            for t in range(4):
                pb = self.pA.next()
                for k in range(8):
                    self.mm(pb, self.mixT[:, k, 128 * t:128 * t + 128], W[:, k, :], k == 0, k == 7)
                t1 = self.f32t.next()
                self.tt("dve", t1, pb, self.g1bc[:, 512 * cc:512 * cc + 512], ALU.mult)
                hv = self.xbuf_t[t][:, 512 * cc:512 * cc + 512]
                self.tt("dve", hv, hv, t1, ALU.add)
        if stopat <= 4:
            return
        S.barrier()
        self.pA = Rot([self.bank(k) for k in range(4)])
        self.pB = Rot([self.bank(k) for k in (4, 5)])
        for t in range(4):
            self.norm_tile(self.xbuf_t[t], t, self.A2, self.B2)
        for j in range(8):
            W1 = self.wnext("w1_%d" % j)
            hT = self.hT.next()
            for m in range(4):
                pb = self.pA.next()
                for k in range(8):
                    self.mm(pb, W1[:, k, 128 * m:128 * m + 128], uT[:, k, :], k == 0, k == 7)
                r = self.relu_t.next()
                self.act(r, pb, AF.Relu)
                self.tt("pool", hT[:, m, :], r, r, ALU.mult)
            W2 = self.wnext("w2_%d" % j)
            for t in range(4):
                for cc in range(2):
                    pb = self.pB.next()
                    for m in range(4):
                        self.mm(pb, hT[:, m, 128 * t:128 * t + 128], W2[:, m, 512 * cc:512 * cc + 512], m == 0, m == 3)
                    av = self.acc_t[t][:, 512 * cc:512 * cc + 512]
                    if j == 0:
                        self.copy("act", av, pb)
                    else:
                        self.tt("dve", av, av, pb, ALU.add)
        for t in range(4):
            self.tt("dve", self.acc_t[t], self.acc_t[t], self.g2bc, ALU.mult)
            self.tt("dve", self.xbuf_t[t], self.xbuf_t[t], self.acc_t[t], ALU.add)
            self.dma("sp", V(self.y[b, (4 * g + t) * 128:(4 * g + t + 1) * 128, :], Buf("y")), self.xbuf_t[t])
        S.barrier()

    def soft_stage1(self, it):
        nb = len(it["vrhs"])
        w = 128 * nb
        sc = self.nsa_score.next()
        col = 0
        for kr, wk in it["krhs"]:
            self.mm(sc[:, col:col + wk], it["qT"], kr)
            col += wk
        E = self.bf512.next()
        h = it["h"]
        self.act(E[:, 0:w], sc[:, 0:w], AF.Exp, scale=0.125)
        it["E"] = E

    def soft_stage2(self, it):
        nb = len(it["vrhs"])
        w = 128 * nb
        E = it["E"]
        for (c0, ncol, mk, is3d) in it["masks"]:
            ev = E[:, c0:c0 + ncol]
            if is3d:
                ev = ev.re("p (n k) -> p n k", k=64)
            self.tt("dve", ev, ev, mk, ALU.mult)
        yield
        tp = self.tp_banks.next().cast(BF16)
        for n_ in range(nb):
            self.tpose(tp[:, 128 * n_:128 * n_ + 128], E[:, 128 * n_:128 * n_ + 128])
        ET = self.bf512.next()
        self.copy("dve", ET[:, 0:w], tp[:, 0:w])
        yield
        for n_ in range(nb):
            self.mm(it["acc"], ET[:, 128 * n_:128 * n_ + 128], it["vrhs"][n_],
                    it["first"] and n_ == 0, it["last"] and n_ == nb - 1)
        if it["fin"] is not None:
            self.copy("dve", self.raws[:, it["fin"], :], it["ob"][:, 0:130])
        yield

    def nsa_prologue(self, g, t):
        i = 4 * g + t
        qc = slice(128 * t, 128 * t + 128)
        Wc = 8 * i + 7
        nch = 1 if Wc <= 128 else 2
        lo, hi = max(0, 8 * i - 9), 8 * i + 7
        off = 8 * i - 9

        def st1(h):
            pbs, j = 64 * (h // 4), h % 4
            qT = self.qaT[pbs:pbs + 64, j, qc]
            sc = self.nsa_score.next()
            self.mm(sc[:, 0:Wc], qT, self.kcmpT[pbs:pbs + 64, 0:Wc])
            E = self.ecb.next()
            self.act(E[:, 0:Wc], sc[:, 0:Wc], AF.Exp, scale=0.125)
            return E

        def st2(h, E):
            kvh = h // 4
            self.tt("dve", E[:, lo:hi], E[:, lo:hi], self.R[:, h, 256 + lo - off:256 + hi - off], ALU.mult)
            yield
            tp = self.tp_banks.next().cast(BF16)
            for ch in range(nch):
                self.tpose(tp[:, 128 * ch:128 * ch + 128], E[:, 128 * ch:128 * ch + 128])
            ET = self.bf512.next()
            self.copy("act", ET[:, 0:128 * nch], tp[:, 0:128 * nch])
            yield
            oc = self.pro_bank
            for ch in range(nch):
                self.mm(oc[:, 0:129], ET[:, 128 * ch:128 * ch + 128], self.vcx[:, ch, kvh, :], ch == 0, ch == nch - 1)
            self.copy("dve", self.rawc[:, h, :], oc[:, 0:129])
            yield

        prev = None
        for h in range(8):
            E = st1(h)
            yield
            if prev is not None:
                yield from st2(*prev)
            prev = (h, E)
        yield from st2(*prev)
        rsc = self.sm8.next()[:, :, 0]
        self.ts("dve", rsc, self.rawc[:, :, 64], 1e-30, None, ALU.max)
        self.S.op("dve", lambda h_: h_.reciprocal(rsc.ap, rsc.ap), [rsc], [rsc])
        tmp = self.f32n.next().re("p (a b) -> p a b", a=8)
        rsc3 = V(rsc.ap.unsqueeze(2), rsc.buf)
        self.tt("dve", tmp, self.rawc[:, :, 65:129], rsc3.bc([128, 8, 64]), ALU.mult)
        self.S.op("dve", lambda h_: h_.tensor_reduce(self.imp.ap, tmp.ap.rearrange("p (k g) n -> p k n g", g=4),
                                                     AX.X, ALU.add), [tmp], [self.imp])
        fw = V(self.fwide.ap[:, 62 - 2 * i:126 - 2 * i].unsqueeze(1), self.fwide.buf)
        self.tt("dve", self.imp, self.imp, fw.bc([128, 2, 64]), ALU.add)
        self.ts("dve", self.imp[:, :, 0:1], self.imp[:, :, 0:1], 1e4, None, ALU.add)
        yield
        for kvh in range(2):
            iv = self.imp[:, kvh, :]
            m8 = self.m8.next()
            self.S.op("dve", lambda h_, m8=m8, iv=iv: h_.max(m8.ap, iv.ap), [iv], [m8])
            self.S.op("dve", lambda h_, m8=m8, iv=iv: h_.match_replace(self.imp2.ap, m8.ap, iv.ap, -1e30),
                      [iv, m8], [self.imp2])
            m8b = self.m8.next()
            self.S.op("dve", lambda h_, m8b=m8b: h_.max(m8b.ap, self.imp2.ap), [self.imp2], [m8b])
            self.ts("dve", self.sel[:, kvh, :], iv, m8b[:, 7:8], None, ALU.is_ge)
            yield
        coef = self.sm8.next()[:, :, 0]
        gv = self.gsig[:, t, :].re("p (h c) -> p h c", c=3)
        self.tt("dve", coef, rsc, gv[:, :, 0], ALU.mult)
        coef3 = V(coef.ap.unsqueeze(2), coef.buf)
        self.tt("dve", self.oacc, self.rawc[:, :, 0:64], coef3.bc([128, 8, 64]), ALU.mult)
        yield

    def nsa_heads(self, g, t, heads, ob):
        i = 4 * g + t
        Kt = (i + 1) * 128
        qc = slice(128 * t, 128 * t + 128)
        nchunk = i // 4 + 1
        items = []
        for h in heads:
            kvh, j, pbs = h // 4, h % 4, 64 * (h // 4)
            qT = self.qaT[pbs:pbs + 64, j, qc]
            for c in range(nchunk):
                w = min(512, Kt - 512 * c)
                nb = w // 128
                masks = []
                selv = V(self.sel.ap[:, kvh, 8 * c:8 * c + w // 64].unsqueeze(2), self.sel.buf)
                masks.append((0, w, selv.bc([128, w // 64, 64]), True))
                for kb in (i - 1, i):
                    if kb >= 0 and kb // 4 == c:
                        ro = (kb - (i - 1)) * 128
                        masks.append(((kb % 4) * 128, 128, self.R[:, h, ro:ro + 128], False))
                items.append(dict(qT=qT, krhs=[(self.kslT[pbs:pbs + 64, 512 * c:512 * c + w], w)], h=h, masks=masks,
                                  vrhs=[self.vsl[:, 4 * c + n_, kvh, :] for n_ in range(nb)],
                                  acc=ob[:, 0:65], first=(c == 0), last=(c == nchunk - 1), fin=None, ob=ob))
            kbs = list(range(max(0, i - 4), i + 1))
            parts = [p for p in (kbs[0:4], kbs[4:]) if p]
            for pi, part in enumerate(parts):
                masks = []
                for li, kb in enumerate(part):
                    if kb == i - 4:
                        masks.append((128 * li, 128, self.ustrict_bf, False))
                    if kb >= i - 1:
                        ro = (kb - (i - 1)) * 128
                        masks.append((128 * li, 128, self.R[:, h, ro:ro + 128], False))
                items.append(dict(qT=qT, krhs=[(self.kwT[pbs:pbs + 64, 128 * (kb % 8):128 * (kb % 8) + 128], 128)
                                               for kb in part], h=h, masks=masks,
                                  vrhs=[self.vw[:, kb % 8, kvh, :] for kb in part],
                                  acc=ob[:, 65:130], first=(pi == 0), last=(pi == len(parts) - 1),
                                  fin=(h if pi == len(parts) - 1 else None), ob=ob))
        prev = None
        for it in items:
            self.soft_stage1(it)
            yield
            if prev is not None:
                yield from self.soft_stage2(prev)
            prev = it
        yield from self.soft_stage2(prev)

    def nsa_epilogue(self, g, t):
        qc = slice(128 * t, 128 * t + 128)
        gv = self.gsig[:, t, :].re("p (h c) -> p h c", c=3)
        rs2 = self.sm8.next()
        r4 = self.raws.re("p h (b k) -> p h b k", k=65)
        self.ts("dve", rs2, r4[:, :, :, 64], 1e-30, None, ALU.max)
        self.S.op("dve", lambda h_: h_.reciprocal(rs2.ap, rs2.ap), [rs2], [rs2])
        for br in (0, 1):
            cf = self.sm8.next()[:, :, 0]
            self.tt("dve", cf, rs2[:, :, br], gv[:, :, 1 + br], ALU.mult)
            cf3 = V(cf.ap.unsqueeze(2), cf.buf)
            tmp = self.f32n.next().re("p (a b) -> p a b", a=8)
            self.tt("dve", tmp, r4[:, :, br, 0:64], cf3.bc([128, 8, 64]), ALU.mult)
            self.tt("dve", self.oacc, self.oacc, tmp, ALU.add)
        ob_ = self.obf.next()
        self.copy("act", ob_, self.oacc.re("p a b -> p (a b)"))
        tp = self.tp_banks.next().cast(BF16)
        for k in range(4):
            self.tpose(tp[:, 128 * k:128 * k + 128], ob_[:, 128 * k:128 * k + 128])
        self.copy("act", self.onT[:, :, qc], tp[:, 0:512].re("p (k n) -> p k n", k=4))

    def sb_heads(self, g, t, heads, cb, cars, er, spr, bfr, sc, acc):
        i = 4 * g + t
        Kt = (i + 1) * 128
        qc = slice(128 * t, 128 * t + 128)
        nchunk = i // 4 + 1
        items = []
        for h in heads:
            pbs, j = 64 * (h % 2), h // 2
            for c in range(nchunk - 1, -1, -1):
                w = min(512, Kt - 512 * c)
                items.append(dict(h=h, c=c, w=w, nb=w // 128, diag=(c == nchunk - 1), pbs=pbs, j=j,
                                  qT=self.qbT[pbs:pbs + 64, j, qc]))

        def st1(it):
            w, c, pbs, j = it["w"], it["c"], it["pbs"], it["j"]
            self.mm(sc[:, 0:w], it["qT"], self.kbT[pbs:pbs + 64, j, 512 * c:512 * c + w])
            e = er.next()
            self.act(e[:, 0:w], sc[:, 0:w], AF.Exp, scale=0.125)
            yield
            sp = spr.next()
            self.act(sp[:, 0:w], e[:, 0:w], AF.Ln, bias=1.0)
            if it["diag"]:
                self.tt("pool", sp[:, w - 128:w], sp[:, w - 128:w], self.mstrict_f, ALU.mult)
            it["e"], it["sp"] = e, sp
            yield

        def st2(it, carry):
            w, nb, h, c = it["w"], it["nb"], it["h"], it["c"]
            e, sp = it["e"], it["sp"]
            ones = self.onecol.bc([128, w])
            init = 0.0 if carry is None else carry
            ins = [ones, sp] + ([carry] if carry is not None else [])

            def scan(h_, sp=sp, w=w, init=init, ones=ones):
                iv = init.ap if isinstance(init, V) else init
                return h_.tensor_tensor_scan(cb.ap[:, 0:w][:, ::-1], ones.ap, sp.ap[:, 0:w][:, ::-1], iv,
                                             ALU.mult, ALU.add)
            self.S.op("dve", scan, ins, [cb])
            yield
            self.act(sp[:, 0:w], cb[:, 0:w], AF.Exp, scale=-1.0)
            ncar = cars.next()
            self.copy("dve", ncar, cb[:, 0:1])
            it["carry_out"] = ncar
            yield
            Ab = bfr.next()
            self.tt("dve", Ab[:, 0:w], e[:, 0:w], sp[:, 0:w], ALU.mult)
            if it["diag"]:
                self.tt("pool", Ab[:, w - 128:w], Ab[:, w - 128:w], self.mstrict_bf, ALU.mult)
            yield
            tp = self.tp_banks.next().cast(BF16)
            for n_ in range(nb):
                self.tpose(tp[:, 128 * n_:128 * n_ + 128], Ab[:, 128 * n_:128 * n_ + 128])
            AT = bfr.next()
            self.copy("act", AT[:, 0:w], tp[:, 0:w])
            yield
            for n_ in range(nb):
                self.mm(acc[:, 64 * (h % 4):64 * (h % 4) + 64], AT[:, 128 * n_:128 * n_ + 128],
                        self.vb[:, 4 * c + n_, 64 * h:64 * h + 64], it["diag"] and n_ == 0, c == 0 and n_ == nb - 1)
            yield

        prev = None
        for it in items:
            yield from st1(it)
            if prev is not None:
                carry = None if prev["diag"] else prev["carry_in"]
                yield from st2(prev, carry)
                it["carry_in"] = prev["carry_out"]
            prev = it
        carry = None if prev["diag"] else prev["carry_in"]
        yield from st2(prev, carry)

    def sb_epilogue(self, g, t):
        qc = slice(128 * t, 128 * t + 128)
        ob_ = self.obf.next()
        self.copy("act", ob_[:, 0:256], self.bank(5)[:, 0:256])
        self.copy("act", ob_[:, 256:512], self.bank(6)[:, 0:256])
        tp = self.tp_banks.next().cast(BF16)
        for k in range(4):
            self.tpose(tp[:, 128 * k:128 * k + 128], ob_[:, 128 * k:128 * k + 128])
        self.copy("dve", self.osT[:, :, qc], tp[:, 0:512].re("p (k n) -> p k n", k=4))

    @staticmethod
    def run_lanes(lanes):
        lanes = list(lanes)
        while lanes:
            nxt = []
            for ln in lanes:
                try:
                    next(ln)
                    nxt.append(ln)
                except StopIteration:
                    pass
            lanes = nxt

    def attn_tile(self, g, t, do_nsa=True, do_sb=True):
        def nsa_lane():
            yield from self.nsa_prologue(g, t)
            yield from self.nsa_heads(g, t, range(0, 8), self.bank(3))
            self.nsa_epilogue(g, t)
        lanes = []
        if do_nsa:
            lanes.append(nsa_lane())
        if do_sb:
            lanes.append(self.sb_heads(g, t, range(0, 4), self.cbufA, self.carA, self.eA, self.spA, self.bfA, self.bank(0), self.bank(5)))
            lanes.append(self.sb_heads(g, t, range(4, 8), self.cbufB, self.carB, self.eB, self.spB, self.bfB, self.bank(1), self.bank(6)))
        self.run_lanes(lanes)
        if do_sb:
            self.sb_epilogue(g, t)


_CACHE = {}


def run(inputs, S, NB, ncores, batch0=0):
    key = (S, NB)
    if key not in _CACHE:
        _CACHE[key] = Prog(S, NB).build()
    nc = _CACHE[key]
    maps = host_prep(inputs, S, NB, ncores, batch0)
    res = run_bass_kernel_spmd(nc, maps, core_ids=list(range(ncores)))
    return np.concatenate([np.asarray(r["y"]) for r in res.results], axis=0)


def kernel(**inputs):
    out = run(inputs, SEQ, BATCH // NCORES, NCORES)
    return out.astype(np.float32)
```

```python
import numpy as np
from contextlib import ExitStack
import concourse.bass as bass
import concourse.mybir as mybir
from concourse.bass_utils import run_bass_kernel_spmd

F32 = mybir.dt.float32
BF16 = mybir.dt.bfloat16
AF = mybir.ActivationFunctionType
ALU = mybir.AluOpType
AX = mybir.AxisListType

D = 1024
DH = 64
NH = 8
EPS = 1e-6
NCORES = 8
SEQ = 4096
BATCH = 16
IN_W = 4888


class Buf:
    __slots__ = ("name", "w", "r")

    def __init__(self, name=""):
        self.name = name
        self.w = {}
        self.r = {}


class V:
    __slots__ = ("ap", "buf")

    def __init__(self, ap, buf):
        self.ap = ap
        self.buf = buf

    def __getitem__(self, k):
        return V(self.ap[k], self.buf)

    def re(self, pat, **kw):
        return V(self.ap.rearrange(pat, **kw), self.buf)

    def bc(self, shape):
        return V(self.ap.to_broadcast(list(shape)), self.buf)

    def cast(self, dt):
        return V(self.ap.bitcast(dt), self.buf)


class Ev:
    __slots__ = ("sem", "seq", "key", "needed", "val")

    def __init__(self, sem, seq, key, val=None):
        self.sem = sem
        self.seq = seq
        self.key = key
        self.needed = False
        self.val = val


class Sched:
    ENGS = ("pe", "act", "dve", "pool", "sp")

    def __init__(self, sems, dma_sems):
        self.sem = dict(zip(self.ENGS, sems))
        self.epoch = 0
        self.cnt = {e: 0 for e in self.ENGS}
        self.prog = {e: [] for e in self.ENGS}
        self.seen = {e: {} for e in self.ENGS}
        self.dma_sems = dma_sems
        self.dma_cnt = {q: 0 for q in dma_sems}
        self.dma_n = 0
        self.dma_last = {}
        self.last_ev = {}
        self.all_ev = {}
        self.ninstr = 0

    def _need(self, eng, ev, waits, raw):
        key = ev.key
        if key[0] == "e" and key[1] == eng and not raw:
            return
        if self.seen[eng].get(key, 0) >= ev.seq:
            return
        cur = waits.get(key)
        if cur is None or cur.seq < ev.seq:
            waits[key] = ev

    def _commit(self, eng, waits):
        wl = list(waits.values())
        for ev in wl:
            ev.needed = True
            if self.seen[eng].get(ev.key, 0) < ev.seq:
                self.seen[eng][ev.key] = ev.seq
        return wl

    def op(self, eng, fn, ins=(), outs=(), dma=False):
        waits = {}
        for v in ins:
            for ev in v.buf.w.values():
                self._need(eng, ev, waits, True)
        for v in outs:
            b = v.buf
            for ev in b.w.values():
                self._need(eng, ev, waits, False)
            for ev in b.r.values():
                self._need(eng, ev, waits, False)
        if dma:
            pool_ = self.dma_sems[eng]
            ns = len(pool_)
            slot = self.dma_cnt[eng] % ns
            rnd = self.dma_cnt[eng] // ns
            self.dma_cnt[eng] += 1
            self.dma_n += 1
            dsem = pool_[slot]
            key = ("dma", eng, slot)
            slot = (eng, slot)
            if rnd > 0:
                self._need(eng, Ev(dsem, rnd, key, 16 * rnd), waits, True)
            ev = Ev(dsem, rnd + 1, key, 16 * (rnd + 1))
            ev.needed = True
            self.dma_last[slot] = ev
        else:
            self.cnt[eng] += 1
            key = ("e", eng, self.epoch)
            ev = Ev(self.sem[eng], self.cnt[eng], key)
            self.last_ev[eng] = ev
            self.all_ev.setdefault(key, []).append(ev)
        wl = self._commit(eng, waits)

        def emit(h, fn=fn, wl=wl, ev=ev, dma=dma):
            for w in wl:
                h.wait_ge(w.sem, w.val)
            ins_ = fn(h)
            if dma:
                ins_.then_inc(ev.sem, 16)
            elif ev.needed:
                ins_.then_inc(ev.sem, 1)

        self.prog[eng].append(emit)
        self.ninstr += 1
        for v in ins:
            v.buf.r[ev.key] = ev
        for v in outs:
            v.buf.w = {ev.key: ev}
            v.buf.r = {}
        return ev

    def barrier(self):
        evs = [self.last_ev[e] for e in self.ENGS if self.cnt[e] > 0 and e in self.last_ev]
        evs += list(self.dma_last.values())
        for eng in self.ENGS:
            waits = {}
            for ev in evs:
                if ev.key[0] == "e" and ev.key[1] == eng:
                    continue
                self._need(eng, ev, waits, True)
            wl = self._commit(eng, waits)
            if wl:
                self.prog[eng].append(lambda h, wl=wl: [h.wait_ge(w.sem, w.val) for w in wl])

    def new_epoch(self, sems):
        for eng in self.ENGS:
            for e2 in self.ENGS:
                self.seen[eng][("e", e2, self.epoch)] = 1 << 40
        self.epoch += 1
        self.sem = dict(zip(self.ENGS, sems))
        self.cnt = {e: 0 for e in self.ENGS}
        self.last_ev = {}

    def final_wait(self, eng="sp"):
        wl = list(self.dma_last.values())
        wl += [self.last_ev[e] for e in self.ENGS if e != eng and e in self.last_ev]
        for w in wl:
            w.needed = True
        self.prog[eng].append(lambda h, wl=wl: [h.wait_ge(w.sem, w.val) for w in wl])

    def finalize(self):
        ninc = 0
        for key, evs in self.all_ev.items():
            c = 0
            for ev in evs:
                if ev.needed:
                    c += 1
                    ninc += 1
                ev.val = c
        self.ninc = ninc

    def emit_all(self, block):
        self.finalize()
        prog = self.prog

        @block.tensor
        def _(h):
            for f in prog["pe"]:
                f(h)

        @block.scalar
        def _(h):
            for f in prog["act"]:
                f(h)

        @block.vector
        def _(h):
            for f in prog["dve"]:
                f(h)

        @block.gpsimd
        def _(h):
            for f in prog["pool"]:
                f(h)

        @block.sync
        def _(h):
            for f in prog["sp"]:
                f(h)


class Rot:
    def __init__(self, items):
        self.items = items
        self.i = 0

    def next(self):
        it = self.items[self.i % len(self.items)]
        self.i += 1
        return it


def _bucket(dist):
    n = np.maximum(dist, 0)
    nf = np.maximum(n, 1).astype(np.float64)
    raw = np.log(nf / 16.0) / np.log(8.0) * 16.0
    large = 16 + np.floor(raw + 1e-9).astype(np.int64)
    large = np.minimum(large, 31)
    return np.where(n < 16, n, large)


def _consts():
    a = np.arange(128)[:, None]
    ind = np.zeros((32, 128, 272), np.float32)
    m = np.arange(256)[None, :]
    dist = (1 - m // 128) * 128 + a - (m % 128)
    bk = _bucket(dist)
    for b in range(32):
        ind[b, :, :256] = ((bk == b) & (dist >= 0))
    w = np.arange(16)[None, :]
    distc = a - 16 * (w - 9) - 31
    bkc = _bucket(distc)
    for b in range(32):
        ind[b, :, 256:] = ((bkc == b) & (distc >= 0))
    fw = np.zeros((128, 126), np.float32)
    rel = np.arange(126)[None, :] - 62
    lo = (a < 64)
    fw[:] = np.where(rel >= 2, -1e9, 0.0)
    fw += np.where(rel == 1, np.where(lo, -1e9, 1e4), 0.0)
    fw += np.where(rel == 0, 1e4, 0.0)
    fw += np.where(rel == -1, np.where(lo, 1e4, 0.0), 0.0)
    cc = np.arange(256)[:, None]
    nn = np.arange(64)[None, :]
    ov = np.minimum(16 * cc + 32, 64 * nn + 64) - np.maximum(16 * cc, 64 * nn)
    ov = np.clip(ov, 0, 32).astype(np.float32) / 32.0
    ov[255] = 0.0
    ovx = np.zeros((128, 2, 65), np.float32)
    ovx[:, :, 0] = 1.0
    ovx[:, 0, 1:] = ov[:128]
    ovx[:, 1, 1:] = ov[128:]
    b_ = np.arange(128)[None, :]
    misc = np.zeros((128, 5, 128), np.float32)
    misc[:, 0] = np.eye(128)
    misc[:, 1] = ((a // 64) == (b_ // 64))
    misc[:, 2] = (b_ < a)
    misc[:, 3] = (b_ > a)
    misc[:, 4] = 1.0
    return ind, fw, ovx, misc


def _win_perm():
    q = []
    for j in range(4):
        q += list(range(j * 64, j * 64 + 64)) + list(range((j + 4) * 64, (j + 4) * 64 + 64))
    o_kc, o_vc, o_ksl, o_vsl, o_kwn, o_vwn, o_ga = 512, 640, 768, 896, 1024, 1152, 1280
    o_qb, o_kb, o_vb, o_ma, o_mb = 1304, 1816, 2328, 2840, 3864
    r = lambda s, n: list(range(s, s + n))
    perm = q + r(o_kc, 128) + r(o_vc, 128) + r(o_ksl, 128) + r(o_kwn, 128)
    perm += r(o_qb, 512) + r(o_kb, 512) + r(o_vb, 512)
    perm += r(o_vsl, 128) + r(o_vwn, 128) + r(o_ga, 24)
    perm += r(o_ma, 1024) + r(o_mb, 1024)
    assert len(perm) == IN_W and len(set(perm)) == IN_W
    return np.array(perm)


WC = [(0, 512), (512, 1024), (1024, 1536), (1536, 2048), (2048, 2560), (2560, 2840),
      (2840, 3352), (3352, 3864), (3864, 4376), (4376, 4888)]


def host_prep(inp, S, NB, ncores, batch0=0):
    f = lambda a: np.ascontiguousarray(a, dtype=np.float32)
    ind, fw, ovx, misc = _consts()
    vecT = lambda v: f(np.asarray(v).reshape(-1, 128).T)
    dup = lambda v: f(np.concatenate([v, v], axis=0))
    w1k = np.asarray(inp["cmp_k_w1"][0]).reshape(32, 64, 64).transpose(1, 0, 2).reshape(64, 2048)
    w1v = np.asarray(inp["cmp_v_w1"][0]).reshape(32, 64, 64).transpose(1, 0, 2).reshape(64, 2048)
    w2k = np.asarray(inp["cmp_k_w2"][0])
    w2kpad = np.zeros((64, 2, 128), np.float32)
    w2kpad[:, 0, 0:64] = w2k
    w2kpad[:, 1, 64:128] = w2k
    kng = np.asarray(inp["k_norm_g"][0])
    shared = {
        "adaw": f(inp["ada_w"][0]),
        "adabT": vecT(inp["ada_b"][0]),
        "adab": f(np.asarray(inp["ada_b"][0]).reshape(1, 6144)),
        "n1g": vecT(inp["norm1_g"][0]),
        "n2g": vecT(inp["norm2_g"][0]),
        "win": f(np.asarray(inp["w_in"][0])[:, _win_perm()]),
        "cw1k": dup(w1k), "cw1v": dup(w1v),
        "cposT": dup(np.asarray(inp["cmp_pos"][0]).T),
        "cw2k": f(w2kpad.reshape(64, 256)),
        "cw2v": f(inp["cmp_v_w2"][0]),
        "qkg": f(np.stack([np.tile(np.asarray(inp["q_norm_g"][0]), 2), np.tile(kng[0], 2),
                           np.tile(kng[1], 2), np.tile(kng[2], 2)], axis=1)),
        "wupn": f(inp["w_up_nsa"][0]), "wups": f(inp["w_up_sb"][0]),
        "wout": f(inp["w_out"][0]), "w1": f(inp["mlp_w1"][0]), "w2": f(inp["mlp_w2"][0]),
        "relb": f(np.asarray(inp["rel_bias"]).reshape(1, 256)),
        "c_ind": ind, "c_fw": fw, "c_ovx": f(ovx.reshape(128, 130)), "c_misc": f(misc.reshape(128, 640)),
    }
    maps = []
    x = np.asarray(inp["x"])
    c = np.asarray(inp["c"])
    for core in range(ncores):
        b0 = batch0 + core * NB
        m = dict(shared)
        m["x"] = f(x[b0:b0 + NB, :S])
        m["cT"] = f(np.stack([c[b0 + i].reshape(8, 128).T for i in range(NB)], axis=0))
        maps.append(m)
    return maps


class Prog:
    def __init__(self, S, NB):
        self.S_len = S
        self.NB = NB
        self.dbg = False
        self.dbg_names = []
        self.NG = S // 512
        self.NT = S // 128

    def mm(self, out, lhsT, rhs, start=True, stop=True):
        self.S.op("pe", lambda h: h.matmul(out.ap, lhsT.ap, rhs.ap, start=start, stop=stop),
                  [lhsT, rhs], [out])

    def tpose(self, out, in_):
        idn = self.ident
        self.S.op("pe", lambda h: h.transpose(out.ap, in_.ap, idn.ap), [in_, idn], [out])

    def act(self, out, in_, func, scale=1.0, bias=0.0, accum=None):
        ins = [in_]
        outs = [out]
        kw = dict(out=out.ap, in_=in_.ap, func=func)
        if isinstance(scale, V):
            ins.append(scale)
            kw["scale"] = scale.ap
        else:
            kw["scale"] = float(scale)
        if isinstance(bias, V):
            ins.append(bias)
            kw["bias"] = bias.ap
        elif bias != 0.0:
            kw["bias"] = float(bias)
        if accum is not None:
            outs.append(accum)
            kw["accum_out"] = accum.ap
        self.S.op("act", lambda h: h.activation(**kw), ins, outs)

    def tt(self, eng, out, a, b, op):
        self.S.op(eng, lambda h: h.tensor_tensor(out.ap, a.ap, b.ap, op), [a, b], [out])

    def ts(self, eng, out, a, s1, s2, op0, op1=None):
        ins = [a]
        if isinstance(s1, V):
            ins.append(s1)
        if isinstance(s2, V):
            ins.append(s2)
        g = lambda s: s.ap if isinstance(s, V) else s
        if op1 is None:
            self.S.op(eng, lambda h: h.tensor_scalar(out.ap, a.ap, g(s1), None, op0), ins, [out])
        else:
            self.S.op(eng, lambda h: h.tensor_scalar(out.ap, a.ap, g(s1), g(s2), op0, op1), ins, [out])

    def stt(self, out, a, s, b, op0, op1):
        ins = [a, b]
        if isinstance(s, V):
            ins.append(s)
        g = s.ap if isinstance(s, V) else s
        self.S.op("dve", lambda h: h.scalar_tensor_tensor(out.ap, a.ap, g, b.ap, op0, op1), ins, [out])

    def copy(self, eng, out, in_):
        if eng == "act":
            self.act(out, in_, AF.Copy)
        else:
            self.S.op(eng, lambda h: h.tensor_copy(out.ap, in_.ap), [in_], [out])

    def memset(self, eng, out, val):
        self.S.op(eng, lambda h: h.memset(out.ap, val), [], [out])

    def dbg_dump(self, name, v):
        if not getattr(self, "dbg", False):
            return
        shp = list(v.ap.shape)
        n = int(np.prod(shp[1:]))
        t = self.nc.dram_tensor("dbg_" + name, [shp[0], n], F32, kind="ExternalOutput").ap()
        if len(shp) == 3:
            t = t.rearrange("p (a b) -> p a b", a=shp[1])
        elif len(shp) == 4:
            t = t.rearrange("p (a b c) -> p a b c", a=shp[1], b=shp[2])
        self.dbg_names.append("dbg_" + name)
        self.dma("pool", V(t, Buf("dbg")), v)

    def dma(self, q, out, in_):
        self.S.op(q, lambda h: h.dma_start(out=out.ap, in_=in_.ap), [in_], [out], dma=True)

    def alloc(self, shape, dt, name=""):
        n = int(np.prod(shape[1:]))
        nbytes = n * (2 if dt == BF16 else 4)
        nbytes = (nbytes + 31) // 32 * 32
        off = self.sb_off
        self.sb_off += nbytes
        assert self.sb_off <= self.sb_bytes, (name, self.sb_off, self.sb_bytes)
        ap = self.sb[:, off // 4:(off + nbytes) // 4]
        if dt == BF16:
            ap = ap.bitcast(BF16)
        ap = ap[:, 0:n]
        v = V(ap, Buf(name))
        if len(shape) == 3:
            v = v.re("p (a b) -> p a b", a=shape[1])
        elif len(shape) == 4:
            v = v.re("p (a b c) -> p a b c", a=shape[1], b=shape[2])
        if shape[0] < 128:
            v = v[0:shape[0]]
        return v

    def bank(self, k, dt=F32):
        ap = self.ps[:, 512 * k:512 * (k + 1)]
        if dt == BF16:
            ap = ap.bitcast(BF16)
        return V(ap, self.bank_buf[k])

    def plan_chunks(self):
        d = self.d
        r8 = lambda ap: ap.rearrange("(k p) n -> p k n", p=128)
        ch = []
        ch.append(("cw1k", d["cw1k"], (128, 2048)))
        ch.append(("cw1v", d["cw1v"], (128, 2048)))
        for b in range(self.NB):
            for j in range(12):
                ch.append(("ada%d" % j, r8(d["adaw"])[:, :, 512 * j:512 * j + 512], (128, 8, 512)))
            for g in range(self.NG):
                for j in range(6):
                    c0, c1 = WC[j]
                    ch.append(("win%d" % j, r8(d["win"])[:, :, c0:c1], (128, 8, c1 - c0)))
                ch.append(("cw1k", d["cw1k"], (128, 2048)))
                ch.append(("cw1v", d["cw1v"], (128, 2048)))
                for j in (6, 8):
                    c0, c1 = WC[j]
                    ch.append(("win%d" % j, r8(d["win"])[:, :, c0:c1], (128, 8, c1 - c0)))
                ch.append(("wupn", r8(d["wupn"]), (128, 4, 1024)))
                ch.append(("wups", r8(d["wups"]), (128, 4, 1024)))
                for j in (7, 9):
                    c0, c1 = WC[j]
                    ch.append(("win%d" % j, r8(d["win"])[:, :, c0:c1], (128, 8, c1 - c0)))
                for j in range(2):
                    ch.append(("wout%d" % j, r8(d["wout"])[:, :, 512 * j:512 * j + 512], (128, 8, 512)))
                for j in range(8):
                    ch.append(("w1_%d" % j, r8(d["w1"])[:, :, 512 * j:512 * j + 512], (128, 8, 512)))
                    ch.append(("w2_%d" % j, r8(d["w2"])[:, 4 * j:4 * j + 4, :], (128, 4, 1024)))
        self.chunks = ch
        self.ch_pos = 0
        self.ch_issued = 0

    def _slot_view(self, k):
        tag, src, shape = self.chunks[k]
        slot = self.wslots[k % len(self.wslots)]
        if len(shape) == 2:
            return slot[:, 0:shape[1]]
        v = slot.re("p (a b) -> p a b", a=shape[1])
        if shape[1] == 8 and shape[2] < 512:
            v = v[:, :, 0:shape[2]]
        return v

    def wnext(self, tag, live=1):
        k = self.ch_pos
        assert self.chunks[k][0] == tag, (self.chunks[k][0], tag)
        self.ch_pos += 1
        ns = len(self.wslots)
        import os
        depth = int(os.environ.get("PREF", "99"))
        while self.ch_issued < min(len(self.chunks), k + min(ns - live, depth) + 1):
            kk = self.ch_issued
            if not (os.environ.get("NOWDMA") == "1" and kk > 40):
                self.dma("pool", self._slot_view(kk), V(self.chunks[kk][1], self.dram_buf))
            self.ch_issued += 1
        return self._slot_view(k)

    def build(self):
        S, NB, NG, NT = self.S_len, self.NB, self.NG, self.NT
        nc = bass.Bass("TRN2", target_bir_lowering=False)
        self.nc = nc
        din = lambda name, shape: nc.dram_tensor(name, list(shape), F32, kind="ExternalInput").ap()
        d = {}
        d["x"] = din("x", [NB, S, D])
        d["cT"] = din("cT", [NB, 128, 8])
        d["adaw"] = din("adaw", [D, 6 * D])
        d["adabT"] = din("adabT", [128, 48])
        d["adab"] = din("adab", [1, 6 * D])
        d["n1g"] = din("n1g", [128, 8])
        d["n2g"] = din("n2g", [128, 8])
        d["win"] = din("win", [D, IN_W])
        d["cw1k"] = din("cw1k", [128, 2048])
        d["cw1v"] = din("cw1v", [128, 2048])
        d["cposT"] = din("cposT", [128, 32])
        d["cw2k"] = din("cw2k", [64, 256])
        d["cw2v"] = din("cw2v", [64, 64])
        d["qkg"] = din("qkg", [128, 4])
        d["wupn"] = din("wupn", [512, D])
        d["wups"] = din("wups", [512, D])
        d["wout"] = din("wout", [D, D])
        d["w1"] = din("w1", [D, 4 * D])
        d["w2"] = din("w2", [4 * D, D])
        d["relb"] = din("relb", [1, 256])
        d["c_ind"] = din("c_ind", [32, 128, 272])
        d["c_fw"] = din("c_fw", [128, 126])
        d["c_ovx"] = din("c_ovx", [128, 130])
        d["c_misc"] = din("c_misc", [128, 640])
        self.d = d
        self.y = nc.dram_tensor("y", [NB, S, D], F32, kind="ExternalOutput").ap()
        self.dram_buf = Buf("dram_in")
        self.y_buf = Buf("y")

        with ExitStack() as st:
            self.sb_bytes = 212832
            self.sb = st.enter_context(nc.sbuf_tensor("sb", [128, self.sb_bytes // 4], F32))
            self.sb_off = 0
            self.ps = st.enter_context(nc.psum_tensor("ps", [128, 4096], F32))
            self.bank_buf = [Buf("bank%d" % k) for k in range(8)]
            sems = [st.enter_context(nc.semaphore("s_" + e)) for e in Sched.ENGS]
            dsems = {q: [st.enter_context(nc.semaphore("d%s_%d" % (q, i))) for i in range(16)] for q in ("sp", "pool")}
            self.esems = [[st.enter_context(nc.semaphore("s%d_%s" % (i, e))) for e in Sched.ENGS]
                          for i in range((NB * NG + 3) // 4)]
            self.S = Sched(sems, dsems)
            self.setup()
            for b in range(NB):
                self.seq_init(b)
                import os
                for g in range(min(NG, int(os.environ.get("MAXG", "99")))):
                    self.group(b, g)
            self.S.final_wait("sp")
            with nc.Block() as block:
                self.S.emit_all(block)
        return nc

    def setup(self):
        d = self.d
        A = self.alloc
        DR = lambda ap: V(ap, self.dram_buf)
        misc = A([128, 5, 128], BF16, "misc")
        self.dma("pool", misc, DR(d["c_misc"].rearrange("p (a b) -> p a b", a=5)))
        self.ident = misc[:, 0, :]
        self.blk1 = misc[:, 1, :]
        self.mstrict_bf = misc[:, 2, :]
        self.ustrict_bf = misc[:, 3, :]
        self.onesbf = misc[:, 4, :]
        self.mstrict_f = A([128, 128], F32, "mstrict_f")
        self.dma("sp", self.mstrict_f, DR(d["c_misc"][:, 256:384]))
        self.onecol = A([128, 1], F32, "onecol")
        self.memset("dve", self.onecol, 1.0)
        self.fwide = A([128, 126], F32, "fwide")
        self.dma("sp", self.fwide, DR(d["c_fw"]))
        self.qkg = A([128, 4], F32, "qkg")
        self.dma("sp", self.qkg, DR(d["qkg"]))
        self.n1g = A([128, 8], F32, "n1g")
        self.dma("sp", self.n1g, DR(d["n1g"]))
        self.n2g = A([128, 8], F32, "n2g")
        self.dma("sp", self.n2g, DR(d["n2g"]))
        self.adabT = A([128, 48], F32, "adabT")
        self.dma("sp", self.adabT, DR(d["adabT"]))
        self.cw2k = A([64, 2, 128], BF16, "cw2k")
        self.dma("pool", self.cw2k, DR(d["cw2k"].rearrange("p (a b) -> p a b", a=2)))
        self.cw2v = A([64, 64], BF16, "cw2v")
        self.dma("pool", self.cw2v, DR(d["cw2v"]))
        self.cposT = A([128, 32], BF16, "cposT")
        self.dma("pool", self.cposT, DR(d["cposT"]))
        self.pbias = A([64, 2], F32, "pbias")
        self.tbl = A([128, 32, 8], F32, "tbl")
        self.dma("sp", self.tbl.re("p a b -> p (a b)"), DR(d["relb"].rearrange("a b -> (a b)").partition_broadcast(128)))
        self.c31 = self.tbl[:, 31, :]
        self.R = A([128, 8, 272], BF16, "R")
        self.A1 = A([128, 8], F32, "A1")
        self.B1 = A([128, 8], F32, "B1")
        self.A2 = A([128, 8], F32, "A2")
        self.B2 = A([128, 8], F32, "B2")
        self.g1bc = A([128, 1024], F32, "g1bc")
        self.g2bc = A([128, 1024], F32, "g2bc")
        self.wslots = [A([128, 4096], BF16, "wslot%d" % i) for i in range(4)]
        S, NT = self.S_len, self.NT
        self.kbT = A([128, 4, S], BF16, "kbT")
        self.vb = A([128, NT, 512], BF16, "vb")
        self.kslT = A([128, S], BF16, "kslT")
        self.vsl = A([128, NT, 2, 65], BF16, "vsl")
        self.kwT = A([128, 1024], BF16, "kwT")
        self.vw = A([128, 8, 2, 65], BF16, "vw")
        self.kcT = A([128, 528], BF16, "kcT")
        self.vcT = A([128, 528], BF16, "vcT")
        self.kcmpT = A([128, 256], BF16, "kcmpT")
        self.hidTv = A([64, 2, 256], BF16, "hidTv")
        self.vcx = A([128, 2, 2, 129], BF16, "vcx")
        self.uT = A([128, 8, 512], BF16, "uT")
        self.gsig = A([128, 4, 24], F32, "gsig")
        self.ecb = Rot([A([128, 256], BF16, "ecb%d" % i) for i in range(2)])
        import os
        padb = int(os.environ.get("KPAD", "0"))
        if padb:
            A([128, padb // 4], F32, "pad")
        self.arena0 = self.sb_off
        self.memset("pool", self.vsl[:, :, :, 64:65], 1.0)
        self.memset("pool", self.vw[:, :, :, 64:65], 1.0)
        for ch in range(2):
            for kvh in range(2):
                self.dma("pool", self.vcx[:, ch, kvh, 64:129], DR(d["c_ovx"][:, 65 * ch:65 * ch + 65]))
        self.plan_chunks()
        etbl = A([128, 32, 8], F32, "etbl")
        R32 = A([128, 8, 272], F32, "R32")
        indb = [A([128, 272], F32, "indb%d" % i) for i in range(2)]
        self.tt("dve", etbl, self.tbl, self.tbl[:, 31:32, :].bc([128, 32, 8]), ALU.subtract)
        self.act(etbl, etbl, AF.Exp)
        for b in range(32):
            ib = indb[b % 2]
            self.dma("sp", ib, DR(d["c_ind"][b]))
            for h in range(8):
                if b == 0:
                    self.ts("dve", R32[:, h, :], ib, etbl[:, b, h:h + 1], None, ALU.mult)
                else:
                    self.stt(R32[:, h, :], ib, etbl[:, b, h:h + 1], R32[:, h, :], ALU.mult, ALU.add)
        self.copy("dve", self.R, R32)
        for i, tag in enumerate(("cw1k", "cw1v")):
            W1 = self.wnext(tag)
            pb = self.bank(i)
            for l in range(32):
                self.mm(pb[0:64, 0:1], W1[0:64, 64 * l:64 * l + 64], self.cposT[0:64, l:l + 1], l == 0, l == 31)
            self.copy("dve", self.pbias[:, i:i + 1], pb[0:64, 0:1])
        self.S.barrier()
        self.sb_off = self.arena0
        self.alloc_arena()

    def alloc_arena(self):
        A = self.alloc
        a0 = self.sb_off
        self.xbuf = A([128, 4, 1024], F32, "xbuf")
        self.xbuf_t = [V(self.xbuf.ap[:, t, :], Buf("xbuf%d" % t)) for t in range(4)]
        x_end = self.sb_off
        self.acc = A([128, 4, 1024], F32, "acc")
        self.acc_t = [V(self.acc.ap[:, t, :], Buf("acc%d" % t)) for t in range(4)]
        self.hT = Rot([A([128, 4, 512], BF16, "hT%d" % i) for i in range(2)])
        self.relu_t = Rot([A([128, 512], F32, "relu%d" % i) for i in range(2)])
        self.xn = Rot([A([128, 1024], BF16, "xn%d" % i) for i in range(2)])
        self.sq_junk = self.relu_t.items[0].cast(BF16)
        self.small = Rot([A([128, 4], F32, "small%d" % i) for i in range(4)])
        a1 = self.sb_off
        self.sb_off = a0
        self.qaT = A([128, 4, 512], BF16, "qaT")
        self.qbT = A([128, 4, 512], BF16, "qbT")
        self.cbufA = A([128, 512], F32, "cbufA")
        self.cbufB = A([128, 512], F32, "cbufB")
        self.carA = Rot([A([128, 1], F32, "carA%d" % i) for i in range(2)])
        self.carB = Rot([A([128, 1], F32, "carB%d" % i) for i in range(2)])
        bfs = [A([128, 512], BF16, "bf512_%d" % i) for i in range(8)]
        self.bf512 = Rot(bfs[0:4])
        self.bfA = Rot(bfs[4:6])
        self.bfB = Rot(bfs[6:8])
        mix_off = self.sb_off
        self.rawc = A([128, 8, 129], F32, "rawc")
        self.raws = A([128, 8, 130], F32, "raws")
        raw_end = self.sb_off
        assert mix_off >= x_end, (mix_off, x_end)
        self.sb_off = mix_off
        self.mixT = A([128, 8, 512], BF16, "mixT")
        assert self.sb_off <= raw_end, (self.sb_off, raw_end)
        self.sb_off = raw_end
        self.oacc = A([128, 8, 64], F32, "oacc")
        self.imp = A([128, 2, 64], F32, "imp")
        self.imp2 = A([128, 64], F32, "imp2")
        self.sel = A([128, 2, 64], BF16, "sel")
        self.m8 = Rot([A([128, 8], F32, "m8_%d" % i) for i in range(2)])
        self.sm8 = Rot([A([128, 8, 2], F32, "sm8_%d" % i) for i in range(4)])
        self.obf = Rot([A([128, 512], BF16, "obf%d" % i) for i in range(1)])
        assert self.sb_off >= x_end, (self.sb_off, x_end)
        self.onT = A([128, 4, 512], BF16, "onT")
        self.osT = A([128, 4, 512], BF16, "osT")
        f32s = [A([128, 512], F32, "f32t%d" % i) for i in range(9)]
        self.f32t = Rot(f32s[0:5])
        self.f32n = Rot(f32s[0:1])
        self.eA = Rot(f32s[1:3])
        self.spA = Rot(f32s[3:5])
        self.eB = Rot(f32s[5:7])
        self.spB = Rot(f32s[7:9])
        self.sgt = Rot([A([128, 512], BF16, "sgt%d" % i) for i in range(2)])
        a2 = self.sb_off
        self.sb_off = max(a1, a2)
        self.arena_bytes = self.sb_off - a0

    def seq_init(self, b):
        d = self.d
        DR = lambda ap: V(ap, self.dram_buf)
        S = self.S
        S.barrier()
        cT = self.f32t.next()[:, 0:8]
        self.dma("sp", cT, DR(d["cT"][b]))
        sc = self.sgt.next()[:, 0:8]
        self.act(sc, cT, AF.Silu)
        scb = self.mixT[:, 0:2, :].re("p a b -> p (a b)").re("p (k m) -> p k m", k=8)
        for k in range(8):
            self.ts("dve", scb[:, k, :], self.onesbf, sc[:, k:k + 1], None, ALU.mult)
        fm_dst = {0: self.B1, 1: self.B1, 2: self.A1, 3: self.A1, 6: self.B2, 7: self.B2, 8: self.A2, 9: self.A2}
        pool = Rot([self.bank(k) for k in range(4)])
        for j in range(12):
            W = self.wnext("ada%d" % j)
            if j in fm_dst:
                dst = fm_dst[j]
                pb = pool.next()
                for m in range(4):
                    for k in range(8):
                        self.mm(pb[:, m:m + 1], W[:, k, 128 * m:128 * m + 128], sc[:, k:k + 1], k == 0, k == 7)
                c0 = (j % 2) * 4
                self.tt("dve", dst[:, c0:c0 + 4], pb[:, 0:4], self.adabT[:, 4 * j:4 * j + 4], ALU.add)
            else:
                dst = self.g1bc if j < 6 else self.g2bc
                pb = pool.next()
                for k in range(8):
                    self.mm(pb, scb[:, k, :], W[:, k, :], k == 0, False)
                arow = self.bf512.next()[0:1, :]
                self.dma("pool", arow, DR(d["adab"][0:1, 512 * j:512 * j + 512]))
                self.mm(pb, self.onesbf[0:1, :], arow, False, True)
                c0 = (j % 2) * 512
                self.copy("act", dst[:, c0:c0 + 512], pb)
        for Av, gv in ((self.A1, self.n1g), (self.A2, self.n2g)):
            self.stt(Av, Av, 1.0, gv, ALU.add, ALU.mult)
        self.memset("pool", self.kcT[:, 0:16], 0.0)
        self.memset("pool", self.vcT[:, 0:16], 0.0)
        self.memset("pool", self.kcmpT, 0.0)
        self.memset("pool", self.hidTv, 0.0)
        for e in self.ecb.items:
            self.memset("pool", e, 0.0)
        S.barrier()

    def norm_tile(self, src, tt_, Am, Bm):
        xn = self.xn.next()
        sm = self.small.next()
        self.act(self.sq_junk, src, AF.Square, accum=sm[:, 0:1])
        self.act(sm[:, 1:2], sm[:, 0:1], AF.Sqrt, scale=1.0 / D, bias=EPS)
        self.S.op("dve", lambda h: h.reciprocal(sm.ap[:, 2:3], sm.ap[:, 1:2]), [sm], [sm])
        self.act(xn, src, AF.Copy, scale=sm[:, 2:3])
        pT = self.tp_banks.next().cast(BF16).re("p (c n) -> p c n", c=8)
        for c in range(8):
            self.tpose(pT[:, c, :], xn[:, 128 * c:128 * c + 128])
        for c in range(8):
            self.ts("dve", self.uT[:, c, 128 * tt_:128 * tt_ + 128], pT[:, c, :], Am[:, c:c + 1], Bm[:, c:c + 1],
                    ALU.mult, ALU.add)

    def qk_norm(self, zps, gcol, out, n):
        sq = self.bf512.next()
        self.act(sq[:, 0:n], zps[:, 0:n], AF.Square)
        sp = self.pB.next()
        self.mm(sp[:, 0:n], self.blk1, sq[:, 0:n])
        rt = self.f32t.next()
        self.act(rt[:, 0:n], sp[:, 0:n], AF.Sqrt, scale=1.0 / DH, bias=EPS)
        self.S.op("dve", lambda h: h.reciprocal(rt.ap[:, 0:n], rt.ap[:, 0:n]), [rt], [rt])
        self.stt(out, zps[:, 0:n], gcol, rt[:, 0:n], ALU.mult, ALU.mult)

    def group(self, b, g):
        d = self.d
        DR = lambda ap: V(ap, self.dram_buf)
        S = self.S
        S.barrier()
        if (b * self.NG + g) % 4 == 0:
            S.new_epoch(self.esems[(b * self.NG + g) // 4])
        self.pA = Rot([self.bank(k) for k in range(4)])
        self.pB = Rot([self.bank(k) for k in (4, 5)])
        self.tp_banks = Rot([self.bank(k) for k in (6, 7)])
        for t in range(4):
            self.dma("sp", self.xbuf_t[t], DR(d["x"][b, (4 * g + t) * 128:(4 * g + t + 1) * 128, :]))
        for t in range(4):
            self.norm_tile(self.xbuf_t[t], t, self.A1, self.B1)
        import os
        stopat = int(os.environ.get("STOPAT", "99")) if g == int(os.environ.get("STOPG", "6")) else 99
        if stopat <= 1:
            return
        S.barrier()
        S.trace_ops = (g == 6 and os.environ.get("TRACEOPS") == "1")
        uT = self.uT
        gs = slice(512 * g, 512 * g + 512)

        def fm(W, m):
            pb = self.pA.next()
            for k in range(8):
                self.mm(pb, W[:, k, 128 * m:128 * m + 128], uT[:, k, :], k == 0, k == 7)
            return pb

        W = self.wnext("win0")
        for m in range(4):
            self.qk_norm(fm(W, m), self.qkg[:, 0:1], self.qaT[:, m, :], 512)
        if g == int(os.environ.get("STOPG", "6")) and os.environ.get("PJ") == "1":
            return
        W = self.wnext("win1")
        self.copy("act", self.kcT[:, 16:528], fm(W, 0))
        self.copy("act", self.vcT[:, 16:528], fm(W, 1))
        self.qk_norm(fm(W, 2), self.qkg[:, 2:3], self.kslT[:, gs], 512)
        rs = slice(512 * (g % 2), 512 * (g % 2) + 512)
        self.qk_norm(fm(W, 3), self.qkg[:, 3:4], self.kwT[:, rs], 512)
        if g == int(os.environ.get("STOPG", "6")) and os.environ.get("PJ") == "2":
            return
        W = self.wnext("win2")
        for m in range(4):
            self.copy("act" if m % 2 else "dve", self.qbT[:, m, :], fm(W, m))
        if g == int(os.environ.get("STOPG", "6")) and os.environ.get("PJ") == "3":
            return
        W = self.wnext("win3")
        for m in range(4):
            self.copy("act" if m % 2 else "dve", self.kbT[:, m, gs], fm(W, m))
        if g == int(os.environ.get("STOPG", "6")) and os.environ.get("PJ") == "4":
            return
        W = self.wnext("win4")
        for t in range(4):
            pb = self.pA.next()
            for k in range(8):
                self.mm(pb, uT[:, k, 128 * t:128 * t + 128], W[:, k, :], k == 0, k == 7)
            self.copy("act" if t % 2 else "dve", self.vb[:, 4 * g + t, :], pb)
        if g == int(os.environ.get("STOPG", "6")) and os.environ.get("PJ") == "5":
            return
        W = self.wnext("win5")
        for t in range(4):
            pb = self.pA.next()
            for k in range(8):
                self.mm(pb[:, 0:280], uT[:, k, 128 * t:128 * t + 128], W[:, k, :], k == 0, k == 7)
            self.copy("act", self.vsl[:, 4 * g + t, :, 0:64], pb[:, 0:128].re("p (a b) -> p a b", a=2))
            self.copy("act", self.vw[:, (4 * g + t) % 8, :, 0:64], pb[:, 128:256].re("p (a b) -> p a b", a=2))
            self.act(self.gsig[:, t, :], pb[:, 256:280], AF.Sigmoid)
        S.trace_ops = False
        if stopat <= 2:
            return
        c_lo, c_hi = max(0, 32 * g - 1), 32 * g + 30
        n = c_hi - c_lo + 1
        for is_k, src, tag in ((True, self.kcT, "cw1k"), (False, self.vcT, "cw1v")):
            W1 = self.wnext(tag)
            hk = self.bf512.next()[0:64, 0:64].re("p (a b) -> p a b", a=2)
            for kvh in range(2):
                pbs = 64 * kvh
                pb = self.pA.next()
                for l in range(32):
                    col0 = 16 + 16 * c_lo - 512 * g + l
                    rhs = src[pbs:pbs + 64, col0:col0 + 16 * (n - 1) + 1:16]
                    self.mm(pb[0:64, 0:n], W1[pbs:pbs + 64, 64 * l:64 * l + 64], rhs, l == 0, l == 31)
                if is_k:
                    self.act(hk[:, kvh, 0:n], pb[0:64, 0:n], AF.Silu, bias=self.pbias[:, 0:1])
                else:
                    self.act(self.hidTv[:, kvh, c_lo:c_hi + 1], pb[0:64, 0:n], AF.Silu, bias=self.pbias[:, 1:2])
            if is_k:
                pb = self.pA.next()
                self.mm(pb[:, 0:n], self.cw2k[:, 0, :], hk[:, 0, 0:n], True, False)
                self.mm(pb[:, 0:n], self.cw2k[:, 1, :], hk[:, 1, 0:n], False, True)
                self.qk_norm(pb, self.qkg[:, 1:2], self.kcmpT[:, c_lo:c_hi + 1], n)
            else:
                for ch in range(c_lo // 128, c_hi // 128 + 1):
                    for kvh in range(2):
                        pb = self.pA.next()
                        self.mm(pb[:, 0:64], self.hidTv[:, kvh, 128 * ch:128 * ch + 128], self.cw2v)
                        self.copy("act", self.vcx[:, ch, kvh, 0:64], pb[:, 0:64])
        self.copy("pool", self.kcT[:, 0:16], self.kcT[:, 512:528])
        self.copy("pool", self.vcT[:, 0:16], self.vcT[:, 512:528])
        if stopat <= 3:
            return
        self.nsa_score = Rot([self.bank(2), self.bank(4)])
        self.pro_bank = self.bank(3)
        self.tp_banks = Rot([self.bank(7)])
        import os
        skn = int(os.environ.get("SKIP_NSA_FROM", "999"))
        sks = int(os.environ.get("SKIP_SB_FROM", "999"))
        for t in range(4):
            self.attn_tile(g, t, 4 * g + t < skn, 4 * g + t < sks)
        if g == 1 and b == 0:
            self.dbg_dump("onT", self.onT)
            self.dbg_dump("osT", self.osT)
            self.dbg_dump("kcmpT", self.kcmpT)
            self.dbg_dump("vcx", self.vcx)
            self.dbg_dump("sel", self.sel)
            self.dbg_dump("imp", self.imp)
            self.dbg_dump("rawc", self.rawc)
            self.dbg_dump("raws", self.raws)
        S.barrier()
        self.pA = Rot([self.bank(k) for k in range(6)])
        for t in range(4):
            self.dma("sp", self.xbuf_t[t], DR(d["x"][b, (4 * g + t) * 128:(4 * g + t + 1) * 128, :]))
        Wma = self.wnext("win6", live=1)
        Wmb = self.wnext("win8", live=2)
        Wn = self.wnext("wupn", live=3)
        Ws = self.wnext("wups", live=4)
        for m in range(8):
            if m == 4:
                Wma = self.wnext("win7", live=4)
                Wmb = self.wnext("win9", live=4)
            sga = self.sgt.next()
            self.act(sga, fm(Wma, m % 4), AF.Sigmoid)
            sgb = self.sgt.next()
            self.act(sgb, fm(Wmb, m % 4), AF.Sigmoid)
            pa = self.pA.next()
            for k in range(4):
                self.mm(pa, Wn[:, k, 128 * m:128 * m + 128], self.onT[:, k, :], k == 0, k == 3)
            pb2 = self.pA.next()
            for k in range(4):
                self.mm(pb2, Ws[:, k, 128 * m:128 * m + 128], self.osT[:, k, :], k == 0, k == 3)
            t1 = self.f32t.next()
            self.tt("dve", t1, pa, sga, ALU.mult)
            t2 = self.f32t.next()
            self.tt("dve", t2, pb2, sgb, ALU.mult)
            self.tt("pool", self.mixT[:, m, :], t1, t2, ALU.add)
        for cc in range(2):
            W = self.wnext("wout%d" % cc)
            for t in range(4):
                pb = self.pA.next()
                for k in range(8):
                    self.mm(pb, self.mixT[:, k, 128 * t:128 * t + 128], W[:, k, :], k == 0, k == 7)
                t1 = self.f32t.next()
                self.tt("dve", t1, pb, self.g1bc[:, 512 * cc:512 * cc + 512], ALU.mult)
                hv = self.xbuf_t[t][:, 512 * cc:512 * cc + 512]
                self.tt("dve", hv, hv, t1, ALU.add)
        if stopat <= 4:
            return
        S.barrier()
        self.pA = Rot([self.bank(k) for k in range(4)])
        self.pB = Rot([self.bank(k) for k in (4, 5)])
        for t in range(4):
            self.norm_tile(self.xbuf_t[t], t, self.A2, self.B2)
        for j in range(8):
            W1 = self.wnext("w1_%d" % j)
            hT = self.hT.next()
            for m in range(4):
                pb = self.pA.next()
                for k in range(8):
                    self.mm(pb, W1[:, k, 128 * m:128 * m + 128], uT[:, k, :], k == 0, k == 7)
                r = self.relu_t.next()
                self.act(r, pb, AF.Relu)
                self.tt("pool", hT[:, m, :], r, r, ALU.mult)
            W2 = self.wnext("w2_%d" % j)
            for t in range(4):
                for cc in range(2):
                    pb = self.pB.next()
                    for m in range(4):
                        self.mm(pb, hT[:, m, 128 * t:128 * t + 128], W2[:, m, 512 * cc:512 * cc + 512], m == 0, m == 3)
                    av = self.acc_t[t][:, 512 * cc:512 * cc + 512]
                    if j == 0:
                        self.copy("act", av, pb)
                    else:
                        self.tt("dve", av, av, pb, ALU.add)
        for t in range(4):
            self.tt("dve", self.acc_t[t], self.acc_t[t], self.g2bc, ALU.mult)
            self.tt("dve", self.xbuf_t[t], self.xbuf_t[t], self.acc_t[t], ALU.add)
            self.dma("sp", V(self.y[b, (4 * g + t) * 128:(4 * g + t + 1) * 128, :], Buf("y")), self.xbuf_t[t])
        S.barrier()

    def soft_stage1(self, it):
        nb = len(it["vrhs"])
        w = 128 * nb
        sc = self.nsa_score.next()
        col = 0
        for kr, wk in it["krhs"]:
            self.mm(sc[:, col:col + wk], it["qT"], kr)
            col += wk
        E = self.bf512.next()
        h = it["h"]
        self.act(E[:, 0:w], sc[:, 0:w], AF.Exp, scale=0.125)
        it["E"] = E

    def soft_stage2(self, it):
        nb = len(it["vrhs"])
        w = 128 * nb
        E = it["E"]
        for (c0, ncol, mk, is3d) in it["masks"]:
            ev = E[:, c0:c0 + ncol]
            if is3d:
                ev = ev.re("p (n k) -> p n k", k=64)
            self.tt("dve", ev, ev, mk, ALU.mult)
        yield
        tp = self.tp_banks.next().cast(BF16)
        for n_ in range(nb):
            self.tpose(tp[:, 128 * n_:128 * n_ + 128], E[:, 128 * n_:128 * n_ + 128])
        ET = self.bf512.next()
        self.copy("dve", ET[:, 0:w], tp[:, 0:w])
        yield
        for n_ in range(nb):
            self.mm(it["acc"], ET[:, 128 * n_:128 * n_ + 128], it["vrhs"][n_],
                    it["first"] and n_ == 0, it["last"] and n_ == nb - 1)
        if it["fin"] is not None:
            self.copy("dve", self.raws[:, it["fin"], :], it["ob"][:, 0:130])
        yield

    def nsa_prologue(self, g, t):
        i = 4 * g + t
        qc = slice(128 * t, 128 * t + 128)
        Wc = 8 * i + 7
        nch = 1 if Wc <= 128 else 2
        lo, hi = max(0, 8 * i - 9), 8 * i + 7
        off = 8 * i - 9

        def st1(h):
            pbs, j = 64 * (h // 4), h % 4
            qT = self.qaT[pbs:pbs + 64, j, qc]
            sc = self.nsa_score.next()
            self.mm(sc[:, 0:Wc], qT, self.kcmpT[pbs:pbs + 64, 0:Wc])
            E = self.ecb.next()
            self.act(E[:, 0:Wc], sc[:, 0:Wc], AF.Exp, scale=0.125)
            return E

        def st2(h, E):
            kvh = h // 4
            self.tt("dve", E[:, lo:hi], E[:, lo:hi], self.R[:, h, 256 + lo - off:256 + hi - off], ALU.mult)
            yield
            tp = self.tp_banks.next().cast(BF16)
            for ch in range(nch):
                self.tpose(tp[:, 128 * ch:128 * ch + 128], E[:, 128 * ch:128 * ch + 128])
            ET = self.bf512.next()
            self.copy("act", ET[:, 0:128 * nch], tp[:, 0:128 * nch])
            yield
            oc = self.pro_bank
            for ch in range(nch):
                self.mm(oc[:, 0:129], ET[:, 128 * ch:128 * ch + 128], self.vcx[:, ch, kvh, :], ch == 0, ch == nch - 1)
            self.copy("dve", self.rawc[:, h, :], oc[:, 0:129])
            yield

        prev = None
        for h in range(8):
            E = st1(h)
            yield
            if prev is not None:
                yield from st2(*prev)
            prev = (h, E)
        yield from st2(*prev)
        rsc = self.sm8.next()[:, :, 0]
        self.ts("dve", rsc, self.rawc[:, :, 64], 1e-30, None, ALU.max)
        self.S.op("dve", lambda h_: h_.reciprocal(rsc.ap, rsc.ap), [rsc], [rsc])
        tmp = self.f32n.next().re("p (a b) -> p a b", a=8)
        rsc3 = V(rsc.ap.unsqueeze(2), rsc.buf)
        self.tt("dve", tmp, self.rawc[:, :, 65:129], rsc3.bc([128, 8, 64]), ALU.mult)
        self.S.op("dve", lambda h_: h_.tensor_reduce(self.imp.ap, tmp.ap.rearrange("p (k g) n -> p k n g", g=4),
                                                     AX.X, ALU.add), [tmp], [self.imp])
        fw = V(self.fwide.ap[:, 62 - 2 * i:126 - 2 * i].unsqueeze(1), self.fwide.buf)
        self.tt("dve", self.imp, self.imp, fw.bc([128, 2, 64]), ALU.add)
        self.ts("dve", self.imp[:, :, 0:1], self.imp[:, :, 0:1], 1e4, None, ALU.add)
        yield
        for kvh in range(2):
            iv = self.imp[:, kvh, :]
            m8 = self.m8.next()
            self.S.op("dve", lambda h_, m8=m8, iv=iv: h_.max(m8.ap, iv.ap), [iv], [m8])
            self.S.op("dve", lambda h_, m8=m8, iv=iv: h_.match_replace(self.imp2.ap, m8.ap, iv.ap, -1e30),
                      [iv, m8], [self.imp2])
            m8b = self.m8.next()
            self.S.op("dve", lambda h_, m8b=m8b: h_.max(m8b.ap, self.imp2.ap), [self.imp2], [m8b])
            self.ts("dve", self.sel[:, kvh, :], iv, m8b[:, 7:8], None, ALU.is_ge)
            yield
        coef = self.sm8.next()[:, :, 0]
        gv = self.gsig[:, t, :].re("p (h c) -> p h c", c=3)
        self.tt("dve", coef, rsc, gv[:, :, 0], ALU.mult)
        coef3 = V(coef.ap.unsqueeze(2), coef.buf)
        self.tt("dve", self.oacc, self.rawc[:, :, 0:64], coef3.bc([128, 8, 64]), ALU.mult)
        yield

    def nsa_heads(self, g, t, heads, ob):
        i = 4 * g + t
        Kt = (i + 1) * 128
        qc = slice(128 * t, 128 * t + 128)
        nchunk = i // 4 + 1
        items = []
        for h in heads:
            kvh, j, pbs = h // 4, h % 4, 64 * (h // 4)
            qT = self.qaT[pbs:pbs + 64, j, qc]
            for c in range(nchunk):
                w = min(512, Kt - 512 * c)
                nb = w // 128
                masks = []
                selv = V(self.sel.ap[:, kvh, 8 * c:8 * c + w // 64].unsqueeze(2), self.sel.buf)
                masks.append((0, w, selv.bc([128, w // 64, 64]), True))
                for kb in (i - 1, i):
                    if kb >= 0 and kb // 4 == c:
                        ro = (kb - (i - 1)) * 128
                        masks.append(((kb % 4) * 128, 128, self.R[:, h, ro:ro + 128], False))
                items.append(dict(qT=qT, krhs=[(self.kslT[pbs:pbs + 64, 512 * c:512 * c + w], w)], h=h, masks=masks,
                                  vrhs=[self.vsl[:, 4 * c + n_, kvh, :] for n_ in range(nb)],
                                  acc=ob[:, 0:65], first=(c == 0), last=(c == nchunk - 1), fin=None, ob=ob))
            kbs = list(range(max(0, i - 4), i + 1))
            parts = [p for p in (kbs[0:4], kbs[4:]) if p]
            for pi, part in enumerate(parts):
                masks = []
                for li, kb in enumerate(part):
                    if kb == i - 4:
                        masks.append((128 * li, 128, self.ustrict_bf, False))
                    if kb >= i - 1:
                        ro = (kb - (i - 1)) * 128
                        masks.append((128 * li, 128, self.R[:, h, ro:ro + 128], False))
                items.append(dict(qT=qT, krhs=[(self.kwT[pbs:pbs + 64, 128 * (kb % 8):128 * (kb % 8) + 128], 128)
                                               for kb in part], h=h, masks=masks,
                                  vrhs=[self.vw[:, kb % 8, kvh, :] for kb in part],
                                  acc=ob[:, 65:130], first=(pi == 0), last=(pi == len(parts) - 1),
                                  fin=(h if pi == len(parts) - 1 else None), ob=ob))
        prev = None
        for it in items:
            self.soft_stage1(it)
            yield
            if prev is not None:
                yield from self.soft_stage2(prev)
            prev = it
        yield from self.soft_stage2(prev)

    def nsa_epilogue(self, g, t):
        qc = slice(128 * t, 128 * t + 128)
        gv = self.gsig[:, t, :].re("p (h c) -> p h c", c=3)
        rs2 = self.sm8.next()
        r4 = self.raws.re("p h (b k) -> p h b k", k=65)
        self.ts("dve", rs2, r4[:, :, :, 64], 1e-30, None, ALU.max)
        self.S.op("dve", lambda h_: h_.reciprocal(rs2.ap, rs2.ap), [rs2], [rs2])
        for br in (0, 1):
            cf = self.sm8.next()[:, :, 0]
            self.tt("dve", cf, rs2[:, :, br], gv[:, :, 1 + br], ALU.mult)
            cf3 = V(cf.ap.unsqueeze(2), cf.buf)
            tmp = self.f32n.next().re("p (a b) -> p a b", a=8)
            self.tt("dve", tmp, r4[:, :, br, 0:64], cf3.bc([128, 8, 64]), ALU.mult)
            self.tt("dve", self.oacc, self.oacc, tmp, ALU.add)
        ob_ = self.obf.next()
        self.copy("act", ob_, self.oacc.re("p a b -> p (a b)"))
        tp = self.tp_banks.next().cast(BF16)
        for k in range(4):
            self.tpose(tp[:, 128 * k:128 * k + 128], ob_[:, 128 * k:128 * k + 128])
        self.copy("act", self.onT[:, :, qc], tp[:, 0:512].re("p (k n) -> p k n", k=4))

    def sb_heads(self, g, t, heads, cb, cars, er, spr, bfr, sc, acc):
        i = 4 * g + t
        Kt = (i + 1) * 128
        qc = slice(128 * t, 128 * t + 128)
        nchunk = i // 4 + 1
        items = []
        for h in heads:
            pbs, j = 64 * (h % 2), h // 2
            for c in range(nchunk - 1, -1, -1):
                w = min(512, Kt - 512 * c)
                items.append(dict(h=h, c=c, w=w, nb=w // 128, diag=(c == nchunk - 1), pbs=pbs, j=j,
                                  qT=self.qbT[pbs:pbs + 64, j, qc]))

        def st1(it):
            w, c, pbs, j = it["w"], it["c"], it["pbs"], it["j"]
            self.mm(sc[:, 0:w], it["qT"], self.kbT[pbs:pbs + 64, j, 512 * c:512 * c + w])
            e = er.next()
            self.act(e[:, 0:w], sc[:, 0:w], AF.Exp, scale=0.125)
            yield
            sp = spr.next()
            self.act(sp[:, 0:w], e[:, 0:w], AF.Ln, bias=1.0)
            if it["diag"]:
                self.tt("pool", sp[:, w - 128:w], sp[:, w - 128:w], self.mstrict_f, ALU.mult)
            it["e"], it["sp"] = e, sp
            yield

        def st2(it, carry):
            w, nb, h, c = it["w"], it["nb"], it["h"], it["c"]
            e, sp = it["e"], it["sp"]
            ones = self.onecol.bc([128, w])
            init = 0.0 if carry is None else carry
            ins = [ones, sp] + ([carry] if carry is not None else [])

            def scan(h_, sp=sp, w=w, init=init, ones=ones):
                iv = init.ap if isinstance(init, V) else init
                return h_.tensor_tensor_scan(cb.ap[:, 0:w][:, ::-1], ones.ap, sp.ap[:, 0:w][:, ::-1], iv,
                                             ALU.mult, ALU.add)
            self.S.op("dve", scan, ins, [cb])
            yield
            self.act(sp[:, 0:w], cb[:, 0:w], AF.Exp, scale=-1.0)
            ncar = cars.next()
            self.copy("dve", ncar, cb[:, 0:1])
            it["carry_out"] = ncar
            yield
            Ab = bfr.next()
            self.tt("dve", Ab[:, 0:w], e[:, 0:w], sp[:, 0:w], ALU.mult)
            if it["diag"]:
                self.tt("pool", Ab[:, w - 128:w], Ab[:, w - 128:w], self.mstrict_bf, ALU.mult)
            yield
            tp = self.tp_banks.next().cast(BF16)
            for n_ in range(nb):
                self.tpose(tp[:, 128 * n_:128 * n_ + 128], Ab[:, 128 * n_:128 * n_ + 128])
            AT = bfr.next()
            self.copy("dve" if heads[0] >= 4 else "act", AT[:, 0:w], tp[:, 0:w])
            yield
            for n_ in range(nb):
                self.mm(acc[:, 64 * (h % 4):64 * (h % 4) + 64], AT[:, 128 * n_:128 * n_ + 128],
                        self.vb[:, 4 * c + n_, 64 * h:64 * h + 64], it["diag"] and n_ == 0, c == 0 and n_ == nb - 1)
            yield

        prev = None
        for it in items:
            yield from st1(it)
            if prev is not None:
                carry = None if prev["diag"] else prev["carry_in"]
                yield from st2(prev, carry)
                it["carry_in"] = prev["carry_out"]
            prev = it
        carry = None if prev["diag"] else prev["carry_in"]
        yield from st2(prev, carry)

    def sb_epilogue(self, g, t):
        qc = slice(128 * t, 128 * t + 128)
        ob_ = self.obf.next()
        self.copy("act", ob_[:, 0:256], self.bank(5)[:, 0:256])
        self.copy("act", ob_[:, 256:512], self.bank(6)[:, 0:256])
        tp = self.tp_banks.next().cast(BF16)
        for k in range(4):
            self.tpose(tp[:, 128 * k:128 * k + 128], ob_[:, 128 * k:128 * k + 128])
        self.copy("dve", self.osT[:, :, qc], tp[:, 0:512].re("p (k n) -> p k n", k=4))

    @staticmethod
    def run_lanes(lanes):
        lanes = list(lanes)
        while lanes:
            nxt = []
            for ln in lanes:
                try:
                    next(ln)
                    nxt.append(ln)
                except StopIteration:
                    pass
            lanes = nxt

    def attn_tile(self, g, t, do_nsa=True, do_sb=True):
        def nsa_lane():
            yield from self.nsa_prologue(g, t)
            yield from self.nsa_heads(g, t, range(0, 8), self.bank(3))
            self.nsa_epilogue(g, t)
        lanes = []
        if do_nsa:
            lanes.append(nsa_lane())
        if do_sb:
            lanes.append(self.sb_heads(g, t, range(0, 4), self.cbufA, self.carA, self.eA, self.spA, self.bfA, self.bank(0), self.bank(5)))
            lanes.append(self.sb_heads(g, t, range(4, 8), self.cbufB, self.carB, self.eB, self.spB, self.bfB, self.bank(1), self.bank(6)))
        self.run_lanes(lanes)
        if do_sb:
            self.sb_epilogue(g, t)


_CACHE = {}


def run(inputs, S, NB, ncores, batch0=0):
    key = (S, NB)
    if key not in _CACHE:
        _CACHE[key] = Prog(S, NB).build()
    nc = _CACHE[key]
    maps = host_prep(inputs, S, NB, ncores, batch0)
    res = run_bass_kernel_spmd(nc, maps, core_ids=list(range(ncores)))
    return np.concatenate([np.asarray(r["y"]) for r in res.results], axis=0)


def kernel(**inputs):
    out = run(inputs, SEQ, BATCH // NCORES, NCORES)
    return out.astype(np.float32)
```
